# Optimizing a Trainium2 kernel written in Bass

```python
import math
import jax, jax.numpy as jnp
from jax import lax
import numpy as np

D_MODEL = 1024
BATCH = 16
SEQ = 2048
DEPTH = 2

N_MIXERS = 2
N_HEADS = 16
HEAD_DIM = D_MODEL // N_HEADS
Q_BLOCK = 128
SSM_GROUP = 16
N_GROUPS = D_MODEL // SSM_GROUP
STATE = 64
D_FF = ((8 * D_MODEL // 3 + 127) // 128) * 128
CONV_W = 3
N_ATTN = (DEPTH + 1) // 2
N_SSM = DEPTH // 2
EPS = 1e-6
DT_MIN = 1e-3
DT_MAX = 1e-1

kernel_name = "hybrid_stickbreak_s5_convffn_adaln"


def rms_norm(x, g):
    xf = x.astype(jnp.float32)
    y = xf * lax.rsqrt(jnp.mean(xf * xf, axis=-1, keepdims=True) + EPS)
    return (y * g.astype(jnp.float32)).astype(x.dtype)


def modulate(h, shift, scale):
    return h * (1 + scale[:, None, :]) + shift[:, None, :]


def stick_breaking_attention(h, w_qkv, w_o):
    b, s, d = h.shape
    q, k, v = jnp.split(h @ w_qkv, 3, axis=-1)
    to_heads = lambda t: t.reshape(b, s, N_HEADS, HEAD_DIM).transpose(0, 2, 1, 3)
    q, k, v = to_heads(q), to_heads(k), to_heads(v)
    kf = k.astype(jnp.float32)
    vf = v.astype(jnp.float32)
    n_blk = s // Q_BLOCK
    qb = q.reshape(b, N_HEADS, n_blk, Q_BLOCK, HEAD_DIM).transpose(2, 0, 1, 3, 4)
    key_pos = jnp.arange(s)
    scale = HEAD_DIM ** -0.5

    def block(args):
        q_blk, blk_idx = args
        q_pos = blk_idx * Q_BLOCK + jnp.arange(Q_BLOCK)
        z = jnp.einsum('bhqd,bhkd->bhqk', q_blk.astype(jnp.float32), kf) * scale
        mask = key_pos[None, :] < q_pos[:, None]
        log_beta = jax.nn.log_sigmoid(z)
        log_1mb = jnp.where(mask, jax.nn.log_sigmoid(-z), 0.0)
        suffix = lax.cumsum(log_1mb, axis=3, reverse=True) - log_1mb
        w = jnp.where(mask, jnp.exp(log_beta + suffix), 0.0)
        o = jnp.einsum('bhqk,bhkd->bhqd', w, vf)
        return o.astype(h.dtype)

    o = lax.map(block, (qb, jnp.arange(n_blk)))
    o = o.transpose(1, 0, 3, 2, 4).reshape(b, s, d)
    return o @ w_o


def s5_ssm(h, w_in, a_re, a_im, log_dt, b_re, b_im, c_re, c_im, d_skip, w_glu, b_glu, w_o):
    b, s, d = h.shape
    u = h @ w_in
    uf = u.astype(jnp.float32)
    ug = uf.reshape(b, s, N_GROUPS, SSM_GROUP)
    lam = lax.complex(a_re.astype(jnp.float32), a_im.astype(jnp.float32))
    dt = jnp.exp(log_dt.astype(jnp.float32))[:, None]
    lam_bar = jnp.exp(lam * dt)
    b_mat = lax.complex(b_re.astype(jnp.float32), b_im.astype(jnp.float32))
    b_bar = ((lam_bar - 1) / lam)[..., None] * b_mat
    bu = jnp.einsum('gph,bsgh->bsgp', b_bar, ug.astype(jnp.complex64))
    a_seq = jnp.broadcast_to(lam_bar, (1, s, N_GROUPS, STATE))

    def combine(e1, e2):
        a1, x1 = e1
        a2, x2 = e2
        return a2 * a1, a2 * x1 + x2

    _, states = lax.associative_scan(combine, (a_seq, bu), axis=1)
    c_mat = lax.complex(c_re.astype(jnp.float32), c_im.astype(jnp.float32))
    y = jnp.einsum('ghp,bsgp->bsgh', c_mat, states).real.reshape(b, s, d)
    y = (y + d_skip.astype(jnp.float32) * uf).astype(h.dtype)
    z = jax.nn.gelu(y)
    g = z * jax.nn.sigmoid(z @ w_glu + b_glu)
    return g @ w_o


def conv_ffn(h, w_up, conv_w, conv_b, w_down):
    up = h @ w_up
    up = lax.conv_general_dilated(
        up, conv_w[:, None, :], window_strides=(1,), padding=[(CONV_W - 1, 0)],
        dimension_numbers=('NWC', 'WIO', 'NWC'), feature_group_count=2 * D_FF) + conv_b
    gate, val = jnp.split(up, 2, axis=-1)
    return (jax.nn.silu(gate) * val) @ w_down


def setup_inputs(seed: int = 0) -> dict:
    key = jax.random.key(seed)
    ks = iter(jax.random.split(key, 40))
    nrm = lambda shape, std: jax.random.normal(next(ks), shape, jnp.float32) * std
    D, G, P, H, F = D_MODEL, N_GROUPS, STATE, SSM_GROUP, D_FF
    n_idx = jnp.arange(P, dtype=jnp.float32)
    inp = {}
    inp["x"] = nrm((BATCH, SEQ, D), 1.0)
    inp["c"] = nrm((BATCH, D), 1.0)
    inp["norm_mix"] = 1.0 + nrm((DEPTH, D), 0.02)
    inp["norm_ffn"] = 1.0 + nrm((DEPTH, D), 0.02)
    inp["w_mod"] = nrm((DEPTH, D, 6 * D), 0.5 * D ** -0.5)
    inp["b_mod"] = nrm((DEPTH, 6 * D), 0.02)
    inp["w_qkv"] = nrm((N_ATTN, D, 3 * D), D ** -0.5)
    inp["w_o_attn"] = nrm((N_ATTN, D, D), D ** -0.5)
    inp["w_in_ssm"] = nrm((N_SSM, D, D), D ** -0.5)
    inp["a_re"] = -0.5 + nrm((N_SSM, G, P), 0.01)
    inp["a_im"] = math.pi * n_idx + nrm((N_SSM, G, P), 0.01)
    inp["log_dt"] = jax.random.uniform(next(ks), (N_SSM, G), jnp.float32,
                                       math.log(DT_MIN), math.log(DT_MAX))
    inp["b_re"] = nrm((N_SSM, G, P, H), (2 * H) ** -0.5)
    inp["b_im"] = nrm((N_SSM, G, P, H), (2 * H) ** -0.5)
    inp["c_re"] = nrm((N_SSM, G, H, P), (2 * P) ** -0.5 * 4.0)
    inp["c_im"] = nrm((N_SSM, G, H, P), (2 * P) ** -0.5 * 4.0)
    inp["d_skip"] = nrm((N_SSM, D), 1.0)
    inp["w_glu"] = nrm((N_SSM, D, D), D ** -0.5)
    inp["b_glu"] = nrm((N_SSM, D), 0.02)
    inp["w_o_ssm"] = nrm((N_SSM, D, D), D ** -0.5)
    inp["w_up"] = nrm((DEPTH, D, 2 * F), D ** -0.5)
    inp["conv_w"] = nrm((DEPTH, CONV_W, 2 * F), CONV_W ** -0.5)
    inp["conv_b"] = nrm((DEPTH, 2 * F), 0.02)
    inp["w_down"] = nrm((DEPTH, F, D), F ** -0.5)
    inp["norm_out"] = 1.0 + nrm((D,), 0.02)
    inp["w_fin"] = nrm((D, 2 * D), 0.5 * D ** -0.5)
    inp["b_fin"] = nrm((2 * D,), 0.02)
    return inp


def reference(x, c, norm_mix, norm_ffn, w_mod, b_mod, w_qkv, w_o_attn, w_in_ssm,
              a_re, a_im, log_dt, b_re, b_im, c_re, c_im, d_skip, w_glu, b_glu, w_o_ssm,
              w_up, conv_w, conv_b, w_down, norm_out, w_fin, b_fin):
    c_act = jax.nn.silu(c)
    for i in range(DEPTH):
        mod = c_act @ w_mod[i] + b_mod[i]
        sh1, sc1, g1, sh2, sc2, g2 = jnp.split(mod, 6, axis=-1)
        h = modulate(rms_norm(x, norm_mix[i]), sh1, sc1)
        j = i // N_MIXERS
        if i % N_MIXERS == 0:
            y = stick_breaking_attention(h, w_qkv[j], w_o_attn[j])
        else:
            y = s5_ssm(h, w_in_ssm[j], a_re[j], a_im[j], log_dt[j], b_re[j], b_im[j],
                       c_re[j], c_im[j], d_skip[j], w_glu[j], b_glu[j], w_o_ssm[j])
        x = x + g1[:, None, :] * y
        h = modulate(rms_norm(x, norm_ffn[i]), sh2, sc2)
        x = x + g2[:, None, :] * conv_ffn(h, w_up[i], conv_w[i], conv_b[i], w_down[i])
    fin = c_act @ w_fin + b_fin
    sh, sc = jnp.split(fin, 2, axis=-1)
    return modulate(rms_norm(x, norm_out), sh, sc)
```

```python
import contextlib
import numpy as np
import concourse.bass as bass
import concourse.mybir as mybir
from concourse.bass_utils import run_bass_kernel_spmd

F32 = mybir.dt.float32
BF16 = mybir.dt.bfloat16
AF = mybir.ActivationFunctionType
ALU = mybir.AluOpType

ENGS = ("pe", "act", "dve", "pool", "sp")
N_DMA_SEMS = 24


class Prog:
    def __init__(self, nc):
        self.nc = nc
        self.ops = {e: [] for e in ENGS}
        self.reg = {}
        self.dma_use = [0] * N_DMA_SEMS
        self.dma_rr = 0
        self.out_tokens = []
        self.arena_dma = {}
        self.last_compute = {}

    def _collect(self, eng, reads, writes):
        need = {}

        def add(k, v):
            if need.get(k, -1) < v:
                need[k] = v

        for r in reads:
            e = self.reg.get(r)
            if e is not None:
                for k, v in e[0].items():
                    add(k, v)
        for w in writes:
            e = self.reg.get(w)
            if e is not None:
                for k, v in e[0].items():
                    add(k, v)
                for k, v in e[1].items():
                    if k == eng:
                        continue
                    add(k, v)
        if eng == "pe":
            need.pop("pe", None)
        return need

    def _commit(self, toks, reads, writes):
        for r in reads:
            e = self.reg.setdefault(r, [{}, {}])
            for k, v in toks:
                if e[1].get(k, -1) < v:
                    e[1][k] = v
        for w in writes:
            self.reg[w] = [dict(toks), {}]

    def op(self, eng, fn, reads=(), writes=(), nosync=False):
        need = self._collect(eng, reads, writes)
        if nosync:
            need.pop(eng, None)
        idx = len(self.ops[eng])
        self.ops[eng].append([fn, list(need.items()), False, None])
        self._commit([(eng, idx)], reads, writes)
        self.last_compute[eng] = idx
        return idx

    def dma(self, eng, fns, reads=(), writes=(), arena=True, out=False):
        if not isinstance(fns, (list, tuple)):
            fns = [fns]
        need = self._collect("x", reads, writes)
        toks = []
        for n, fn in enumerate(fns):
            i = self.dma_rr
            self.dma_rr = (self.dma_rr + 1) % N_DMA_SEMS
            nd = dict(need) if n == 0 else {}
            if self.dma_use[i] > 0:
                k = ("d", i)
                v = 16 * self.dma_use[i]
                if nd.get(k, -1) < v:
                    nd[k] = v
            self.dma_use[i] += 1
            tok = (("d", i), 16 * self.dma_use[i])
            self.ops[eng].append([fn, list(nd.items()), False, tok])
            toks.append(tok)
            if arena:
                self.arena_dma[tok[0]] = tok[1]
            if out:
                self.out_tokens.append(tok)
        self._commit(toks, reads, writes)
        return toks

    def barrier(self, keep=lambda key: False):
        toks = [(e, i) for e, i in self.last_compute.items()]
        toks += list(self.arena_dma.items())
        for e in ENGS:
            self.ops[e].append([None, [t for t in toks if t[0] != e], False, None])
        self.arena_dma = {}
        self.reg = {k: v for k, v in self.reg.items() if keep(k)}

    def finish(self):
        self.ops["sp"].append([None, list(self.out_tokens), False, None])

    def emit(self):
        nc = self.nc
        for e in ENGS:
            for rec in self.ops[e]:
                for k, v in rec[1]:
                    if isinstance(k, str):
                        assert self.ops[k][v][0] is not None and self.ops[k][v][3] is None
                        self.ops[k][v][2] = True
        sig = {}
        for e in ENGS:
            c = 0
            s = []
            for rec in self.ops[e]:
                if rec[2]:
                    c += 1
                s.append(c)
            sig[e] = s
            assert c < 60000, (e, c)
        self.stats = {e: (len(self.ops[e]), sig[e][-1] if sig[e] else 0) for e in ENGS}
        with contextlib.ExitStack() as st:
            esem = {e: st.enter_context(nc.semaphore("s_" + e)) for e in ENGS}
            dsem = [st.enter_context(nc.semaphore("d%d" % i)) for i in range(N_DMA_SEMS)]
            block = st.enter_context(nc.Block())

            def run(e):
                def body(h):
                    water = {}
                    for fn, deps, signal, dtok in self.ops[e]:
                        for k, v in deps:
                            if isinstance(k, str):
                                sem = esem[k]
                                val = sig[k][v]
                            else:
                                sem = dsem[k[1]]
                                val = v
                            if water.get(k, 0) >= val:
                                continue
                            water[k] = val
                            h.wait_ge(sem, val)
                        if fn is None:
                            continue
                        ins = fn(h)
                        if dtok is not None:
                            ins.then_inc(dsem[dtok[0][1]], 16)
                        elif signal:
                            ins.then_inc(esem[e], 1)
                return body

            block.tensor(run("pe"))
            block.scalar(run("act"))
            block.vector(run("dve"))
            block.gpsimd(run("pool"))
            block.sync(run("sp"))


D = 1024
KC = 8
S = 2048
NB = 2
NH = 16
DH = 64
FF = 2816
FJ = 22
NG = 64
EPS = 1e-6
TWO_PI = 6.283185307179586

V_NMIX, V_NFFN, V_NOUT, V_BMOD, V_BFIN, V_BGLU, V_CW, V_CB, V_DSK, NV = 0, 16, 32, 40, 136, 152, 160, 424, 512, 576
NMOD = 112


class WStream:
    def __init__(self, P, bufs, schedule=None):
        self.P = P
        self.bufs = bufs
        self.n = len(bufs)
        self.schedule = schedule
        self.req = []
        self.issued = 0
        self.cur = 0

    def _issue(self, idx):
        parts = (self.schedule[idx] if self.schedule is not None else self.req[idx])(self.C)
        slot = idx % self.n
        buf = self.bufs[slot]
        fns = []
        off = 0
        for (src, shape) in parts:
            n = int(np.prod(shape))
            dst = buf[:, off:off + n]
            if len(shape) == 2:
                dst = dst.rearrange("p (k n) -> p k n", k=shape[0])
            fns.append(lambda h, dst=dst, src=src: h.dma_start(out=dst, in_=src))
            off += n
        self.P.dma("pool", fns, writes=[("wb", slot)], arena=False)

    def next(self, parts_fn, ahead=2):
        idx = self.cur
        self.cur += 1
        self.req.append(parts_fn)
        parts = parts_fn(self.C)
        lim = idx + ahead if self.schedule is not None else idx
        while self.issued <= lim and (self.schedule is None or self.issued < len(self.schedule)):
            self._issue(self.issued)
            self.issued += 1
        slot = idx % self.n
        buf = self.bufs[slot]
        views = []
        off = 0
        for (src, shape) in parts:
            n = int(np.prod(shape))
            v = buf[:, off:off + n]
            if len(shape) == 2:
                v = v.rearrange("p (k n) -> p k n", k=shape[0])
            views.append(v)
            off += n
        return views, ("wb", slot)


def wslab(w2d, c0, n, r0=0, kk=KC):
    return (w2d[r0:r0 + kk * 128, c0:c0 + n].rearrange("(k p) n -> p k n", p=128), (kk, n))


class Ctx:
    pass


def build(stages=("attn", "ffn0", "ssm", "ffn1"), schedule=None, nseq=NB, debug=None):
    nc = bass.Bass("TRN2", target_bir_lowering=False)
    C = Ctx()
    C.debug = debug
    if isinstance(debug, str) and debug.startswith('pro'):
        C.pro_lim = int(debug[3:4])
        C.tt_lim = int(debug[4:5]) if len(debug) > 4 else 9
    if isinstance(debug, str) and debug.startswith('ng'):
        C.ngroups = int(debug[2:])
    C.nc = nc
    C.stages = stages
    di = lambda name, shape: nc.dram_tensor(name, shape, F32, kind="ExternalInput").ap()
    C.xT = di("xT", [NB, D, S])
    C.cT = di("cT", [D, NB])
    C.vecs = di("vecs", [128, NV])
    C.w_mod = di("w_mod", [2, D, 6 * D])
    C.w_fin = di("w_fin", [D, 2 * D])
    C.w_qkv = di("w_qkv", [D, 3 * D])
    C.w_o_attn = di("w_o_attn", [D, D])
    C.w_in = di("w_in_ssm", [D, D])
    C.w_glu = di("w_glu", [D, D])
    C.w_o_ssm = di("w_o_ssm", [D, D])
    C.w_up = di("w_up", [2, D, 2 * FF])
    C.w_down = di("w_down", [2, FF, D])
    C.ssm_s = di("ssm_s", [128, 3, 32])
    C.ssm_bc = di("ssm_bc", [128, 4, 32, 16])
    C.outT = nc.dram_tensor("outT", [NB, D, S], F32, kind="ExternalOutput").ap()
    C.scr_mats = nc.dram_tensor("scr_mats", [4, 128, 32 * 256], BF16, kind="Internal").ap()
    C.scr_tt = nc.dram_tensor("scr_tt", [128, 64 * 128], BF16, kind="Internal").ap()
    C.scr_u = nc.dram_tensor("scr_u", [8, 64 * 16, 128], BF16, kind="Internal").ap()
    C.scr_y = nc.dram_tensor("scr_y", [8, 64 * 16, 128], BF16, kind="Internal").ap()

    P = Prog(nc)
    C.P = P
    with contextlib.ExitStack() as st:
        sb = lambda name, shape, dtype: st.enter_context(nc.sbuf_tensor(name, shape, dtype))
        C.x = sb("x_sb", [128, KC, S], F32)
        C.vec = sb("vec_sb", [128, NV], F32)
        C.mod = sb("mod_sb", [128, NMOD, NB], F32)
        C.A = sb("A_sb", [128, 5, KC, NB], F32)
        C.cact = sb("cact", [128, KC, NB], BF16)
        C.cin = sb("cin", [128, KC, NB], F32)
        C.ones = sb("ones_bf", [128, 128], BF16)
        C.tri_i = sb("tri_i", [128, 128], BF16)
        C.tri_r = sb("tri_r", [128, 128], BF16)
        C.ident = sb("ident", [128, 128], BF16)
        C.maskbig = sb("maskbig", [128, 896], BF16)
        C.bmask = sb("bmask", [128, 128], F32)
        C.identf = sb("identf", [128, 128], F32)
        C.pid_i = sb("pid_i", [128, 2], mybir.dt.int32)
        C.pid_f = sb("pid_f", [128, 2], F32)
        C.halo = sb("halo", [128, 2 * FJ, 2], F32)
        C.sscar = sb("sscar", [128, 2, 32], F32)
        C.ssD = sb("ssD", [128, 2, 32], F32)
        wb = [sb("wbuf%d" % i, [128, 4096], BF16) for i in range(3)]
        C.W = WStream(P, wb, schedule)
        C.W.C = C
        C.ARENA_BYTES = 110 * 1024
        C.arena = sb("arena", [128, C.ARENA_BYTES // 2], BF16)
        C.iot_i = C.arena[:, 0:1792].bitcast(mybir.dt.int32)
        C.iot_f = C.arena[:, 2048:2048 + 1792].bitcast(F32)
        C.NT_OFF = C.ARENA_BYTES - 8192
        pq = [st.enter_context(nc.psum_tensor("pq%d" % i, [128, 1024], F32)) for i in range(4)]
        C.pq = pq
        C.bank = lambda i: pq[i // 2][:, 512 * (i % 2):512 * (i % 2) + 512]

        prologue(C)
        for b in range(nseq):
            run_sequence(C, b)
        P.finish()
        P.emit()
    C.requests = C.W.req
    return nc, C


def carve(C, off, shape, dtype):
    n = int(np.prod(shape))
    if dtype == F32:
        ap = C.arena[:, off // 2: off // 2 + 2 * n].bitcast(F32)
    else:
        ap = C.arena[:, off // 2: off // 2 + n]
    if len(shape) == 2:
        ap = ap.rearrange("p (a b) -> p a b", a=shape[0])
    elif len(shape) == 3:
        ap = ap.rearrange("p (a b c) -> p a b c", a=shape[0], b=shape[1])
    return ap


def prologue(C):
    P, nc = C.P, C.nc
    I32 = mybir.dt.int32
    P.dma("sp", lambda h: h.dma_start(out=C.vec[:], in_=C.vecs), writes=["vec"])
    P.dma("sp", lambda h: h.dma_start(out=C.cin[:], in_=C.cT.rearrange("(k p) b -> p k b", p=128)), writes=["cin"])
    P.op("pool", lambda h: h.iota(C.iot_i[:], pattern=[[1, 896]], base=-384, channel_multiplier=0), writes=["iot_i"])
    P.op("pool", lambda h: h.iota(C.pid_i[:, 0:1], pattern=[[0, 1]], base=0, channel_multiplier=1), writes=["pid_i"])
    P.op("dve", lambda h: h.tensor_copy(out=C.iot_f[:], in_=C.iot_i[:]), reads=["iot_i"], writes=["iot_f"])
    P.op("dve", lambda h: h.tensor_copy(out=C.pid_f[:, 0:1], in_=C.pid_i[:, 0:1]), reads=["pid_i"], writes=["pid_f"])
    P.op("dve", lambda h: h.tensor_single_scalar(out=C.pid_i[:, 1:2], in_=C.pid_i[:, 0:1], scalar=4, op=ALU.arith_shift_right),
         reads=["pid_i"], writes=["pid_i2"])
    P.op("dve", lambda h: h.tensor_copy(out=C.pid_f[:, 1:2], in_=C.pid_i[:, 1:2]), reads=["pid_i2"], writes=["pid_f2"])
    pidx = C.pid_f[:, 0:1]
    col128 = C.iot_f[:, 384:512]
    rd = ["iot_f", "pid_f"]
    P.op("dve", lambda h: h.tensor_single_scalar(out=C.maskbig[:], in_=C.iot_f[:], scalar=pidx, op=ALU.is_gt), reads=rd, writes=["maskbig"])
    P.op("dve", lambda h: h.tensor_single_scalar(out=C.tri_i[:], in_=col128, scalar=pidx, op=ALU.is_le), reads=rd, writes=["tri_i"])
    P.op("dve", lambda h: h.tensor_single_scalar(out=C.tri_r[:], in_=col128, scalar=pidx, op=ALU.is_gt), reads=rd, writes=["tri_r"])
    P.op("dve", lambda h: h.tensor_single_scalar(out=C.ident[:], in_=col128, scalar=pidx, op=ALU.is_equal), reads=rd, writes=["ident"])
    P.op("dve", lambda h: h.tensor_single_scalar(out=C.identf[:], in_=col128, scalar=pidx, op=ALU.is_equal), reads=rd, writes=["identf"])
    P.op("dve", lambda h: h.memset(C.ones[:], 1.0 / D), writes=["ones"])
    P.op("dve", lambda h: h.tensor_single_scalar(out=C.iot_i[:, 0:128], in_=C.iot_i[:, 384:512], scalar=4, op=ALU.arith_shift_right),
         reads=["iot_i", "iot_f"], writes=["iot_i"])
    P.op("dve", lambda h: h.tensor_copy(out=C.iot_f[:, 0:128], in_=C.iot_i[:, 0:128]), reads=["iot_i", "maskbig"], writes=["iot_f"])
    P.op("dve", lambda h: h.tensor_single_scalar(out=C.bmask[:], in_=C.iot_f[:, 0:128], scalar=C.pid_f[:, 1:2], op=ALU.is_ge),
         reads=["iot_f", "pid_f2"], writes=["bmask"])
    P.op("act", lambda h: h.activation(out=C.cact[:], in_=C.cin[:], func=AF.Silu), reads=["cin"], writes=["cact"])
    def mod_tasks(w2d_fn, ncols, col0, bias0, bank):
        tasks = []
        for s_ in range(ncols // 512):
            def task(s_=s_):
                ps = C.bank(bank)
                (wv,), wk = C.W.next(lambda C_, w2d_fn=w2d_fn, s_=s_: [wslab(w2d_fn(C_), 512 * s_, 512)])
                for q in range(4):
                    for k in range(KC):
                        P.op("pe", lambda h, wv=wv, q=q, k=k, ps=ps: h.matmul(
                            ps[:, NB * q:NB * q + NB], lhsT=wv[:, k, 128 * q:128 * q + 128], rhs=C.cact[:, k, :],
                            start=(k == 0), stop=(k == KC - 1)),
                            reads=[wk, "cact"], writes=[("ps", bank)])
                c_ = col0 + 4 * s_
                b_ = bias0 + 4 * s_
                P.op("dve", lambda h, ps=ps, c_=c_, b_=b_: h.tensor_tensor(
                    out=C.mod[:, c_:c_ + 4, :], in0=ps[:, 0:NB * 4].rearrange("p (c b) -> p c b", b=NB),
                    in1=C.vec[:, b_:b_ + 4].unsqueeze(2).to_broadcast([128, 4, NB]), op=ALU.add),
                    reads=[("ps", bank), "vec"], writes=["mod"])
            tasks.append(task)
        return tasks

    sites = [(V_NMIX, 8), (V_NFFN, 32), (V_NMIX + 8, 48 + 8), (V_NFFN + 8, 48 + 32), (V_NOUT, 96 + 8)]

    def site_task(si):
        gcol, sccol = sites[si]
        P.op("dve", lambda h, si=si, gcol=gcol, sccol=sccol: h.scalar_tensor_tensor(
            out=C.A[:, si, :, :], in0=C.mod[:, sccol:sccol + 8, :], scalar=1.0,
            in1=C.vec[:, gcol:gcol + 8].unsqueeze(2).to_broadcast([128, 8, NB]), op0=ALU.add, op1=ALU.mult),
            reads=["mod", "vec"], writes=[("A", si)])

    for t_ in mod_tasks(lambda C_: C_.w_mod[0], 6 * D, 0, V_BMOD, 7):
        t_()
    site_task(0)
    site_task(1)
    C.deferred = (mod_tasks(lambda C_: C_.w_mod[1], 6 * D, 48, V_BMOD + 48, 7)
                  + mod_tasks(lambda C_: C_.w_fin, 2 * D, 96, V_BFIN, 7)
                  + [lambda: site_task(2), lambda: site_task(3), lambda: site_task(4)])
    if "attn" not in C.stages:
        run_deferred(C, 10 ** 6)
    if "ssm" in C.stages:
        ssm_prologue(C)
    P.barrier(keep=lambda k: isinstance(k, tuple) and k[0] == "wb")


def run_deferred(C, n):
    while n > 0 and C.deferred:
        C.deferred.pop(0)()
        n -= 1


def norm_mod(C, b, site, shcol, out_fn, out_dtype_bf16=True):
    P = C.P
    sq = [carve(C, C.NT_OFF + 1024 * i, [512], BF16) for i in range(2)]
    rs = carve(C, C.NT_OFF + 2048, [512], F32)
    tmp = [carve(C, C.NT_OFF + 4096 + 2048 * i, [512], F32) for i in range(2)]
    for tt in range(4):
        ts = slice(512 * tt, 512 * tt + 512)
        ps = C.bank(tt % 2)
        for k in range(KC):
            P.op("act", lambda h, k=k, ts=ts: h.activation(out=sq[k % 2], in_=C.x[:, k, ts], func=AF.Square),
                 reads=[("x", k, tt)], writes=[("nsq", k % 2)])
            P.op("pe", lambda h, k=k, ps=ps: h.matmul(ps, lhsT=C.ones[:], rhs=sq[k % 2], start=(k == 0), stop=(k == KC - 1)),
                 reads=[("nsq", k % 2), "ones"], writes=[("ps", tt % 2)])
        P.op("act", lambda h, ps=ps: h.activation(out=rs, in_=ps, func=AF.Sqrt, bias=EPS, scale=1.0),
             reads=[("ps", tt % 2)], writes=["nrs"])
        P.op("dve", lambda h: h.reciprocal(out=rs, in_=rs), reads=["nrs"], writes=["nrs"])
        for k in range(KC):
            o, okey = out_fn(k, tt)
            P.op("dve", lambda h, k=k, ts=ts: h.tensor_tensor(out=tmp[k % 2], in0=C.x[:, k, ts], in1=rs, op=ALU.mult),
                 reads=[("x", k, tt), "nrs"], writes=[("ntmp", k % 2)])
            P.op("act", lambda h, k=k, o=o: h.activation(out=o, in_=tmp[k % 2], func=AF.Identity,
                                                         bias=C.mod[:, shcol + k, b:b + 1], scale=C.A[:, site, k, b:b + 1]),
                 reads=[("ntmp", k % 2), "mod", ("A", site)], writes=[okey])


def load_x(C, b):
    P = C.P
    for k in range(KC):
        P.dma("sp", lambda h, k=k: h.dma_start(out=C.x[:, k, :], in_=C.xT[b, 128 * k:128 * k + 128, :]),
              writes=[("x", k, tt) for tt in range(4)])


def final_out(C, b):
    obuf = [carve(C, 16384 + 2048 * i, [512], F32) for i in range(4)]
    cnt = [0]

    def out_fn(k, tt):
        i = cnt[0] % 4
        cnt[0] += 1
        return obuf[i], ("obuf", i)

    norm_mod_out(C, b, 4, 96, out_fn, obuf)


def norm_mod_out(C, b, site, shcol, out_fn, obuf):
    P = C.P
    state = {"n": 0}

    def wrapped(k, tt):
        return out_fn(k, tt)

    sq = [carve(C, C.NT_OFF + 1024 * i, [512], BF16) for i in range(2)]
    rs = carve(C, C.NT_OFF + 2048, [512], F32)
    tmp = [carve(C, C.NT_OFF + 4096 + 2048 * i, [512], F32) for i in range(2)]
    for tt in range(4):
        ts = slice(512 * tt, 512 * tt + 512)
        ps = C.bank(tt % 2)
        for k in range(KC):
            P.op("act", lambda h, k=k, ts=ts: h.activation(out=sq[k % 2], in_=C.x[:, k, ts], func=AF.Square),
                 reads=[("x", k, tt)], writes=[("nsq", k % 2)])
            P.op("pe", lambda h, k=k, ps=ps: h.matmul(ps, lhsT=C.ones[:], rhs=sq[k % 2], start=(k == 0), stop=(k == KC - 1)),
                 reads=[("nsq", k % 2), "ones"], writes=[("ps", tt % 2)])
        P.op("act", lambda h, ps=ps: h.activation(out=rs, in_=ps, func=AF.Sqrt, bias=EPS, scale=1.0),
             reads=[("ps", tt % 2)], writes=["nrs"])
        P.op("dve", lambda h: h.reciprocal(out=rs, in_=rs), reads=["nrs"], writes=["nrs"])
        for k in range(KC):
            o, okey = wrapped(k, tt)
            P.op("dve", lambda h, k=k, ts=ts: h.tensor_tensor(out=tmp[k % 2], in0=C.x[:, k, ts], in1=rs, op=ALU.mult),
                 reads=[("x", k, tt), "nrs"], writes=[("ntmp", k % 2)])
            P.op("act", lambda h, k=k, o=o: h.activation(out=o, in_=tmp[k % 2], func=AF.Identity,
                                                         bias=C.mod[:, shcol + k, b:b + 1], scale=C.A[:, site, k, b:b + 1]),
                 reads=[("ntmp", k % 2), "mod", ("A", site)], writes=[okey])
            P.dma("sp", lambda h, k=k, ts=ts, o=o: h.dma_start(out=C.outT[b, 128 * k:128 * k + 128, ts], in_=o),
                  reads=[okey], writes=[("out", b, k, tt)], out=True)


def dump_x(C, b):
    P = C.P
    for k in range(KC):
        P.dma("sp", lambda h, k=k: h.dma_start(out=C.outT[b, 128 * k:128 * k + 128, :], in_=C.x[:, k, :]),
              reads=[("x", k, tt) for tt in range(4)], writes=[("out", b, k)], out=True)


def run_sequence(C, b):
    P = C.P
    keepw = lambda k: isinstance(k, tuple) and k[0] == "wb"
    load_x(C, b)
    for stg in C.stages:
        if stg == "attn":
            attn_layer(C, b)
            run_deferred(C, 10 ** 6)
            if isinstance(C.debug, str) and (C.debug.startswith("attn") or C.debug == "wodump"):
                P.barrier(keep=keepw)
                return
        elif stg == "ffn0":
            ffn_layer(C, b, 0)
        elif stg == "ssm":
            ssm_layer(C, b)
        elif stg == "ffn1":
            ffn_layer(C, b, 1)
        elif stg == "dump":
            dump_x(C, b)
            P.barrier(keep=keepw)
            return
        P.barrier(keep=keepw)
    final_out(C, b)
    P.barrier(keep=keepw)


def _colT(v):
    return np.ascontiguousarray(v.reshape(-1, 128).T)


def prep_inputs(inp, core):
    f = lambda a: np.ascontiguousarray(np.asarray(a, dtype=np.float32))
    bs = slice(NB * core, NB * core + NB)
    m = {}
    m["xT"] = f(np.asarray(inp["x"])[bs].transpose(0, 2, 1))
    m["cT"] = f(np.asarray(inp["c"])[bs].T)
    vec = np.zeros((128, NV), np.float32)
    for i in range(2):
        vec[:, V_NMIX + 8 * i:V_NMIX + 8 * i + 8] = _colT(np.asarray(inp["norm_mix"])[i])
        vec[:, V_NFFN + 8 * i:V_NFFN + 8 * i + 8] = _colT(np.asarray(inp["norm_ffn"])[i])
        vec[:, V_BMOD + 48 * i:V_BMOD + 48 * i + 48] = _colT(np.asarray(inp["b_mod"])[i])
        cw = np.asarray(inp["conv_w"])[i].reshape(3, 2 * FJ, 128).transpose(2, 1, 0).reshape(128, 2 * FJ * 3)
        vec[:, V_CW + 132 * i:V_CW + 132 * i + 132] = cw
        vec[:, V_CB + 44 * i:V_CB + 44 * i + 44] = _colT(np.asarray(inp["conv_b"])[i])
    vec[:, V_NOUT:V_NOUT + 8] = _colT(np.asarray(inp["norm_out"]))
    vec[:, V_BFIN:V_BFIN + 16] = _colT(np.asarray(inp["b_fin"]))
    vec[:, V_BGLU:V_BGLU + 8] = _colT(np.asarray(inp["b_glu"])[0])
    dsk = np.asarray(inp["d_skip"])[0].reshape(NG, 16).T
    vec[:, V_DSK:V_DSK + 64] = np.tile(dsk, (8, 1))
    m["vecs"] = vec
    for k in ("w_mod", "w_fin", "w_up", "w_down"):
        m[k] = f(inp[k])
    for k in ("w_qkv", "w_o_attn", "w_in_ssm", "w_glu", "w_o_ssm"):
        m[k] = f(np.asarray(inp[k])[0])
    r2 = lambda a: np.asarray(a)[0].reshape(32, 2, 64).transpose(1, 2, 0).reshape(128, 32)
    ldt = np.broadcast_to(np.asarray(inp["log_dt"])[0].reshape(32, 2, 1), (32, 2, 64)).transpose(1, 2, 0).reshape(128, 32)
    m["ssm_s"] = f(np.stack([r2(inp["a_re"]), r2(inp["a_im"]), ldt], axis=1))
    bp = lambda a: np.asarray(a)[0].reshape(32, 2, 64, 16).transpose(1, 2, 0, 3).reshape(128, 32, 16)
    cp = lambda a: np.asarray(a)[0].reshape(32, 2, 16, 64).transpose(1, 3, 0, 2).reshape(128, 32, 16)
    m["ssm_bc"] = f(np.stack([bp(inp["b_re"]), bp(inp["b_im"]), cp(inp["c_re"]), cp(inp["c_im"])], axis=1))
    return m


_CACHE = {}


def get_program(stages=("attn", "ffn0", "ssm", "ffn1"), debug=None):
    key = (tuple(stages), debug)
    if key not in _CACHE:
        _, C0 = build(stages, schedule=None, debug=debug)
        nc, C = build(stages, schedule=C0.requests, debug=debug)
        _CACHE[key] = (nc, C)
    return _CACHE[key]


def kernel(**inputs):
    nc, C = get_program()
    in_maps = [prep_inputs(inputs, c) for c in range(8)]
    res = run_bass_kernel_spmd(nc, in_maps, core_ids=list(range(8)))
    outs = [np.asarray(r["outT"]).transpose(0, 2, 1) for r in res.results]
    return np.ascontiguousarray(np.concatenate(outs, axis=0).astype(np.float32))


def attn_layer(C, b):
    P = C.P
    h = carve(C, 0, [KC, S], BF16)
    qT = carve(C, 32768, [S], BF16)
    kT = carve(C, 36864, [S], BF16)
    V = carve(C, 40960, [16, 128], BF16)
    oT = carve(C, 45056, [S], BF16)
    NWK = 4
    wk = lambda nm, i: carve(C, 49152 + 4096 * {"e": 0, "ln": 1, "en": 2, "wt": 3}[nm] + 1024 * i, [512], BF16)

    def hout(k, tt):
        return h[:, k, 512 * tt:512 * tt + 512], ("h", k, tt)

    norm_mod(C, b, 0, 0, hout)
    hkeys_tt = lambda tt: [("h", k, tt) for k in range(KC)]
    scale = DH ** -0.5
    g1col = 16

    for g in range(getattr(C, 'ngroups', 8)):
        (wq, wkk, wv), wkey = C.W.next(lambda C_, g=g: [wslab(C_.w_qkv, 128 * g, 128), wslab(C_.w_qkv, D + 128 * g, 128),
                                                      wslab(C_.w_qkv, 2 * D + 128 * g, 128)])
        for which, wmat, dst, dkey in ((0, wq, qT, "qT"), (1, wkk, kT, "kT")):
            for tt in range(4):
                bk = 6 + (tt % 2)
                ps = C.bank(bk)
                for k in range(KC):
                    P.op("pe", lambda h_, wmat=wmat, k=k, tt=tt, ps=ps: h_.matmul(
                        ps, lhsT=wmat[:, k, :], rhs=h[:, k, 512 * tt:512 * tt + 512], start=(k == 0), stop=(k == KC - 1)),
                        reads=[wkey, ("h", k, tt)], writes=[("ps", bk)])
                eng = "act" if tt % 2 == 0 else "dve"
                if eng == "act":
                    P.op("act", lambda h_, dst=dst, tt=tt, ps=ps: h_.activation(out=dst[:, 512 * tt:512 * tt + 512], in_=ps, func=AF.Copy),
                         reads=[("ps", bk)], writes=[(dkey, tt)])
                else:
                    P.op("dve", lambda h_, dst=dst, tt=tt, ps=ps: h_.tensor_copy(out=dst[:, 512 * tt:512 * tt + 512], in_=ps),
                         reads=[("ps", bk)], writes=[(dkey, tt)])
        for q4 in range(4):
            bk = 6 + (q4 % 2)
            ps = C.bank(bk)
            for j in range(4):
                kb = 4 * q4 + j
                for k in range(KC):
                    P.op("pe", lambda h_, k=k, kb=kb, j=j, ps=ps, wv=wv: h_.matmul(
                        ps[:, 128 * j:128 * j + 128], lhsT=h[:, k, 128 * kb:128 * kb + 128], rhs=wv[:, k, :],
                        start=(k == 0), stop=(k == KC - 1)),
                        reads=[wkey, ("h", k, kb // 4)], writes=[("ps", bk)])
            P.op("act" if q4 % 2 == 0 else "dve",
                 (lambda h_, q4=q4, ps=ps: h_.activation(out=V[:, 4 * q4:4 * q4 + 4, :], in_=ps.rearrange("p (a b) -> p a b", a=4), func=AF.Copy))
                 if q4 % 2 == 0 else
                 (lambda h_, q4=q4, ps=ps: h_.tensor_copy(out=V[:, 4 * q4:4 * q4 + 4, :], in_=ps.rearrange("p (a b) -> p a b", a=4))),
                 reads=[("ps", bk)], writes=[("V", q4)])

        tiles = []
        for qc in range(4):
            nkb = 4 * qc + 4
            for n, kb in enumerate(range(nkb - 1, -1, -1)):
                for hd in range(2):
                    tiles.append(dict(qc=qc, hd=hd, kb=kb, first=(n == 0), last=(kb == 0), diag=(kb >= 4 * qc), i=kb - 4 * qc,
                                      c0=(128 * (kb - 4 * qc) if kb >= 4 * qc else 0)))
        NT = len(tiles)

        def s_qk(t):
            T = tiles[t]
            zb = t % 2
            c0 = T["c0"]
            hp = slice(64 * T["hd"], 64 * T["hd"] + 64)
            P.op("pe", lambda h_, T=T, zb=zb, hp=hp, c0=c0: h_.matmul(
                C.bank(zb)[:, c0:512], lhsT=kT[hp, 128 * T["kb"]:128 * T["kb"] + 128],
                rhs=qT[hp, 512 * T["qc"] + c0:512 * T["qc"] + 512], start=True, stop=True),
                reads=[("kT", T["kb"] // 4), ("qT", T["qc"])], writes=[("ps", zb)])

        def s_expa(t):
            T = tiles[t]
            zb = t % 2
            w = t % NWK
            c0 = T["c0"]
            P.op("act", lambda h_, zb=zb, w=w, c0=c0: h_.activation(out=wk("e", w)[:, c0:512], in_=C.bank(zb)[:, c0:512], func=AF.Exp, scale=scale),
                 reads=[("ps", zb)], writes=[("e", w)])
            if T["diag"]:
                P.op("dve", lambda h_, w=w, c0=c0: h_.tensor_tensor(
                    out=wk("e", w)[:, c0:c0 + 128], in0=wk("e", w)[:, c0:c0 + 128], in1=C.maskbig[:, 384:512], op=ALU.mult),
                    reads=[("e", w), "maskbig"], writes=[("e", w)])

        def s_expb(t):
            w = t % NWK
            c0 = tiles[t]["c0"]
            P.op("act", lambda h_, w=w, c0=c0: h_.activation(out=wk("ln", w)[:, c0:512], in_=wk("e", w)[:, c0:512], func=AF.Ln, bias=1.0, scale=1.0),
                 reads=[("e", w)], writes=[("ln", w)])

        def split_cols(T):
            return [(T["c0"], 512, T["first"])]

        def s_mm1(t):
            T = tiles[t]
            sb_ = 2 + (T["qc"] * 2 + T["hd"]) % 2
            w = t % NWK
            for (a_, b_, first) in split_cols(T):
                P.op("pe", lambda h_, sb_=sb_, w=w, a_=a_, b_=b_, first=first: h_.matmul(
                    C.bank(sb_)[:, a_:b_], lhsT=C.tri_i[:], rhs=wk("ln", w)[:, a_:b_], start=first, stop=True),
                    reads=[("ln", w), "tri_i"], writes=[("ps", sb_)])

        def s_en(t):
            T = tiles[t]
            sb_ = 2 + (T["qc"] * 2 + T["hd"]) % 2
            w = t % NWK
            c0 = T["c0"]
            P.op("act", lambda h_, sb_=sb_, w=w, c0=c0: h_.activation(out=wk("en", w)[:, c0:512], in_=C.bank(sb_)[:, c0:512], func=AF.Exp, scale=-1.0),
                 reads=[("ps", sb_)], writes=[("en", w)])

        def s_mm2(t):
            T = tiles[t]
            if T["last"]:
                return
            sb_ = 2 + (T["qc"] * 2 + T["hd"]) % 2
            w = t % NWK
            c0 = T["c0"]
            P.op("pe", lambda h_, sb_=sb_, w=w, c0=c0: h_.matmul(
                C.bank(sb_)[:, c0:512], lhsT=C.tri_r[:], rhs=wk("ln", w)[:, c0:512], start=False, stop=True),
                reads=[("ln", w), "tri_r"], writes=[("ps", sb_)])

        def s_w(t):
            w = t % NWK
            c0 = tiles[t]["c0"]
            P.op("dve", lambda h_, w=w, c0=c0: h_.tensor_tensor(out=wk("wt", w)[:, c0:512], in0=wk("e", w)[:, c0:512], in1=wk("en", w)[:, c0:512], op=ALU.mult),
                 reads=[("e", w), ("en", w)], writes=[("wt", w)])

        def s_pv(t):
            T = tiles[t]
            ob = 4 + T["qc"] % 2
            w = t % NWK
            hp = slice(64 * T["hd"], 64 * T["hd"] + 64)
            for (a_, b_, first) in split_cols(T):
                P.op("pe", lambda h_, T=T, ob=ob, w=w, hp=hp, a_=a_, b_=b_, first=first: h_.matmul(
                    C.bank(ob)[hp, a_:b_], lhsT=V[:, T["kb"], hp], rhs=wk("wt", w)[:, a_:b_], start=first, stop=True),
                    reads=[("V", T["kb"] // 4), ("wt", w)], writes=[("ps", ob)])
            if T["last"] and T["hd"] == 1:
                qc = T["qc"]
                P.op("dve", lambda h_, ob=ob, qc=qc: h_.tensor_copy(out=oT[:, 512 * qc:512 * qc + 512], in_=C.bank(ob)),
                     reads=[("ps", ob)], writes=[("oT", qc)])

        s_qk(0)
        if NT > 1:
            s_qk(1)
        s_expa(0)
        s_expb(0)
        for t in range(NT):
            if t + 1 < NT:
                s_expa(t + 1)
            s_mm1(t)
            if t + 2 < NT:
                s_qk(t + 2)
            if t >= 1:
                s_pv(t - 1)
            s_en(t)
            if t + 1 < NT:
                s_expb(t + 1)
            s_mm2(t)
            s_w(t)
        s_pv(NT - 1)

        if isinstance(getattr(C, "debug", None), str) and C.debug.startswith("attn") and g == int(C.debug[4:]):
            dbg = carve(C, 65536, [4, S], F32)
            for i, (src, keys) in enumerate(((qT, [("qT", t_) for t_ in range(4)]), (kT, [("kT", t_) for t_ in range(4)]),
                                             (oT, [("oT", t_) for t_ in range(4)]))):
                P.op("dve", lambda h_, i=i, src=src: h_.tensor_copy(out=dbg[:, i, :], in_=src), reads=keys, writes=[("dbg", i)])
                P.dma("sp", lambda h_, i=i: h_.dma_start(out=C.outT[b, 128 * i:128 * i + 128, :], in_=dbg[:, i, :]),
                      reads=[("dbg", i)], writes=[("dbgo", i)], out=True)
            P.op("dve", lambda h_: h_.tensor_copy(out=dbg[:, 3, :], in_=V.rearrange("p a b -> p (a b)")),
                 reads=[("V", q_) for q_ in range(4)], writes=[("dbg", 3)])
            P.dma("sp", lambda h_: h_.dma_start(out=C.outT[b, 384:512, :], in_=dbg[:, 3, :]), reads=[("dbg", 3)], writes=[("dbgo", 3)], out=True)
            for k_ in range(4):
                P.op("dve", lambda h_, k_=k_: h_.tensor_copy(out=dbg[:, k_, :], in_=h[:, k_, :]),
                     reads=[("h", k_, t_) for t_ in range(4)] + [("dbgo", k_)], writes=[("dbg", k_)])
                P.dma("sp", lambda h_, k_=k_: h_.dma_start(out=C.outT[b, 512 + 128 * k_:640 + 128 * k_, :], in_=dbg[:, k_, :]),
                      reads=[("dbg", k_)], writes=[("dbgo2", k_)], out=True)
            return
        (wo,), wokey = C.W.next(lambda C_, g=g: [(C_.w_o_attn[128 * g:128 * g + 128, :], (D,))])
        if getattr(C, "debug", None) == "wodump" and g == 1:
            dbg = carve(C, 65536, [1024], F32)
            P.op("dve", lambda h_: h_.tensor_copy(out=dbg, in_=wo), reads=[wokey], writes=["dbgw"])
            P.dma("sp", lambda h_: h_.dma_start(out=C.outT[b, 0:128, 0:1024], in_=dbg), reads=["dbgw"], writes=["dbgwo"], out=True)
            return
        for oc in range(KC):
            for tt in range(4):
                bk = 6 + (tt % 2)
                ps = C.bank(bk)
                P.op("pe", lambda h_, oc=oc, tt=tt, ps=ps, wo=wo: h_.matmul(
                    ps, lhsT=wo[:, 128 * oc:128 * oc + 128], rhs=oT[:, 512 * tt:512 * tt + 512], start=True, stop=True),
                    reads=[wokey, ("oT", tt)], writes=[("ps", bk)])
                P.op("dve", lambda h_, oc=oc, tt=tt, ps=ps: h_.scalar_tensor_tensor(
                    out=C.x[:, oc, 512 * tt:512 * tt + 512], in0=ps, scalar=C.mod[:, g1col + oc, b:b + 1],
                    in1=C.x[:, oc, 512 * tt:512 * tt + 512], op0=ALU.mult, op1=ALU.add),
                    reads=[("ps", bk), ("x", oc, tt), "mod"], writes=[("x", oc, tt)])
        run_deferred(C, 3)


def ffn_layer(C, b, li):
    P = C.P
    h = carve(C, 0, [KC, S], BF16)
    gT = carve(C, 32768, [FJ, 1024], BF16)
    acc = [carve(C, 77824 + 4096 * i, [1024], F32) for i in range(4)]
    sg = [carve(C, 94208 + 2048 * i, [1024], BF16) for i in range(2)]
    site = 1 if li == 0 else 3
    shcol = 48 * li + 24
    g2col = 48 * li + 40

    def hout(k, tt):
        return h[:, k, 512 * tt:512 * tt + 512], ("h", k, tt)

    norm_mod(C, b, site, shcol, hout)
    cwv = lambda ch, tap: C.vec[:, V_CW + 132 * li + 3 * ch + tap:V_CW + 132 * li + 3 * ch + tap + 1]
    cbv = lambda ch: C.vec[:, V_CB + 44 * li + ch:V_CB + 44 * li + ch + 1]
    for hf in range(2):
        tok0 = 1024 * hf
        for J in range(6):
            nj = min(4, FJ - 4 * J)
            ncol = 128 * nj
            (wg,), wgk = C.W.next(lambda C_, J=J, ncol=ncol: [wslab(C_.w_up[li], 512 * J, ncol)], ahead=1)
            (wvv,), wvk = C.W.next(lambda C_, J=J, ncol=ncol: [wslab(C_.w_up[li], FF + 512 * J, ncol)], ahead=1)
            for jj in range(nj):
                j = 4 * J + jj
                a = (j % 2) * 2
                for which, wmat, wkey_ in ((0, wg, wgk), (1, wvv, wvk)):
                    pqi = a + which
                    pst = C.pq[pqi][:, :]
                    for tl in range(2):
                        bank = 2 * pqi + tl
                        for k in range(KC):
                            P.op("pe", lambda h_, wmat=wmat, k=k, jj=jj, tl=tl, pst=pst, tok0=tok0: h_.matmul(
                                pst[:, 512 * tl:512 * tl + 512], lhsT=wmat[:, k, 128 * jj:128 * jj + 128],
                                rhs=h[:, k, tok0 + 512 * tl:tok0 + 512 * tl + 512], start=(k == 0), stop=(k == KC - 1)),
                                reads=[wkey_, ("h", k, 2 * hf + tl)], writes=[("ps", bank)])
                    ch = j if which == 0 else FJ + j
                    ai = 2 * (j % 2) + which
                    accw = acc[ai]
                    akey = ("acc", ai)
                    pkeys = [("ps", 2 * pqi), ("ps", 2 * pqi + 1)]
                    P.op("act", lambda h_, accw=accw, pst=pst, ch=ch: h_.activation(
                        out=accw, in_=pst, func=AF.Identity, bias=cbv(ch), scale=cwv(ch, 2)),
                        reads=pkeys + ["vec"], writes=[akey])
                    P.op("dve", lambda h_, accw=accw, pst=pst, ch=ch: h_.scalar_tensor_tensor(
                        out=accw[:, 1:1024], in0=pst[:, 0:1023], scalar=cwv(ch, 1), in1=accw[:, 1:1024], op0=ALU.mult, op1=ALU.add),
                        reads=pkeys + ["vec", akey], writes=[akey])
                    P.op("dve", lambda h_, accw=accw, pst=pst, ch=ch: h_.scalar_tensor_tensor(
                        out=accw[:, 2:1024], in0=pst[:, 0:1022], scalar=cwv(ch, 0), in1=accw[:, 2:1024], op0=ALU.mult, op1=ALU.add),
                        reads=pkeys + ["vec", akey], writes=[akey])
                    if hf == 0:
                        P.op("act", lambda h_, pst=pst, ch=ch: h_.activation(out=C.halo[:, ch, :], in_=pst[:, 1022:1024], func=AF.Copy),
                             reads=pkeys, writes=[("halo", ch)])
                    else:
                        P.op("dve", lambda h_, accw=accw, ch=ch: h_.scalar_tensor_tensor(
                            out=accw[:, 0:1], in0=C.halo[:, ch, 1:2], scalar=cwv(ch, 1), in1=accw[:, 0:1], op0=ALU.mult, op1=ALU.add),
                            reads=[("halo", ch), "vec", akey], writes=[akey])
                        P.op("dve", lambda h_, accw=accw, ch=ch: h_.scalar_tensor_tensor(
                            out=accw[:, 0:2], in0=C.halo[:, ch, 0:2], scalar=cwv(ch, 0), in1=accw[:, 0:2], op0=ALU.mult, op1=ALU.add),
                            reads=[("halo", ch), "vec", akey], writes=[akey])
                sgw = sg[j % 2]
                ag, av = acc[2 * (j % 2)], acc[2 * (j % 2) + 1]
                P.op("act", lambda h_, sgw=sgw, ag=ag: h_.activation(out=sgw, in_=ag, func=AF.Silu),
                     reads=[("acc", 2 * (j % 2))], writes=[("sg", j % 2)])
                P.op("pool", lambda h_, sgw=sgw, av=av, j=j: h_.tensor_tensor(out=gT[:, j, :], in0=sgw, in1=av, op=ALU.mult),
                     reads=[("sg", j % 2), ("acc", 2 * (j % 2) + 1)], writes=[("gT", j)])
        for oc in range(KC):
            (wd,), wdk = C.W.next(lambda C_, oc=oc: [wslab(C_.w_down[li], 128 * oc, 128, kk=FJ)], ahead=2)
            for tl in range(2):
                bank = (2 * oc + tl) % 8
                ps = C.bank(bank)
                for j in range(FJ):
                    P.op("pe", lambda h_, wd=wd, j=j, tl=tl, ps=ps: h_.matmul(
                        ps, lhsT=wd[:, j, :], rhs=gT[:, j, 512 * tl:512 * tl + 512], start=(j == 0), stop=(j == FJ - 1)),
                        reads=[wdk, ("gT", j)], writes=[("ps", bank)])
                tt = 2 * hf + tl
                P.op("dve", lambda h_, oc=oc, tt=tt, ps=ps: h_.scalar_tensor_tensor(
                    out=C.x[:, oc, 512 * tt:512 * tt + 512], in0=ps, scalar=C.mod[:, g2col + oc, b:b + 1],
                    in1=C.x[:, oc, 512 * tt:512 * tt + 512], op0=ALU.mult, op1=ALU.add),
                    reads=[("ps", bank), ("x", oc, tt), "mod"], writes=[("x", oc, tt)])


def ssm_prologue(C):
    P = C.P
    I32 = mybir.dt.int32
    off = [8192]

    def al(shape, dtype=F32):
        n = int(np.prod(shape)) * (4 if dtype in (F32, I32) else 2)
        o = off[0]
        off[0] += (n + 63) // 64 * 64
        if dtype == I32:
            return C.arena[:, o // 2:o // 2 + 2 * int(np.prod(shape))].bitcast(I32)
        return carve(C, o, shape, dtype)

    cnt = [0]

    def tt_(eng, out, in0, in1, op, rd, wr):
        P.op(eng, lambda h, out=out, in0=in0, in1=in1, op=op: h.tensor_tensor(out=out, in0=in0, in1=in1, op=op), reads=rd, writes=wr)

    S_ = al([3, 32])
    BC = al([4, 32, 16])
    P.dma("sp", lambda h: h.dma_start(out=S_, in_=C.ssm_s), writes=["S_"])
    P.dma("sp", lambda h: h.dma_start(out=BC, in_=C.ssm_bc), writes=["BC"])
    a_re, a_im, ldt = S_[:, 0, :], S_[:, 1, :], S_[:, 2, :]
    dt_ = al([32]); ar = al([32]); th = al([32]); mag = al([32])
    P.op("act", lambda h: h.activation(out=dt_, in_=ldt, func=AF.Exp), reads=["S_"], writes=["dt"])
    tt_("dve", ar, a_re, dt_, ALU.mult, ["S_", "dt"], ["ar"])
    tt_("dve", th, a_im, dt_, ALU.mult, ["S_", "dt"], ["th"])
    P.op("act", lambda h: h.activation(out=mag, in_=ar, func=AF.Exp), reads=["ar"], writes=["mag"])
    trig = {}
    for nm, shift in (("sin", 0.0), ("cos", 0.25)):
        y = al([32]); ni = al([32], I32); nf = al([32]); f = al([32]); v = al([32])
        P.op("dve", lambda h, y=y, shift=shift: h.tensor_scalar(out=y, in0=th, scalar1=1.0 / TWO_PI, scalar2=shift, op0=ALU.mult, op1=ALU.add),
             reads=["th"], writes=[("y", nm)])
        P.op("dve", lambda h, y=y, ni=ni: h.tensor_copy(out=ni, in_=y), reads=[("y", nm)], writes=[("ni", nm)])
        P.op("dve", lambda h, nf=nf, ni=ni: h.tensor_copy(out=nf, in_=ni), reads=[("ni", nm)], writes=[("nf", nm)])
        tt_("dve", f, y, nf, ALU.subtract, [("y", nm), ("nf", nm)], [("f", nm)])
        P.op("act", lambda h, v=v, f=f: h.activation(out=v, in_=f, func=AF.Sin, scale=TWO_PI * (1.0 - 1e-6)), reads=[("f", nm)], writes=[("trig", nm)])
        trig[nm] = v
    if getattr(C, 'pro_lim', 99) < 1:
        return
    Lr = al([32]); Li = al([32])
    tt_("dve", Lr, mag, trig["cos"], ALU.mult, ["mag", ("trig", "cos")], ["Lr"])
    tt_("dve", Li, mag, trig["sin"], ALU.mult, ["mag", ("trig", "sin")], ["Li"])
    nr = al([32]); den = al([32]); t1 = al([32]); t2 = al([32]); cr = al([32]); ci = al([32])
    P.op("dve", lambda h: h.tensor_scalar(out=nr, in0=Lr, scalar1=-1.0, scalar2=None, op0=ALU.add), reads=["Lr"], writes=["nr"])
    tt_("dve", t1, a_re, a_re, ALU.mult, ["S_"], ["t1"])
    tt_("dve", t2, a_im, a_im, ALU.mult, ["S_"], ["t2"])
    tt_("dve", den, t1, t2, ALU.add, ["t1", "t2"], ["den"])
    P.op("dve", lambda h: h.reciprocal(out=den, in_=den), reads=["den"], writes=["den"])
    tt_("dve", t1, nr, a_re, ALU.mult, ["nr", "S_", "den"], ["t1"])
    tt_("dve", t2, Li, a_im, ALU.mult, ["Li", "S_", "den"], ["t2"])
    tt_("dve", cr, t1, t2, ALU.add, ["t1", "t2"], ["cr0"])
    tt_("dve", cr, cr, den, ALU.mult, ["cr0", "den"], ["cr"])
    tt_("dve", t1, Li, a_re, ALU.mult, ["Li", "S_", "cr0"], ["t1"])
    tt_("dve", t2, nr, a_im, ALU.mult, ["nr", "S_", "cr0"], ["t2"])
    tt_("dve", ci, t1, t2, ALU.subtract, ["t1", "t2"], ["ci0"])
    tt_("dve", ci, ci, den, ALU.mult, ["ci0", "den"], ["ci"])
    if getattr(C, 'pro_lim', 99) < 2:
        return
    PW = al([2, 9, 32])
    P.op("dve", lambda h: h.memset(PW[:, 0, 0, :], 1.0), reads=["ci"], writes=[("pw", 0)])
    P.op("dve", lambda h: h.memset(PW[:, 1, 0, :], 0.0), reads=[("pw", 0)], writes=[("pw", 0)])
    for j in range(1, 9):
        pr, pi_ = PW[:, 0, j - 1, :], PW[:, 1, j - 1, :]
        tt_("dve", t1, pr, Lr, ALU.mult, [("pw", j - 1), "Lr", "ci", ("pw", j - 2)], ["t1"])
        tt_("dve", t2, pi_, Li, ALU.mult, [("pw", j - 1), "Li", "ci", ("pw", j - 2)], ["t2"])
        tt_("dve", PW[:, 0, j, :], t1, t2, ALU.subtract, ["t1", "t2"], [("pwr", j)])
        tt_("dve", t1, pr, Li, ALU.mult, [("pw", j - 1), "Li", ("pwr", j)], ["t1"])
        tt_("dve", t2, pi_, Lr, ALU.mult, [("pw", j - 1), "Lr", ("pwr", j)], ["t2"])
        tt_("dve", PW[:, 1, j, :], t1, t2, ALU.add, ["t1", "t2", ("pwr", j)], [("pw", j)])
    pwk = [("pw", j) for j in range(9)]
    P.op("dve", lambda h: h.tensor_copy(out=C.ssD[:, 0, :], in_=PW[:, 0, 8, :]), reads=pwk, writes=["ssD0"])
    P.op("dve", lambda h: h.tensor_copy(out=C.ssD[:, 1, :], in_=PW[:, 1, 8, :]), reads=pwk, writes=["ssD"])
    m2 = al([8, 32]); m3 = al([8, 32]); ivr = al([8, 32]); ivi = al([8, 32]); br = al([8, 32]); bi = al([8, 32])
    pr8, pi8 = PW[:, 0, 0:8, :], PW[:, 1, 0:8, :]
    tt_("dve", m2, pr8, pr8, ALU.mult, pwk, ["m2"])
    tt_("dve", m3, pi8, pi8, ALU.mult, pwk, ["m3"])
    tt_("dve", m2, m2, m3, ALU.add, ["m2", "m3"], ["m2s"])
    P.op("dve", lambda h: h.reciprocal(out=m2, in_=m2), reads=["m2s"], writes=["rm"])
    tt_("dve", ivr, pr8, m2, ALU.mult, pwk + ["rm"], ["ivr"])
    tt_("dve", ivi, pi8, m2, ALU.mult, pwk + ["rm"], ["ivi0"])
    P.op("dve", lambda h: h.tensor_scalar(out=ivi, in0=ivi, scalar1=-1.0, scalar2=None, op0=ALU.mult), reads=["ivi0"], writes=["ivi"])
    crb = cr.unsqueeze(1).to_broadcast([128, 8, 32])
    cib = ci.unsqueeze(1).to_broadcast([128, 8, 32])
    tt_("dve", m2, ivr, crb, ALU.mult, ["ivr", "cr", "ivi"], ["q1"])
    tt_("dve", m3, ivi, cib, ALU.mult, ["ivi", "ci", "ivr"], ["q2"])
    tt_("dve", br, m2, m3, ALU.subtract, ["q1", "q2"], ["br"])
    tt_("dve", m2, ivr, cib, ALU.mult, ["ivr", "ci", "br"], ["q1"])
    tt_("dve", m3, ivi, crb, ALU.mult, ["ivi", "cr", "br"], ["q2"])
    tt_("dve", bi, m2, m3, ALU.add, ["q1", "q2"], ["bi"])
    if getattr(C, 'pro_lim', 99) < 3:
        return
    bar_ = lambda: P.barrier(keep=lambda k: isinstance(k, tuple) and k[0] == "wb")
    off_keep = off[0]
    off[0] = 8192 + 49152
    QMr = al([32, 2, 128], BF16); QMi = al([32, 2, 128], BF16); Kr = al([32, 128], BF16); Ki = al([32, 128], BF16)
    persist_end = off[0]
    off[0] = off_keep
    assert off_keep <= 8192 + 49152 - 8192, off_keep
    u1 = al([32, 16]); u2 = al([32, 16])
    b_re, b_im, c_re, c_im = BC[:, 0], BC[:, 1], BC[:, 2], BC[:, 3]
    P.op("pool", lambda h: h.memset(QMr, 0.0), writes=["QMr0"])
    P.op("pool", lambda h: h.memset(QMi, 0.0), writes=["QMi0"])
    lo, hi = slice(0, 64), slice(64, 128)
    for j in range(8):
        bc = lambda ap: ap.unsqueeze(2).to_broadcast([128, 32, 16])
        sl = slice(16 * j, 16 * j + 16)
        prj, pij = bc(PW[:, 0, j, :]), bc(PW[:, 1, j, :])
        brj, bij = bc(br[:, j, :]), bc(bi[:, j, :])
        dep = ["BC", "br", "bi"] + pwk
        tt_("dve", u1, c_re, prj, ALU.mult, dep + [("Q", j - 1)], ["u1"])
        tt_("dve", u2, c_im, pij, ALU.mult, dep + [("Q", j - 1)], ["u2"])
        tt_("dve", QMr[lo, :, 0, sl], u1[lo], u2[lo], ALU.subtract, ["u1", "u2", "QMr0"], [("Qr0", j)])
        tt_("dve", QMr[hi, :, 1, sl], u1[hi], u2[hi], ALU.subtract, ["u1", "u2", "QMr0"], [("Qr", j)])
        tt_("dve", u1, c_re, pij, ALU.mult, dep + [("Qr", j), ("Qr0", j)], ["u1"])
        tt_("dve", u2, c_im, prj, ALU.mult, dep + [("Qr", j), ("Qr0", j)], ["u2"])
        P.op("dve", lambda h, sl=sl: h.scalar_tensor_tensor(out=QMi[lo, :, 0, sl], in0=u1[lo], scalar=-1.0, in1=u2[lo], op0=ALU.mult, op1=ALU.subtract),
             reads=["u1", "u2", "QMi0"], writes=[("Qi0", j)])
        P.op("dve", lambda h, sl=sl: h.scalar_tensor_tensor(out=QMi[hi, :, 1, sl], in0=u1[hi], scalar=-1.0, in1=u2[hi], op0=ALU.mult, op1=ALU.subtract),
             reads=["u1", "u2", "QMi0"], writes=[("Qi", j)])
        tt_("dve", u1, b_re, brj, ALU.mult, dep + [("Qi", j), ("Qi0", j)], ["u1"])
        tt_("dve", u2, b_im, bij, ALU.mult, dep + [("Qi", j), ("Qi0", j)], ["u2"])
        tt_("dve", Kr[:, :, sl], u1, u2, ALU.subtract, ["u1", "u2"], [("Kr", j)])
        tt_("dve", u1, b_re, bij, ALU.mult, dep + [("Kr", j)], ["u1"])
        tt_("dve", u2, b_im, brj, ALU.mult, dep + [("Kr", j)], ["u2"])
        tt_("dve", Ki[:, :, sl], u1, u2, ALU.add, ["u1", "u2"], [("Q", j)])
    bar_()
    off[0] = 8192
    allq = []
    if getattr(C, 'pro_lim', 99) < 4:
        return
    KTMr = al([32, 2, 128], BF16); KTMi = al([32, 2, 128], BF16)
    assert off[0] <= 8192 + 49152
    tmpK = [carve(C, 110592, [8, 128], BF16) for i_ in range(2)]
    P.op("pool", lambda h: h.memset(KTMr, 0.0), writes=["KTM0"])
    P.op("pool", lambda h: h.memset(KTMi, 0.0), writes=["KTM1"])
    for ri, (Ksrc, KTdst) in enumerate(((Kr, KTMr), (Ki, KTMi))):
        for q in range(4):
            bk = 2 * ri + (q % 2)
            psb = C.bank(bk).bitcast(BF16)
            for e in range(8):
                g2 = 8 * q + e
                P.op("pe", lambda h, Ksrc=Ksrc, g2=g2, e=e, psb=psb: h.transpose(psb[:, 128 * e:128 * e + 128], Ksrc[:, g2, :], C.ident[:]),
                     reads=["ident"], writes=[("ps", bk)])
            tk = tmpK[q % 2]
            P.op("act", lambda h, tk=tk, psb=psb: h.activation(out=tk, in_=psb.rearrange("p (a b) -> p a b", a=8), func=AF.Copy),
                 reads=[("ps", bk)], writes=[("tmpK", 0)])
            P.op("dve", lambda h, KTdst=KTdst, q=q, tk=tk: h.tensor_copy(out=KTdst[:, 8 * q:8 * q + 8, 0, 0:64], in_=tk[:, :, 0:64]),
                 reads=[("tmpK", 0), "KTM0", "KTM1"], writes=[("KT", ri, q, 0)])
            P.op("dve", lambda h, KTdst=KTdst, q=q, tk=tk: h.tensor_copy(out=KTdst[:, 8 * q:8 * q + 8, 1, 64:128], in_=tk[:, :, 64:128]),
                 reads=[("tmpK", 0), "KTM0", "KTM1"], writes=[("KT", ri, q, 1)])
    if getattr(C, 'pro_lim', 99) < 5:
        return
    TT = al([64, 128], BF16)
    tmpT = [carve(C, 106496 + 2048 * i_, [4, 128], F32) for i_ in range(2)]
    assert off[0] <= 8192 + 49152 and persist_end <= 106496, (off[0], persist_end)
    for q in range(16):
        bk = 4 + (q % 2)
        ps = C.bank(bk)
        for e in range(2):
            g2 = 2 * q + e
            P.op("pe", lambda h, g2=g2, e=e, ps=ps: h.matmul(
                ps[:, 256 * e:256 * e + 256], lhsT=Kr[:, g2, :], rhs=QMr[:, g2, :, :].rearrange("p a b -> p (a b)"), start=True, stop=False),
                reads=[], writes=[("ps", bk)])
            P.op("pe", lambda h, g2=g2, e=e, ps=ps: h.matmul(
                ps[:, 256 * e:256 * e + 256], lhsT=Ki[:, g2, :], rhs=QMi[:, g2, :, :].rearrange("p a b -> p (a b)"), start=False, stop=True),
                reads=[], writes=[("ps", bk)])
        tm = tmpT[q % 2]
        P.op("dve", lambda h, tm=tm, ps=ps: h.tensor_tensor(
            out=tm, in0=ps.rearrange("p (a b) -> p a b", a=4), in1=C.bmask[:].unsqueeze(1).to_broadcast([128, 4, 128]), op=ALU.mult),
            reads=[("ps", bk), "bmask"], writes=[("tmT", q % 2)])
        for e in range(4):
            g = 4 * q + e
            P.op("dve", lambda h, tm=tm, g=g, e=e: h.scalar_tensor_tensor(
                out=TT[:, g, :], in0=C.identf[:], scalar=C.vec[:, V_DSK + g:V_DSK + g + 1], in1=tm[:, e, :], op0=ALU.mult, op1=ALU.add),
                reads=[("tmT", q % 2), "identf", "vec"], writes=[("TT", g)])
    if getattr(C, 'pro_lim', 99) < 6:
        return
    bar_()
    flat4 = lambda ap: ap.rearrange("p a b c -> p (a b c)")
    for m_, src in enumerate((QMr, QMi, KTMr, KTMi)):
        P.dma("sp", lambda h, m_=m_, src=src: h.dma_start(out=C.scr_mats[m_], in_=flat4(src)), writes=[("scr_mats", m_)])
    P.dma("sp", lambda h: h.dma_start(out=C.scr_tt, in_=TT.rearrange("p a b -> p (a b)")), writes=["scr_tt"])


def ssm_layer(C, b):
    P = C.P
    keepw = lambda k: isinstance(k, tuple) and k[0] in ("wb", "x")
    bar = lambda: P.barrier(keep=keepw)
    A0, B0, C0, M0 = 0, 32768, 65536, 98304
    h = carve(C, A0, [KC, S], BF16)
    uD = carve(C, B0, [KC, 2, 1024], BF16)
    U = carve(C, A0, [NG, 128], BF16)
    Zbf = carve(C, A0 + 16384, [2, 32, 128], BF16)
    Wst = carve(C, C0, [2, 32 * 128], F32)
    Yg = carve(C, C0, [NG, 128], BF16)
    zt = carve(C, A0, [KC, 1024], BF16)
    gl = carve(C, A0 + 16384, [KC, 1024], BF16)
    tmpf = [carve(C, C0 + 16384 + 4096 * i, [1024], F32) for i in range(2)]
    tmps = [carve(C, C0 + 24576 + 1024 * i, [512], BF16) for i in range(2)]
    ring = [dict(Qr=carve(C, M0 + 6144 * r, [4, 2, 128], BF16), Qi=carve(C, M0 + 6144 * r + 2048, [4, 2, 128], BF16),
                 KTr=carve(C, M0 + 6144 * r, [4, 2, 128], BF16), KTi=carve(C, M0 + 6144 * r + 2048, [4, 2, 128], BF16),
                 TT=carve(C, M0 + 6144 * r + 4096, [8, 128], BF16)) for r in range(2)]
    X0 = M0 + 12288
    sA = carve(C, X0, [2, 32], F32)
    sM1 = carve(C, X0 + 256, [2, 32], F32)
    sM2 = carve(C, X0 + 512, [2, 32], F32)
    DD = carve(C, X0 + 768, [2, 32], F32)
    DX = carve(C, X0 + 1024, [2, 32], F32)
    g1col = 48 + 16

    def hout(k, tt):
        return h[:, k, 512 * tt:512 * tt + 512], ("h", k, tt)

    norm_mod(C, b, 2, 48, hout)
    bar()
    for sl_ in range(2):
        (wi,), wik = C.W.next(lambda C_, sl_=sl_: [wslab(C_.w_in, 512 * sl_, 512)])
        for q in range(4):
            oc = 4 * sl_ + q
            for tt in range(4):
                bk = 6 + (tt % 2)
                ps = C.bank(bk)
                for k in range(KC):
                    P.op("pe", lambda h_, wi=wi, k=k, q=q, tt=tt, ps=ps: h_.matmul(
                        ps, lhsT=wi[:, k, 128 * q:128 * q + 128], rhs=h[:, k, 512 * tt:512 * tt + 512], start=(k == 0), stop=(k == KC - 1)),
                        reads=[wik, ("h", k, tt)], writes=[("ps", bk)])
                hf, c0 = tt // 2, 64 * (tt % 2)
                dst = uD[:, oc, hf, :].rearrange("p (i c) -> p i c", i=8)[:, :, c0:c0 + 64]
                src = ps.rearrange("p (c i) -> p i c", i=8)
                if tt % 2 == 0:
                    P.op("act", lambda h_, dst=dst, src=src: h_.activation(out=dst, in_=src, func=AF.Copy),
                         reads=[("ps", bk)], writes=[("uD", oc, hf, tt % 2)])
                else:
                    P.op("dve", lambda h_, dst=dst, src=src: h_.tensor_copy(out=dst, in_=src),
                         reads=[("ps", bk)], writes=[("uD", oc, hf, tt % 2)])
    P.op("dve", lambda h_: h_.tensor_copy(out=DD[:, 0, :], in_=C.ssD[:, 0, :]), writes=["DD0"])
    P.op("dve", lambda h_: h_.tensor_copy(out=DD[:, 1, :], in_=C.ssD[:, 0, :]), reads=["DD0"], writes=["DD1"])
    P.op("dve", lambda h_: h_.tensor_scalar(out=DX[:, 0, :], in0=C.ssD[:, 1, :], scalar1=-1.0, scalar2=None, op0=ALU.mult), reads=["DD1"], writes=["DX0"])
    P.op("dve", lambda h_: h_.tensor_copy(out=DX[:, 1, :], in_=C.ssD[:, 1, :]), reads=["DX0"], writes=["DX"])
    P.op("dve", lambda h_: h_.memset(C.sscar[:], 0.0), reads=["DX"], writes=["sscar"])
    bar()

    def load_mats(gb, r, which):
        R = ring[r]
        fns = []
        f3 = lambda ap: ap.rearrange("p a b c -> p (a b c)")
        if which == "K":
            fns.append(lambda h_, R=R, gb=gb: h_.dma_start(out=f3(R["KTr"]), in_=C.scr_mats[2][:, 1024 * gb:1024 * gb + 1024]))
            fns.append(lambda h_, R=R, gb=gb: h_.dma_start(out=f3(R["KTi"]), in_=C.scr_mats[3][:, 1024 * gb:1024 * gb + 1024]))
        else:
            fns.append(lambda h_, R=R, gb=gb: h_.dma_start(out=f3(R["Qr"]), in_=C.scr_mats[0][:, 1024 * gb:1024 * gb + 1024]))
            fns.append(lambda h_, R=R, gb=gb: h_.dma_start(out=f3(R["Qi"]), in_=C.scr_mats[1][:, 1024 * gb:1024 * gb + 1024]))
            fns.append(lambda h_, R=R, gb=gb: h_.dma_start(out=R["TT"].rearrange("p a b -> p (a b)"), in_=C.scr_tt[:, 1024 * gb:1024 * gb + 1024]))
        P.dma("sp", fns, writes=[("ring", r)])

    for hf in range(2):
        for k in range(KC):
            P.dma("sp", lambda h_, k=k, hf=hf: h_.dma_start(
                out=C.scr_u[:, 128 * k:128 * k + 128, :].rearrange("i p c -> p i c"),
                in_=uD[:, k, hf, :].rearrange("p (i c) -> p i c", i=8)),
                reads=[("uD", k, hf, 0), ("uD", k, hf, 1)], writes=[("scr_u", k)])
        for i in range(8):
            P.dma("sp", lambda h_, i=i: h_.dma_start(
                out=U[16 * i:16 * i + 16, :, :], in_=C.scr_u[i].rearrange("(g h) c -> h g c", h=16)),
                reads=[("scr_u", k) for k in range(KC)], writes=[("U", i)])
        Ukeys = [("U", i) for i in range(8)]
        load_mats(0, 0, "K")
        for gb in range(8):
            if gb + 1 < 8:
                load_mats(gb + 1, (gb + 1) % 2, "K")
            R = ring[gb % 2]
            br_, bi_ = 2 * (gb % 2), 2 * (gb % 2) + 1
            for g2l in range(4):
                g2 = 4 * gb + g2l
                for bk_, KT in ((br_, R["KTr"]), (bi_, R["KTi"])):
                    for gp in range(2):
                        P.op("pe", lambda h_, bk_=bk_, KT=KT, g2l=g2l, gp=gp, g2=g2: h_.matmul(
                            C.bank(bk_)[:, 128 * g2l:128 * g2l + 128], lhsT=KT[:, g2l, gp, :], rhs=U[:, 2 * g2 + gp, :],
                            start=(gp == 0), stop=(gp == 1)),
                            reads=[("ring", gb % 2)] + Ukeys, writes=[("ps", bk_)])
            for ri, bk_ in ((0, br_), (1, bi_)):
                eng = "act" if ri == 0 else "dve"
                dst = Wst[:, ri, :].rearrange("p (c g) -> p c g", g=32)[:, :, 4 * gb:4 * gb + 4]
                src = C.bank(bk_).rearrange("p (g c) -> p c g", g=4)
                if eng == "act":
                    P.op("act", lambda h_, dst=dst, src=src: h_.activation(out=dst, in_=src, func=AF.Copy),
                         reads=[("ps", bk_)], writes=[("W", gb)])
                else:
                    P.op("dve", lambda h_, dst=dst, src=src: h_.tensor_copy(out=dst, in_=src),
                         reads=[("ps", bk_)], writes=[("W2", gb)])
        bar()
        Wv = Wst.rearrange("p r (c g) -> p r c g", g=32)
        for c in range(128):
            zprev = C.sscar[:] if c == 0 else Wv[:, :, c - 1, :]
            wc = Wv[:, :, c, :]
            ns = c > 0
            P.op("dve", lambda h_, zprev=zprev, wc=wc: h_.tensor_tensor(out=sA, in0=zprev, in1=wc, op=ALU.add), reads=["rec"], writes=["rec"], nosync=ns)
            P.op("dve", lambda h_: h_.tensor_tensor(out=sM1, in0=DD, in1=sA, op=ALU.mult), reads=["rec"], writes=["rec"], nosync=True)
            P.op("dve", lambda h_: h_.tensor_tensor(out=sM2[:, 0, :], in0=DX[:, 0, :], in1=sA[:, 1, :], op=ALU.mult), reads=["rec"], writes=["rec"], nosync=True)
            P.op("dve", lambda h_: h_.tensor_tensor(out=sM2[:, 1, :], in0=DX[:, 1, :], in1=sA[:, 0, :], op=ALU.mult), reads=["rec"], writes=["rec"], nosync=True)
            P.op("dve", lambda h_, wc=wc: h_.tensor_tensor(out=wc, in0=sM1, in1=sM2, op=ALU.add), reads=["rec"], writes=["rec"], nosync=True)
        Zv = Zbf
        P.op("dve", lambda h_: h_.tensor_copy(out=Zv[:, :, :, 0], in_=C.sscar[:]), reads=["rec"], writes=["rec"])
        P.op("dve", lambda h_: h_.tensor_copy(out=Zv[:, 0, :, 1:128], in_=Wv[:, 0, 0:127, :].rearrange("p c g -> p g c")), reads=["rec"], writes=["rec"])
        P.op("act", lambda h_: h_.activation(out=Zv[:, 1, :, 1:128], in_=Wv[:, 1, 0:127, :].rearrange("p c g -> p g c"), func=AF.Copy), reads=["rec"], writes=["rec2"])
        P.op("dve", lambda h_: h_.tensor_copy(out=C.sscar[:], in_=Wv[:, :, 127, :]), reads=["rec"], writes=["rec"])
        bar()
        load_mats(0, 0, "Q")
        for gb in range(8):
            if gb + 1 < 8:
                load_mats(gb + 1, (gb + 1) % 2, "Q")
            R = ring[gb % 2]
            for half in range(2):
                bk_ = 4 + (2 * gb + half) % 2
                for e in range(4):
                    gi = 4 * half + e
                    g = 8 * gb + gi
                    g2l, gp = gi // 2, gi % 2
                    g2 = g // 2
                    rows = slice(64 * gp, 64 * gp + 64)
                    o_ = C.bank(bk_)[:, 128 * e:128 * e + 128]
                    P.op("pe", lambda h_, o_=o_, R=R, gi=gi, g=g: h_.matmul(o_, lhsT=R["TT"][:, gi, :], rhs=U[:, g, :], start=True, stop=False),
                         reads=[("ring", gb % 2)] + Ukeys, writes=[("ps", bk_)])
                    P.op("pe", lambda h_, o_=o_, R=R, g2l=g2l, gp=gp, g2=g2: h_.matmul(
                        o_, lhsT=R["Qr"][:, g2l, gp, :], rhs=Zbf[:, 0, g2, :], start=False, stop=False),
                        reads=[("ring", gb % 2)], writes=[("ps", bk_)])
                    P.op("pe", lambda h_, o_=o_, R=R, g2l=g2l, gp=gp, g2=g2: h_.matmul(
                        o_, lhsT=R["Qi"][:, g2l, gp, :], rhs=Zbf[:, 1, g2, :], start=False, stop=True),
                        reads=[("ring", gb % 2)], writes=[("ps", bk_)])
                g0 = 8 * gb + 4 * half
                dst = Yg[:, g0:g0 + 4, :]
                src = C.bank(bk_).rearrange("p (a b) -> p a b", a=4)
                if half == 0:
                    P.op("act", lambda h_, dst=dst, src=src: h_.activation(out=dst, in_=src, func=AF.Copy), reads=[("ps", bk_)], writes=[("Yg", gb, half)])
                else:
                    P.op("dve", lambda h_, dst=dst, src=src: h_.tensor_copy(out=dst, in_=src), reads=[("ps", bk_)], writes=[("Yg", gb, half)])
        Ygkeys = [("Yg", gb, hh) for gb in range(8) for hh in range(2)]
        for j in range(8):
            P.dma("sp", lambda h_, j=j: h_.dma_start(
                out=C.scr_y[j].rearrange("(g h) c -> h g c", h=16), in_=Yg[16 * j:16 * j + 16, :, :]),
                reads=Ygkeys, writes=[("scr_y", j)])
        bar()
        for k in range(KC):
            P.dma("sp", lambda h_, k=k, hf=hf: h_.dma_start(
                out=uD[:, k, hf, :].rearrange("p (j c) -> p j c", j=8),
                in_=C.scr_y[:, 128 * k:128 * k + 128, :].rearrange("j p c -> p j c")),
                writes=[("yD", k)])
        for k in range(KC):
            yv = uD[:, k, hf, :]
            tf = tmpf[k % 2]
            tkey = ("tf", k % 2)
            P.op("act", lambda h_, tf=tf, yv=yv: h_.activation(out=tf, in_=yv, func=AF.Square), reads=[("yD", k)], writes=[tkey])
            P.op("dve", lambda h_, tf=tf: h_.tensor_scalar(out=tf, in0=tf, scalar1=0.044715, scalar2=1.0, op0=ALU.mult, op1=ALU.add),
                 reads=[tkey], writes=[tkey])
            P.op("dve", lambda h_, tf=tf, yv=yv: h_.tensor_tensor(out=tf, in0=tf, in1=yv, op=ALU.mult), reads=[tkey, ("yD", k)], writes=[tkey])
            P.op("act", lambda h_, tf=tf: h_.activation(out=tf, in_=tf, func=AF.Sigmoid, scale=1.5957691216057308), reads=[tkey], writes=[tkey])
            P.op("dve", lambda h_, tf=tf, yv=yv, k=k: h_.tensor_tensor(out=zt[:, k, :], in0=tf, in1=yv, op=ALU.mult),
                 reads=[tkey, ("yD", k)], writes=[("zt", k)])
        for sl_ in range(2):
            (wg_,), wgk = C.W.next(lambda C_, sl_=sl_: [wslab(C_.w_glu, 512 * sl_, 512)])
            for q in range(4):
                oc = 4 * sl_ + q
                for tl in range(2):
                    bk = 6 + (tl % 2)
                    ps = C.bank(bk)
                    for k in range(KC):
                        P.op("pe", lambda h_, wg_=wg_, k=k, q=q, tl=tl, ps=ps: h_.matmul(
                            ps, lhsT=wg_[:, k, 128 * q:128 * q + 128], rhs=zt[:, k, 512 * tl:512 * tl + 512], start=(k == 0), stop=(k == KC - 1)),
                            reads=[wgk, ("zt", k)], writes=[("ps", bk)])
                    ts_ = tmps[tl % 2]
                    P.op("act", lambda h_, ts_=ts_, ps=ps, oc=oc: h_.activation(
                        out=ts_, in_=ps, func=AF.Sigmoid, bias=C.vec[:, V_BGLU + oc:V_BGLU + oc + 1], scale=1.0),
                        reads=[("ps", bk), "vec"], writes=[("tmps", tl % 2)])
                    P.op("dve", lambda h_, ts_=ts_, oc=oc, tl=tl: h_.tensor_tensor(
                        out=gl[:, oc, 512 * tl:512 * tl + 512], in0=zt[:, oc, 512 * tl:512 * tl + 512], in1=ts_, op=ALU.mult),
                        reads=[("tmps", tl % 2), ("zt", oc)], writes=[("gl", oc)])
        for sl_ in range(2):
            (wo_,), wok = C.W.next(lambda C_, sl_=sl_: [wslab(C_.w_o_ssm, 512 * sl_, 512)])
            for q in range(4):
                oc = 4 * sl_ + q
                for tl in range(2):
                    bk = 6 + (tl % 2)
                    ps = C.bank(bk)
                    for k in range(KC):
                        P.op("pe", lambda h_, wo_=wo_, k=k, q=q, tl=tl, ps=ps: h_.matmul(
                            ps, lhsT=wo_[:, k, 128 * q:128 * q + 128], rhs=gl[:, k, 512 * tl:512 * tl + 512], start=(k == 0), stop=(k == KC - 1)),
                            reads=[wok] + [("gl", kk) for kk in range(KC)], writes=[("ps", bk)])
                    xv = C.x[:, oc, 1024 * hf:1024 * hf + 1024].rearrange("p (c j) -> p j c", j=8)[:, 4 * tl:4 * tl + 4, :]
                    pv = ps.rearrange("p (j c) -> p j c", j=4)
                    P.op("dve", lambda h_, xv=xv, pv=pv, oc=oc: h_.scalar_tensor_tensor(
                        out=xv, in0=pv, scalar=C.mod[:, g1col + oc, b:b + 1], in1=xv, op0=ALU.mult, op1=ALU.add),
                        reads=[("ps", bk), ("x", oc, 2 * hf), ("x", oc, 2 * hf + 1), "mod"], writes=[("x", oc, 2 * hf), ("x", oc, 2 * hf + 1)])
        bar()
```

```python
import contextlib
import numpy as np
import concourse.bass as bass
import concourse.mybir as mybir
from concourse.bass_utils import run_bass_kernel_spmd

F32 = mybir.dt.float32
BF16 = mybir.dt.bfloat16
AF = mybir.ActivationFunctionType
ALU = mybir.AluOpType

ENGS = ("pe", "act", "dve", "pool", "sp")
N_DMA_SEMS = 24


class Prog:
    def __init__(self, nc):
        self.nc = nc
        self.ops = {e: [] for e in ENGS}
        self.reg = {}
        self.dma_use = [0] * N_DMA_SEMS
        self.dma_rr = 0
        self.out_tokens = []
        self.arena_dma = {}
        self.last_compute = {}

    def _collect(self, eng, reads, writes):
        need = {}

        def add(k, v):
            if need.get(k, -1) < v:
                need[k] = v

        for r in reads:
            e = self.reg.get(r)
            if e is not None:
                for k, v in e[0].items():
                    add(k, v)
        for w in writes:
            e = self.reg.get(w)
            if e is not None:
                for k, v in e[0].items():
                    add(k, v)
                for k, v in e[1].items():
                    if k == eng:
                        continue
                    add(k, v)
        if eng == "pe":
            need.pop("pe", None)
        return need

    def _commit(self, toks, reads, writes):
        for r in reads:
            e = self.reg.setdefault(r, [{}, {}])
            for k, v in toks:
                if e[1].get(k, -1) < v:
                    e[1][k] = v
        for w in writes:
            self.reg[w] = [dict(toks), {}]

    def op(self, eng, fn, reads=(), writes=(), nosync=False):
        need = self._collect(eng, reads, writes)
        if nosync:
            need.pop(eng, None)
        idx = len(self.ops[eng])
        self.ops[eng].append([fn, list(need.items()), False, None])
        self._commit([(eng, idx)], reads, writes)
        self.last_compute[eng] = idx
        return idx

    def dma(self, eng, fns, reads=(), writes=(), arena=True, out=False):
        if not isinstance(fns, (list, tuple)):
            fns = [fns]
        need = self._collect("x", reads, writes)
        toks = []
        for n, fn in enumerate(fns):
            i = self.dma_rr
            self.dma_rr = (self.dma_rr + 1) % N_DMA_SEMS
            nd = dict(need) if n == 0 else {}
            if self.dma_use[i] > 0:
                k = ("d", i)
                v = 16 * self.dma_use[i]
                if nd.get(k, -1) < v:
                    nd[k] = v
            self.dma_use[i] += 1
            tok = (("d", i), 16 * self.dma_use[i])
            self.ops[eng].append([fn, list(nd.items()), False, tok])
            toks.append(tok)
            if arena:
                self.arena_dma[tok[0]] = tok[1]
            if out:
                self.out_tokens.append(tok)
        self._commit(toks, reads, writes)
        return toks

    def barrier(self, keep=lambda key: False):
        toks = [(e, i) for e, i in self.last_compute.items()]
        toks += list(self.arena_dma.items())
        for e in ENGS:
            self.ops[e].append([None, [t for t in toks if t[0] != e], False, None])
        self.arena_dma = {}
        self.reg = {k: v for k, v in self.reg.items() if keep(k)}

    def finish(self):
        self.ops["sp"].append([None, list(self.out_tokens), False, None])

    def emit(self):
        nc = self.nc
        for e in ENGS:
            for rec in self.ops[e]:
                for k, v in rec[1]:
                    if isinstance(k, str):
                        assert self.ops[k][v][0] is not None and self.ops[k][v][3] is None
                        self.ops[k][v][2] = True
        sig = {}
        for e in ENGS:
            c = 0
            s = []
            for rec in self.ops[e]:
                if rec[2]:
                    c += 1
                s.append(c)
            sig[e] = s
            assert c < 60000, (e, c)
        self.stats = {e: (len(self.ops[e]), sig[e][-1] if sig[e] else 0) for e in ENGS}
        with contextlib.ExitStack() as st:
            esem = {e: st.enter_context(nc.semaphore("s_" + e)) for e in ENGS}
            dsem = [st.enter_context(nc.semaphore("d%d" % i)) for i in range(N_DMA_SEMS)]
            block = st.enter_context(nc.Block())

            def run(e):
                def body(h):
                    water = {}
                    for fn, deps, signal, dtok in self.ops[e]:
                        for k, v in deps:
                            if isinstance(k, str):
                                sem = esem[k]
                                val = sig[k][v]
                            else:
                                sem = dsem[k[1]]
                                val = v
                            if water.get(k, 0) >= val:
                                continue
                            water[k] = val
                            h.wait_ge(sem, val)
                        if fn is None:
                            continue
                        ins = fn(h)
                        if dtok is not None:
                            ins.then_inc(dsem[dtok[0][1]], 16)
                        elif signal:
                            ins.then_inc(esem[e], 1)
                return body

            block.tensor(run("pe"))
            block.scalar(run("act"))
            block.vector(run("dve"))
            block.gpsimd(run("pool"))
            block.sync(run("sp"))


D = 1024
KC = 8
S = 2048
NB = 2
NH = 16
DH = 64
FF = 2816
FJ = 22
NG = 64
EPS = 1e-6
TWO_PI = 6.283185307179586

V_NMIX, V_NFFN, V_NOUT, V_BMOD, V_BFIN, V_BGLU, V_CW, V_CB, V_DSK, NV = 0, 16, 32, 40, 136, 152, 160, 424, 512, 576
NMOD = 112


class WStream:
    def __init__(self, P, bufs, schedule=None):
        self.P = P
        self.bufs = bufs
        self.n = len(bufs)
        self.schedule = schedule
        self.req = []
        self.issued = 0
        self.cur = 0

    def _issue(self, idx):
        parts = (self.schedule[idx] if self.schedule is not None else self.req[idx])(self.C)
        slot = idx % self.n
        buf = self.bufs[slot]
        fns = []
        off = 0
        for (src, shape) in parts:
            n = int(np.prod(shape))
            dst = buf[:, off:off + n]
            if len(shape) == 2:
                dst = dst.rearrange("p (k n) -> p k n", k=shape[0])
            fns.append(lambda h, dst=dst, src=src: h.dma_start(out=dst, in_=src))
            off += n
        self.P.dma("pool", fns, writes=[("wb", slot)], arena=False)

    def next(self, parts_fn, ahead=2):
        idx = self.cur
        self.cur += 1
        self.req.append(parts_fn)
        parts = parts_fn(self.C)
        lim = idx + ahead if self.schedule is not None else idx
        while self.issued <= lim and (self.schedule is None or self.issued < len(self.schedule)):
            self._issue(self.issued)
            self.issued += 1
        slot = idx % self.n
        buf = self.bufs[slot]
        views = []
        off = 0
        for (src, shape) in parts:
            n = int(np.prod(shape))
            v = buf[:, off:off + n]
            if len(shape) == 2:
                v = v.rearrange("p (k n) -> p k n", k=shape[0])
            views.append(v)
            off += n
        return views, ("wb", slot)


def wslab(w2d, c0, n, r0=0, kk=KC):
    return (w2d[r0:r0 + kk * 128, c0:c0 + n].rearrange("(k p) n -> p k n", p=128), (kk, n))


class Ctx:
    pass


def build(stages=("attn", "ffn0", "ssm", "ffn1"), schedule=None, nseq=NB, debug=None):
    nc = bass.Bass("TRN2", target_bir_lowering=False)
    C = Ctx()
    C.debug = debug
    if isinstance(debug, str) and debug.startswith('pro'):
        C.pro_lim = int(debug[3:4])
        C.tt_lim = int(debug[4:5]) if len(debug) > 4 else 9
    if isinstance(debug, str) and debug.startswith('ng'):
        C.ngroups = int(debug[2:])
    C.nc = nc
    C.stages = stages
    di = lambda name, shape: nc.dram_tensor(name, shape, F32, kind="ExternalInput").ap()
    C.xT = di("xT", [NB, D, S])
    C.cT = di("cT", [D, NB])
    C.vecs = di("vecs", [128, NV])
    C.w_mod = di("w_mod", [2, D, 6 * D])
    C.w_fin = di("w_fin", [D, 2 * D])
    C.w_qkv = di("w_qkv", [D, 3 * D])
    C.w_o_attn = di("w_o_attn", [D, D])
    C.w_in = di("w_in_ssm", [D, D])
    C.w_glu = di("w_glu", [D, D])
    C.w_o_ssm = di("w_o_ssm", [D, D])
    C.w_up = di("w_up", [2, D, 2 * FF])
    C.w_down = di("w_down", [2, FF, D])
    C.ssm_s = di("ssm_s", [128, 3, 32])
    C.ssm_bc = di("ssm_bc", [128, 4, 32, 16])
    C.outT = nc.dram_tensor("outT", [NB, D, S], F32, kind="ExternalOutput").ap()
    C.scr_mats = nc.dram_tensor("scr_mats", [4, 128, 32 * 256], BF16, kind="Internal").ap()
    C.scr_tt = nc.dram_tensor("scr_tt", [128, 64 * 128], BF16, kind="Internal").ap()
    C.scr_u = nc.dram_tensor("scr_u", [8, 64 * 16, 128], BF16, kind="Internal").ap()
    C.scr_y = nc.dram_tensor("scr_y", [8, 64 * 16, 128], BF16, kind="Internal").ap()

    P = Prog(nc)
    C.P = P
    with contextlib.ExitStack() as st:
        sb = lambda name, shape, dtype: st.enter_context(nc.sbuf_tensor(name, shape, dtype))
        C.x = sb("x_sb", [128, KC, S], F32)
        C.vec = sb("vec_sb", [128, NV], F32)
        C.mod = sb("mod_sb", [128, NMOD, NB], F32)
        C.A = sb("A_sb", [128, 5, KC, NB], F32)
        C.cact = sb("cact", [128, KC, NB], BF16)
        C.cin = sb("cin", [128, KC, NB], F32)
        C.ones = sb("ones_bf", [128, 128], BF16)
        C.tri_i = sb("tri_i", [128, 128], BF16)
        C.tri_r = sb("tri_r", [128, 128], BF16)
        C.ident = sb("ident", [128, 128], BF16)
        C.maskbig = sb("maskbig", [128, 896], BF16)
        C.bmask = sb("bmask", [128, 128], F32)
        C.identf = sb("identf", [128, 128], F32)
        C.pid_i = sb("pid_i", [128, 2], mybir.dt.int32)
        C.pid_f = sb("pid_f", [128, 2], F32)
        C.halo = sb("halo", [128, 2 * FJ, 2], F32)
        C.sscar = sb("sscar", [128, 2, 32], F32)
        C.ssD = sb("ssD", [128, 2, 32], F32)
        wb = [sb("wbuf%d" % i, [128, 4096], BF16) for i in range(3)]
        C.W = WStream(P, wb, schedule)
        C.W.C = C
        C.ARENA_BYTES = 110 * 1024
        C.arena = sb("arena", [128, C.ARENA_BYTES // 2], BF16)
        C.iot_i = C.arena[:, 0:1792].bitcast(mybir.dt.int32)
        C.iot_f = C.arena[:, 2048:2048 + 1792].bitcast(F32)
        C.NT_OFF = C.ARENA_BYTES - 8192
        pq = [st.enter_context(nc.psum_tensor("pq%d" % i, [128, 1024], F32)) for i in range(4)]
        C.pq = pq
        C.bank = lambda i: pq[i // 2][:, 512 * (i % 2):512 * (i % 2) + 512]

        prologue(C)
        for b in range(nseq):
            run_sequence(C, b)
        P.finish()
        P.emit()
    C.requests = C.W.req
    return nc, C


def carve(C, off, shape, dtype):
    n = int(np.prod(shape))
    if dtype == F32:
        ap = C.arena[:, off // 2: off // 2 + 2 * n].bitcast(F32)
    else:
        ap = C.arena[:, off // 2: off // 2 + n]
    if len(shape) == 2:
        ap = ap.rearrange("p (a b) -> p a b", a=shape[0])
    elif len(shape) == 3:
        ap = ap.rearrange("p (a b c) -> p a b c", a=shape[0], b=shape[1])
    return ap


def prologue(C):
    P, nc = C.P, C.nc
    I32 = mybir.dt.int32
    P.dma("sp", lambda h: h.dma_start(out=C.vec[:], in_=C.vecs), writes=["vec"])
    P.dma("sp", lambda h: h.dma_start(out=C.cin[:], in_=C.cT.rearrange("(k p) b -> p k b", p=128)), writes=["cin"])
    P.op("pool", lambda h: h.iota(C.iot_i[:], pattern=[[1, 896]], base=-384, channel_multiplier=0), writes=["iot_i"])
    P.op("pool", lambda h: h.iota(C.pid_i[:, 0:1], pattern=[[0, 1]], base=0, channel_multiplier=1), writes=["pid_i"])
    P.op("dve", lambda h: h.tensor_copy(out=C.iot_f[:], in_=C.iot_i[:]), reads=["iot_i"], writes=["iot_f"])
    P.op("dve", lambda h: h.tensor_copy(out=C.pid_f[:, 0:1], in_=C.pid_i[:, 0:1]), reads=["pid_i"], writes=["pid_f"])
    P.op("dve", lambda h: h.tensor_single_scalar(out=C.pid_i[:, 1:2], in_=C.pid_i[:, 0:1], scalar=4, op=ALU.arith_shift_right),
         reads=["pid_i"], writes=["pid_i2"])
    P.op("dve", lambda h: h.tensor_copy(out=C.pid_f[:, 1:2], in_=C.pid_i[:, 1:2]), reads=["pid_i2"], writes=["pid_f2"])
    pidx = C.pid_f[:, 0:1]
    col128 = C.iot_f[:, 384:512]
    rd = ["iot_f", "pid_f"]
    P.op("dve", lambda h: h.tensor_single_scalar(out=C.maskbig[:], in_=C.iot_f[:], scalar=pidx, op=ALU.is_gt), reads=rd, writes=["maskbig"])
    P.op("dve", lambda h: h.tensor_single_scalar(out=C.tri_i[:], in_=col128, scalar=pidx, op=ALU.is_le), reads=rd, writes=["tri_i"])
    P.op("dve", lambda h: h.tensor_single_scalar(out=C.tri_r[:], in_=col128, scalar=pidx, op=ALU.is_gt), reads=rd, writes=["tri_r"])
    P.op("dve", lambda h: h.tensor_single_scalar(out=C.ident[:], in_=col128, scalar=pidx, op=ALU.is_equal), reads=rd, writes=["ident"])
    P.op("dve", lambda h: h.tensor_single_scalar(out=C.identf[:], in_=col128, scalar=pidx, op=ALU.is_equal), reads=rd, writes=["identf"])
    P.op("dve", lambda h: h.memset(C.ones[:], 1.0 / D), writes=["ones"])
    P.op("dve", lambda h: h.tensor_single_scalar(out=C.iot_i[:, 0:128], in_=C.iot_i[:, 384:512], scalar=4, op=ALU.arith_shift_right),
         reads=["iot_i", "iot_f"], writes=["iot_i"])
    P.op("dve", lambda h: h.tensor_copy(out=C.iot_f[:, 0:128], in_=C.iot_i[:, 0:128]), reads=["iot_i", "maskbig"], writes=["iot_f"])
    P.op("dve", lambda h: h.tensor_single_scalar(out=C.bmask[:], in_=C.iot_f[:, 0:128], scalar=C.pid_f[:, 1:2], op=ALU.is_ge),
         reads=["iot_f", "pid_f2"], writes=["bmask"])
    P.op("act", lambda h: h.activation(out=C.cact[:], in_=C.cin[:], func=AF.Silu), reads=["cin"], writes=["cact"])
    def mod_tasks(w2d_fn, ncols, col0, bias0, bank):
        tasks = []
        for s_ in range(ncols // 512):
            def task(s_=s_):
                ps = C.bank(bank)
                (wv,), wk = C.W.next(lambda C_, w2d_fn=w2d_fn, s_=s_: [wslab(w2d_fn(C_), 512 * s_, 512)])
                for q in range(4):
                    for k in range(KC):
                        P.op("pe", lambda h, wv=wv, q=q, k=k, ps=ps: h.matmul(
                            ps[:, NB * q:NB * q + NB], lhsT=wv[:, k, 128 * q:128 * q + 128], rhs=C.cact[:, k, :],
                            start=(k == 0), stop=(k == KC - 1)),
                            reads=[wk, "cact"], writes=[("ps", bank)])
                c_ = col0 + 4 * s_
                b_ = bias0 + 4 * s_
                P.op("dve", lambda h, ps=ps, c_=c_, b_=b_: h.tensor_tensor(
                    out=C.mod[:, c_:c_ + 4, :], in0=ps[:, 0:NB * 4].rearrange("p (c b) -> p c b", b=NB),
                    in1=C.vec[:, b_:b_ + 4].unsqueeze(2).to_broadcast([128, 4, NB]), op=ALU.add),
                    reads=[("ps", bank), "vec"], writes=["mod"])
            tasks.append(task)
        return tasks

    sites = [(V_NMIX, 8), (V_NFFN, 32), (V_NMIX + 8, 48 + 8), (V_NFFN + 8, 48 + 32), (V_NOUT, 96 + 8)]

    def site_task(si):
        gcol, sccol = sites[si]
        P.op("dve", lambda h, si=si, gcol=gcol, sccol=sccol: h.scalar_tensor_tensor(
            out=C.A[:, si, :, :], in0=C.mod[:, sccol:sccol + 8, :], scalar=1.0,
            in1=C.vec[:, gcol:gcol + 8].unsqueeze(2).to_broadcast([128, 8, NB]), op0=ALU.add, op1=ALU.mult),
            reads=["mod", "vec"], writes=[("A", si)])

    for t_ in mod_tasks(lambda C_: C_.w_mod[0], 6 * D, 0, V_BMOD, 7):
        t_()
    site_task(0)
    site_task(1)
    C.deferred = (mod_tasks(lambda C_: C_.w_mod[1], 6 * D, 48, V_BMOD + 48, 7)
                  + mod_tasks(lambda C_: C_.w_fin, 2 * D, 96, V_BFIN, 7)
                  + [lambda: site_task(2), lambda: site_task(3), lambda: site_task(4)])
    if "attn" not in C.stages:
        run_deferred(C, 10 ** 6)
    if "ssm" in C.stages:
        ssm_prologue(C)
    P.barrier(keep=lambda k: isinstance(k, tuple) and k[0] == "wb")


def run_deferred(C, n):
    while n > 0 and C.deferred:
        C.deferred.pop(0)()
        n -= 1


def norm_mod(C, b, site, shcol, out_fn, out_dtype_bf16=True):
    P = C.P
    sq = [carve(C, C.NT_OFF + 1024 * i, [512], BF16) for i in range(2)]
    rs = carve(C, C.NT_OFF + 2048, [512], F32)
    tmp = [carve(C, C.NT_OFF + 4096 + 2048 * i, [512], F32) for i in range(2)]
    for tt in range(4):
        ts = slice(512 * tt, 512 * tt + 512)
        ps = C.bank(tt % 2)
        for k in range(KC):
            P.op("act", lambda h, k=k, ts=ts: h.activation(out=sq[k % 2], in_=C.x[:, k, ts], func=AF.Square),
                 reads=[("x", k, tt)], writes=[("nsq", k % 2)])
            P.op("pe", lambda h, k=k, ps=ps: h.matmul(ps, lhsT=C.ones[:], rhs=sq[k % 2], start=(k == 0), stop=(k == KC - 1)),
                 reads=[("nsq", k % 2), "ones"], writes=[("ps", tt % 2)])
        P.op("act", lambda h, ps=ps: h.activation(out=rs, in_=ps, func=AF.Sqrt, bias=EPS, scale=1.0),
             reads=[("ps", tt % 2)], writes=["nrs"])
        P.op("dve", lambda h: h.reciprocal(out=rs, in_=rs), reads=["nrs"], writes=["nrs"])
        for k in range(KC):
            o, okey = out_fn(k, tt)
            P.op("dve", lambda h, k=k, ts=ts: h.tensor_tensor(out=tmp[k % 2], in0=C.x[:, k, ts], in1=rs, op=ALU.mult),
                 reads=[("x", k, tt), "nrs"], writes=[("ntmp", k % 2)])
            P.op("act", lambda h, k=k, o=o: h.activation(out=o, in_=tmp[k % 2], func=AF.Identity,
                                                         bias=C.mod[:, shcol + k, b:b + 1], scale=C.A[:, site, k, b:b + 1]),
                 reads=[("ntmp", k % 2), "mod", ("A", site)], writes=[okey])


def load_x(C, b):
    P = C.P
    for k in range(KC):
        P.dma("sp", lambda h, k=k: h.dma_start(out=C.x[:, k, :], in_=C.xT[b, 128 * k:128 * k + 128, :]),
              writes=[("x", k, tt) for tt in range(4)])


def final_out(C, b):
    obuf = [carve(C, 16384 + 2048 * i, [512], F32) for i in range(4)]
    cnt = [0]

    def out_fn(k, tt):
        i = cnt[0] % 4
        cnt[0] += 1
        return obuf[i], ("obuf", i)

    norm_mod_out(C, b, 4, 96, out_fn, obuf)


def norm_mod_out(C, b, site, shcol, out_fn, obuf):
    P = C.P
    state = {"n": 0}

    def wrapped(k, tt):
        return out_fn(k, tt)

    sq = [carve(C, C.NT_OFF + 1024 * i, [512], BF16) for i in range(2)]
    rs = carve(C, C.NT_OFF + 2048, [512], F32)
    tmp = [carve(C, C.NT_OFF + 4096 + 2048 * i, [512], F32) for i in range(2)]
    for tt in range(4):
        ts = slice(512 * tt, 512 * tt + 512)
        ps = C.bank(tt % 2)
        for k in range(KC):
            P.op("act", lambda h, k=k, ts=ts: h.activation(out=sq[k % 2], in_=C.x[:, k, ts], func=AF.Square),
                 reads=[("x", k, tt)], writes=[("nsq", k % 2)])
            P.op("pe", lambda h, k=k, ps=ps: h.matmul(ps, lhsT=C.ones[:], rhs=sq[k % 2], start=(k == 0), stop=(k == KC - 1)),
                 reads=[("nsq", k % 2), "ones"], writes=[("ps", tt % 2)])
        P.op("act", lambda h, ps=ps: h.activation(out=rs, in_=ps, func=AF.Sqrt, bias=EPS, scale=1.0),
             reads=[("ps", tt % 2)], writes=["nrs"])
        P.op("dve", lambda h: h.reciprocal(out=rs, in_=rs), reads=["nrs"], writes=["nrs"])
        for k in range(KC):
            o, okey = wrapped(k, tt)
            P.op("dve", lambda h, k=k, ts=ts: h.tensor_tensor(out=tmp[k % 2], in0=C.x[:, k, ts], in1=rs, op=ALU.mult),
                 reads=[("x", k, tt), "nrs"], writes=[("ntmp", k % 2)])
            P.op("act", lambda h, k=k, o=o: h.activation(out=o, in_=tmp[k % 2], func=AF.Identity,
                                                         bias=C.mod[:, shcol + k, b:b + 1], scale=C.A[:, site, k, b:b + 1]),
                 reads=[("ntmp", k % 2), "mod", ("A", site)], writes=[okey])
            P.dma("sp", lambda h, k=k, ts=ts, o=o: h.dma_start(out=C.outT[b, 128 * k:128 * k + 128, ts], in_=o),
                  reads=[okey], writes=[("out", b, k, tt)], out=True)


def dump_x(C, b):
    P = C.P
    for k in range(KC):
        P.dma("sp", lambda h, k=k: h.dma_start(out=C.outT[b, 128 * k:128 * k + 128, :], in_=C.x[:, k, :]),
              reads=[("x", k, tt) for tt in range(4)], writes=[("out", b, k)], out=True)


def run_sequence(C, b):
    P = C.P
    keepw = lambda k: isinstance(k, tuple) and k[0] == "wb"
    load_x(C, b)
    for stg in C.stages:
        if stg == "attn":
            attn_layer(C, b)
            run_deferred(C, 10 ** 6)
            if isinstance(C.debug, str) and (C.debug.startswith("attn") or C.debug == "wodump"):
                P.barrier(keep=keepw)
                return
        elif stg == "ffn0":
            ffn_layer(C, b, 0)
        elif stg == "ssm":
            ssm_layer(C, b)
        elif stg == "ffn1":
            ffn_layer(C, b, 1)
        elif stg == "dump":
            dump_x(C, b)
            P.barrier(keep=keepw)
            return
        P.barrier(keep=keepw)
    final_out(C, b)
    P.barrier(keep=keepw)


def _colT(v):
    return np.ascontiguousarray(v.reshape(-1, 128).T)


def prep_inputs(inp, core):
    f = lambda a: np.ascontiguousarray(np.asarray(a, dtype=np.float32))
    bs = slice(NB * core, NB * core + NB)
    m = {}
    m["xT"] = f(np.asarray(inp["x"])[bs].transpose(0, 2, 1))
    m["cT"] = f(np.asarray(inp["c"])[bs].T)
    vec = np.zeros((128, NV), np.float32)
    for i in range(2):
        vec[:, V_NMIX + 8 * i:V_NMIX + 8 * i + 8] = _colT(np.asarray(inp["norm_mix"])[i])
        vec[:, V_NFFN + 8 * i:V_NFFN + 8 * i + 8] = _colT(np.asarray(inp["norm_ffn"])[i])
        vec[:, V_BMOD + 48 * i:V_BMOD + 48 * i + 48] = _colT(np.asarray(inp["b_mod"])[i])
        cw = np.asarray(inp["conv_w"])[i].reshape(3, 2 * FJ, 128).transpose(2, 1, 0).reshape(128, 2 * FJ * 3)
        vec[:, V_CW + 132 * i:V_CW + 132 * i + 132] = cw
        vec[:, V_CB + 44 * i:V_CB + 44 * i + 44] = _colT(np.asarray(inp["conv_b"])[i])
    vec[:, V_NOUT:V_NOUT + 8] = _colT(np.asarray(inp["norm_out"]))
    vec[:, V_BFIN:V_BFIN + 16] = _colT(np.asarray(inp["b_fin"]))
    vec[:, V_BGLU:V_BGLU + 8] = _colT(np.asarray(inp["b_glu"])[0])
    dsk = np.asarray(inp["d_skip"])[0].reshape(NG, 16).T
    vec[:, V_DSK:V_DSK + 64] = np.tile(dsk, (8, 1))
    m["vecs"] = vec
    for k in ("w_mod", "w_fin", "w_up", "w_down"):
        m[k] = f(inp[k])
    for k in ("w_qkv", "w_o_attn", "w_in_ssm", "w_glu", "w_o_ssm"):
        m[k] = f(np.asarray(inp[k])[0])
    r2 = lambda a: np.asarray(a)[0].reshape(32, 2, 64).transpose(1, 2, 0).reshape(128, 32)
    ldt = np.broadcast_to(np.asarray(inp["log_dt"])[0].reshape(32, 2, 1), (32, 2, 64)).transpose(1, 2, 0).reshape(128, 32)
    m["ssm_s"] = f(np.stack([r2(inp["a_re"]), r2(inp["a_im"]), ldt], axis=1))
    bp = lambda a: np.asarray(a)[0].reshape(32, 2, 64, 16).transpose(1, 2, 0, 3).reshape(128, 32, 16)
    cp = lambda a: np.asarray(a)[0].reshape(32, 2, 16, 64).transpose(1, 3, 0, 2).reshape(128, 32, 16)
    m["ssm_bc"] = f(np.stack([bp(inp["b_re"]), bp(inp["b_im"]), cp(inp["c_re"]), cp(inp["c_im"])], axis=1))
    return m


_CACHE = {}


def get_program(stages=("attn", "ffn0", "ssm", "ffn1"), debug=None):
    key = (tuple(stages), debug)
    if key not in _CACHE:
        _, C0 = build(stages, schedule=None, debug=debug)
        nc, C = build(stages, schedule=C0.requests, debug=debug)
        _CACHE[key] = (nc, C)
    return _CACHE[key]


def kernel(**inputs):
    nc, C = get_program()
    in_maps = [prep_inputs(inputs, c) for c in range(8)]
    res = run_bass_kernel_spmd(nc, in_maps, core_ids=list(range(8)))
    outs = [np.asarray(r["outT"]).transpose(0, 2, 1) for r in res.results]
    return np.ascontiguousarray(np.concatenate(outs, axis=0).astype(np.float32))


def attn_layer(C, b):
    P = C.P
    h = carve(C, 0, [KC, S], BF16)
    qT = carve(C, 32768, [S], BF16)
    kT = carve(C, 36864, [S], BF16)
    V = carve(C, 40960, [16, 128], BF16)
    oT = carve(C, 45056, [S], BF16)
    NWK = 4
    wk = lambda nm, i: carve(C, 49152 + 4096 * {"e": 0, "ln": 1, "en": 2, "wt": 3}[nm] + 1024 * i, [512], BF16)

    def hout(k, tt):
        return h[:, k, 512 * tt:512 * tt + 512], ("h", k, tt)

    norm_mod(C, b, 0, 0, hout)
    hkeys_tt = lambda tt: [("h", k, tt) for k in range(KC)]
    scale = DH ** -0.5
    g1col = 16

    for g in range(getattr(C, 'ngroups', 8)):
        (wq, wkk, wv), wkey = C.W.next(lambda C_, g=g: [wslab(C_.w_qkv, 128 * g, 128), wslab(C_.w_qkv, D + 128 * g, 128),
                                                      wslab(C_.w_qkv, 2 * D + 128 * g, 128)])
        for which, wmat, dst, dkey in ((0, wq, qT, "qT"), (1, wkk, kT, "kT")):
            for tt in range(4):
                bk = 6 + (tt % 2)
                ps = C.bank(bk)
                for k in range(KC):
                    P.op("pe", lambda h_, wmat=wmat, k=k, tt=tt, ps=ps: h_.matmul(
                        ps, lhsT=wmat[:, k, :], rhs=h[:, k, 512 * tt:512 * tt + 512], start=(k == 0), stop=(k == KC - 1)),
                        reads=[wkey, ("h", k, tt)], writes=[("ps", bk)])
                eng = "act" if tt % 2 == 0 else "dve"
                if eng == "act":
                    P.op("act", lambda h_, dst=dst, tt=tt, ps=ps: h_.activation(out=dst[:, 512 * tt:512 * tt + 512], in_=ps, func=AF.Copy),
                         reads=[("ps", bk)], writes=[(dkey, tt)])
                else:
                    P.op("dve", lambda h_, dst=dst, tt=tt, ps=ps: h_.tensor_copy(out=dst[:, 512 * tt:512 * tt + 512], in_=ps),
                         reads=[("ps", bk)], writes=[(dkey, tt)])
        for q4 in range(4):
            bk = 6 + (q4 % 2)
            ps = C.bank(bk)
            for j in range(4):
                kb = 4 * q4 + j
                for k in range(KC):
                    P.op("pe", lambda h_, k=k, kb=kb, j=j, ps=ps, wv=wv: h_.matmul(
                        ps[:, 128 * j:128 * j + 128], lhsT=h[:, k, 128 * kb:128 * kb + 128], rhs=wv[:, k, :],
                        start=(k == 0), stop=(k == KC - 1)),
                        reads=[wkey, ("h", k, kb // 4)], writes=[("ps", bk)])
            P.op("act" if q4 % 2 == 0 else "dve",
                 (lambda h_, q4=q4, ps=ps: h_.activation(out=V[:, 4 * q4:4 * q4 + 4, :], in_=ps.rearrange("p (a b) -> p a b", a=4), func=AF.Copy))
                 if q4 % 2 == 0 else
                 (lambda h_, q4=q4, ps=ps: h_.tensor_copy(out=V[:, 4 * q4:4 * q4 + 4, :], in_=ps.rearrange("p (a b) -> p a b", a=4))),
                 reads=[("ps", bk)], writes=[("V", q4)])

        tiles = []
        for qc in range(4):
            nkb = 4 * qc + 4
            for n, kb in enumerate(range(nkb - 1, -1, -1)):
                for hd in range(2):
                    tiles.append(dict(qc=qc, hd=hd, kb=kb, first=(n == 0), last=(kb == 0), diag=(kb >= 4 * qc), i=kb - 4 * qc,
                                      c0=(128 * (kb - 4 * qc) if kb >= 4 * qc else 0)))
        NT = len(tiles)

        def s_qk(t):
            T = tiles[t]
            zb = t % 2
            c0 = T["c0"]
            hp = slice(64 * T["hd"], 64 * T["hd"] + 64)
            P.op("pe", lambda h_, T=T, zb=zb, hp=hp, c0=c0: h_.matmul(
                C.bank(zb)[:, c0:512], lhsT=kT[hp, 128 * T["kb"]:128 * T["kb"] + 128],
                rhs=qT[hp, 512 * T["qc"] + c0:512 * T["qc"] + 512], start=True, stop=True),
                reads=[("kT", T["kb"] // 4), ("qT", T["qc"])], writes=[("ps", zb)])

        def s_expa(t):
            T = tiles[t]
            zb = t % 2
            w = t % NWK
            c0 = T["c0"]
            P.op("act", lambda h_, zb=zb, w=w, c0=c0: h_.activation(out=wk("e", w)[:, c0:512], in_=C.bank(zb)[:, c0:512], func=AF.Exp, scale=scale),
                 reads=[("ps", zb)], writes=[("e", w)])
            if T["diag"]:
                P.op("dve", lambda h_, w=w, c0=c0: h_.tensor_tensor(
                    out=wk("e", w)[:, c0:c0 + 128], in0=wk("e", w)[:, c0:c0 + 128], in1=C.maskbig[:, 384:512], op=ALU.mult),
                    reads=[("e", w), "maskbig"], writes=[("e", w)])

        def s_expb(t):
            w = t % NWK
            c0 = tiles[t]["c0"]
            P.op("act", lambda h_, w=w, c0=c0: h_.activation(out=wk("ln", w)[:, c0:512], in_=wk("e", w)[:, c0:512], func=AF.Ln, bias=1.0, scale=1.0),
                 reads=[("e", w)], writes=[("ln", w)])

        def split_cols(T):
            return [(T["c0"], 512, T["first"])]

        def s_mm1(t):
            T = tiles[t]
            sb_ = 2 + (T["qc"] * 2 + T["hd"]) % 2
            w = t % NWK
            for (a_, b_, first) in split_cols(T):
                P.op("pe", lambda h_, sb_=sb_, w=w, a_=a_, b_=b_, first=first: h_.matmul(
                    C.bank(sb_)[:, a_:b_], lhsT=C.tri_i[:], rhs=wk("ln", w)[:, a_:b_], start=first, stop=True),
                    reads=[("ln", w), "tri_i"], writes=[("ps", sb_)])

        def s_en(t):
            T = tiles[t]
            sb_ = 2 + (T["qc"] * 2 + T["hd"]) % 2
            w = t % NWK
            c0 = T["c0"]
            P.op("act", lambda h_, sb_=sb_, w=w, c0=c0: h_.activation(out=wk("en", w)[:, c0:512], in_=C.bank(sb_)[:, c0:512], func=AF.Exp, scale=-1.0),
                 reads=[("ps", sb_)], writes=[("en", w)])

        def s_mm2(t):
            T = tiles[t]
            if T["last"]:
                return
            sb_ = 2 + (T["qc"] * 2 + T["hd"]) % 2
            w = t % NWK
            c0 = T["c0"]
            P.op("pe", lambda h_, sb_=sb_, w=w, c0=c0: h_.matmul(
                C.bank(sb_)[:, c0:512], lhsT=C.tri_r[:], rhs=wk("ln", w)[:, c0:512], start=False, stop=True),
                reads=[("ln", w), "tri_r"], writes=[("ps", sb_)])

        def s_w(t):
            w = t % NWK
            c0 = tiles[t]["c0"]
            P.op("dve", lambda h_, w=w, c0=c0: h_.tensor_tensor(out=wk("wt", w)[:, c0:512], in0=wk("e", w)[:, c0:512], in1=wk("en", w)[:, c0:512], op=ALU.mult),
                 reads=[("e", w), ("en", w)], writes=[("wt", w)])

        def s_pv(t):
            T = tiles[t]
            ob = 4 + T["qc"] % 2
            w = t % NWK
            hp = slice(64 * T["hd"], 64 * T["hd"] + 64)
            for (a_, b_, first) in split_cols(T):
                P.op("pe", lambda h_, T=T, ob=ob, w=w, hp=hp, a_=a_, b_=b_, first=first: h_.matmul(
                    C.bank(ob)[hp, a_:b_], lhsT=V[:, T["kb"], hp], rhs=wk("wt", w)[:, a_:b_], start=first, stop=True),
                    reads=[("V", T["kb"] // 4), ("wt", w)], writes=[("ps", ob)])
            if T["last"] and T["hd"] == 1:
                qc = T["qc"]
                P.op("dve", lambda h_, ob=ob, qc=qc: h_.tensor_copy(out=oT[:, 512 * qc:512 * qc + 512], in_=C.bank(ob)),
                     reads=[("ps", ob)], writes=[("oT", qc)])

        s_qk(0)
        if NT > 1:
            s_qk(1)
        s_expa(0)
        s_expb(0)
        for t in range(NT):
            if t + 1 < NT:
                s_expa(t + 1)
                s_expb(t + 1)
            s_mm1(t)
            if t + 2 < NT:
                s_qk(t + 2)
            if t >= 1:
                s_pv(t - 1)
            s_en(t)
            s_mm2(t)
            s_w(t)
        s_pv(NT - 1)

        if isinstance(getattr(C, "debug", None), str) and C.debug.startswith("attn") and g == int(C.debug[4:]):
            dbg = carve(C, 65536, [4, S], F32)
            for i, (src, keys) in enumerate(((qT, [("qT", t_) for t_ in range(4)]), (kT, [("kT", t_) for t_ in range(4)]),
                                             (oT, [("oT", t_) for t_ in range(4)]))):
                P.op("dve", lambda h_, i=i, src=src: h_.tensor_copy(out=dbg[:, i, :], in_=src), reads=keys, writes=[("dbg", i)])
                P.dma("sp", lambda h_, i=i: h_.dma_start(out=C.outT[b, 128 * i:128 * i + 128, :], in_=dbg[:, i, :]),
                      reads=[("dbg", i)], writes=[("dbgo", i)], out=True)
            P.op("dve", lambda h_: h_.tensor_copy(out=dbg[:, 3, :], in_=V.rearrange("p a b -> p (a b)")),
                 reads=[("V", q_) for q_ in range(4)], writes=[("dbg", 3)])
            P.dma("sp", lambda h_: h_.dma_start(out=C.outT[b, 384:512, :], in_=dbg[:, 3, :]), reads=[("dbg", 3)], writes=[("dbgo", 3)], out=True)
            for k_ in range(4):
                P.op("dve", lambda h_, k_=k_: h_.tensor_copy(out=dbg[:, k_, :], in_=h[:, k_, :]),
                     reads=[("h", k_, t_) for t_ in range(4)] + [("dbgo", k_)], writes=[("dbg", k_)])
                P.dma("sp", lambda h_, k_=k_: h_.dma_start(out=C.outT[b, 512 + 128 * k_:640 + 128 * k_, :], in_=dbg[:, k_, :]),
                      reads=[("dbg", k_)], writes=[("dbgo2", k_)], out=True)
            return
        (wo,), wokey = C.W.next(lambda C_, g=g: [(C_.w_o_attn[128 * g:128 * g + 128, :], (D,))])
        if getattr(C, "debug", None) == "wodump" and g == 1:
            dbg = carve(C, 65536, [1024], F32)
            P.op("dve", lambda h_: h_.tensor_copy(out=dbg, in_=wo), reads=[wokey], writes=["dbgw"])
            P.dma("sp", lambda h_: h_.dma_start(out=C.outT[b, 0:128, 0:1024], in_=dbg), reads=["dbgw"], writes=["dbgwo"], out=True)
            return
        for oc in range(KC):
            for tt in range(4):
                bk = 6 + (tt % 2)
                ps = C.bank(bk)
                P.op("pe", lambda h_, oc=oc, tt=tt, ps=ps, wo=wo: h_.matmul(
                    ps, lhsT=wo[:, 128 * oc:128 * oc + 128], rhs=oT[:, 512 * tt:512 * tt + 512], start=True, stop=True),
                    reads=[wokey, ("oT", tt)], writes=[("ps", bk)])
                P.op("dve", lambda h_, oc=oc, tt=tt, ps=ps: h_.scalar_tensor_tensor(
                    out=C.x[:, oc, 512 * tt:512 * tt + 512], in0=ps, scalar=C.mod[:, g1col + oc, b:b + 1],
                    in1=C.x[:, oc, 512 * tt:512 * tt + 512], op0=ALU.mult, op1=ALU.add),
                    reads=[("ps", bk), ("x", oc, tt), "mod"], writes=[("x", oc, tt)])
        run_deferred(C, 3)


def ffn_layer(C, b, li):
    P = C.P
    h = carve(C, 0, [KC, S], BF16)
    gT = carve(C, 32768, [FJ, 1024], BF16)
    acc = [carve(C, 77824 + 4096 * i, [1024], F32) for i in range(4)]
    sg = [carve(C, 94208 + 2048 * i, [1024], BF16) for i in range(2)]
    site = 1 if li == 0 else 3
    shcol = 48 * li + 24
    g2col = 48 * li + 40

    def hout(k, tt):
        return h[:, k, 512 * tt:512 * tt + 512], ("h", k, tt)

    norm_mod(C, b, site, shcol, hout)
    cwv = lambda ch, tap: C.vec[:, V_CW + 132 * li + 3 * ch + tap:V_CW + 132 * li + 3 * ch + tap + 1]
    cbv = lambda ch: C.vec[:, V_CB + 44 * li + ch:V_CB + 44 * li + ch + 1]
    for hf in range(2):
        tok0 = 1024 * hf
        for J in range(6):
            nj = min(4, FJ - 4 * J)
            ncol = 128 * nj
            (wg,), wgk = C.W.next(lambda C_, J=J, ncol=ncol: [wslab(C_.w_up[li], 512 * J, ncol)], ahead=1)
            (wvv,), wvk = C.W.next(lambda C_, J=J, ncol=ncol: [wslab(C_.w_up[li], FF + 512 * J, ncol)], ahead=1)
            for jj in range(nj):
                j = 4 * J + jj
                a = (j % 2) * 2
                for which, wmat, wkey_ in ((0, wg, wgk), (1, wvv, wvk)):
                    pqi = a + which
                    pst = C.pq[pqi][:, :]
                    for tl in range(2):
                        bank = 2 * pqi + tl
                        for k in range(KC):
                            P.op("pe", lambda h_, wmat=wmat, k=k, jj=jj, tl=tl, pst=pst, tok0=tok0: h_.matmul(
                                pst[:, 512 * tl:512 * tl + 512], lhsT=wmat[:, k, 128 * jj:128 * jj + 128],
                                rhs=h[:, k, tok0 + 512 * tl:tok0 + 512 * tl + 512], start=(k == 0), stop=(k == KC - 1)),
                                reads=[wkey_, ("h", k, 2 * hf + tl)], writes=[("ps", bank)])
                    ch = j if which == 0 else FJ + j
                    ai = 2 * (j % 2) + which
                    accw = acc[ai]
                    akey = ("acc", ai)
                    pkeys = [("ps", 2 * pqi), ("ps", 2 * pqi + 1)]
                    P.op("act", lambda h_, accw=accw, pst=pst, ch=ch: h_.activation(
                        out=accw, in_=pst, func=AF.Identity, bias=cbv(ch), scale=cwv(ch, 2)),
                        reads=pkeys + ["vec"], writes=[akey])
                    P.op("dve", lambda h_, accw=accw, pst=pst, ch=ch: h_.scalar_tensor_tensor(
                        out=accw[:, 1:1024], in0=pst[:, 0:1023], scalar=cwv(ch, 1), in1=accw[:, 1:1024], op0=ALU.mult, op1=ALU.add),
                        reads=pkeys + ["vec", akey], writes=[akey])
                    P.op("dve", lambda h_, accw=accw, pst=pst, ch=ch: h_.scalar_tensor_tensor(
                        out=accw[:, 2:1024], in0=pst[:, 0:1022], scalar=cwv(ch, 0), in1=accw[:, 2:1024], op0=ALU.mult, op1=ALU.add),
                        reads=pkeys + ["vec", akey], writes=[akey])
                    if hf == 0:
                        P.op("act", lambda h_, pst=pst, ch=ch: h_.activation(out=C.halo[:, ch, :], in_=pst[:, 1022:1024], func=AF.Copy),
                             reads=pkeys, writes=[("halo", ch)])
                    else:
                        P.op("dve", lambda h_, accw=accw, ch=ch: h_.scalar_tensor_tensor(
                            out=accw[:, 0:1], in0=C.halo[:, ch, 1:2], scalar=cwv(ch, 1), in1=accw[:, 0:1], op0=ALU.mult, op1=ALU.add),
                            reads=[("halo", ch), "vec", akey], writes=[akey])
                        P.op("dve", lambda h_, accw=accw, ch=ch: h_.scalar_tensor_tensor(
                            out=accw[:, 0:2], in0=C.halo[:, ch, 0:2], scalar=cwv(ch, 0), in1=accw[:, 0:2], op0=ALU.mult, op1=ALU.add),
                            reads=[("halo", ch), "vec", akey], writes=[akey])
                sgw = sg[j % 2]
                ag, av = acc[2 * (j % 2)], acc[2 * (j % 2) + 1]
                P.op("act", lambda h_, sgw=sgw, ag=ag: h_.activation(out=sgw, in_=ag, func=AF.Silu),
                     reads=[("acc", 2 * (j % 2))], writes=[("sg", j % 2)])
                P.op("pool", lambda h_, sgw=sgw, av=av, j=j: h_.tensor_tensor(out=gT[:, j, :], in0=sgw, in1=av, op=ALU.mult),
                     reads=[("sg", j % 2), ("acc", 2 * (j % 2) + 1)], writes=[("gT", j)])
        for oc in range(KC):
            (wd,), wdk = C.W.next(lambda C_, oc=oc: [wslab(C_.w_down[li], 128 * oc, 128, kk=FJ)], ahead=2)
            for tl in range(2):
                bank = (2 * oc + tl) % 8
                ps = C.bank(bank)
                for j in range(FJ):
                    P.op("pe", lambda h_, wd=wd, j=j, tl=tl, ps=ps: h_.matmul(
                        ps, lhsT=wd[:, j, :], rhs=gT[:, j, 512 * tl:512 * tl + 512], start=(j == 0), stop=(j == FJ - 1)),
                        reads=[wdk, ("gT", j)], writes=[("ps", bank)])
                tt = 2 * hf + tl
                P.op("dve", lambda h_, oc=oc, tt=tt, ps=ps: h_.scalar_tensor_tensor(
                    out=C.x[:, oc, 512 * tt:512 * tt + 512], in0=ps, scalar=C.mod[:, g2col + oc, b:b + 1],
                    in1=C.x[:, oc, 512 * tt:512 * tt + 512], op0=ALU.mult, op1=ALU.add),
                    reads=[("ps", bank), ("x", oc, tt), "mod"], writes=[("x", oc, tt)])


def ssm_prologue(C):
    P = C.P
    I32 = mybir.dt.int32
    off = [8192]

    def al(shape, dtype=F32):
        n = int(np.prod(shape)) * (4 if dtype in (F32, I32) else 2)
        o = off[0]
        off[0] += (n + 63) // 64 * 64
        if dtype == I32:
            return C.arena[:, o // 2:o // 2 + 2 * int(np.prod(shape))].bitcast(I32)
        return carve(C, o, shape, dtype)

    cnt = [0]

    def tt_(eng, out, in0, in1, op, rd, wr):
        P.op(eng, lambda h, out=out, in0=in0, in1=in1, op=op: h.tensor_tensor(out=out, in0=in0, in1=in1, op=op), reads=rd, writes=wr)

    S_ = al([3, 32])
    BC = al([4, 32, 16])
    P.dma("sp", lambda h: h.dma_start(out=S_, in_=C.ssm_s), writes=["S_"])
    P.dma("sp", lambda h: h.dma_start(out=BC, in_=C.ssm_bc), writes=["BC"])
    a_re, a_im, ldt = S_[:, 0, :], S_[:, 1, :], S_[:, 2, :]
    dt_ = al([32]); ar = al([32]); th = al([32]); mag = al([32])
    P.op("act", lambda h: h.activation(out=dt_, in_=ldt, func=AF.Exp), reads=["S_"], writes=["dt"])
    tt_("dve", ar, a_re, dt_, ALU.mult, ["S_", "dt"], ["ar"])
    tt_("dve", th, a_im, dt_, ALU.mult, ["S_", "dt"], ["th"])
    P.op("act", lambda h: h.activation(out=mag, in_=ar, func=AF.Exp), reads=["ar"], writes=["mag"])
    trig = {}
    for nm, shift in (("sin", 0.0), ("cos", 0.25)):
        y = al([32]); ni = al([32], I32); nf = al([32]); f = al([32]); v = al([32])
        P.op("dve", lambda h, y=y, shift=shift: h.tensor_scalar(out=y, in0=th, scalar1=1.0 / TWO_PI, scalar2=shift, op0=ALU.mult, op1=ALU.add),
             reads=["th"], writes=[("y", nm)])
        P.op("dve", lambda h, y=y, ni=ni: h.tensor_copy(out=ni, in_=y), reads=[("y", nm)], writes=[("ni", nm)])
        P.op("dve", lambda h, nf=nf, ni=ni: h.tensor_copy(out=nf, in_=ni), reads=[("ni", nm)], writes=[("nf", nm)])
        tt_("dve", f, y, nf, ALU.subtract, [("y", nm), ("nf", nm)], [("f", nm)])
        P.op("act", lambda h, v=v, f=f: h.activation(out=v, in_=f, func=AF.Sin, scale=TWO_PI * (1.0 - 1e-6)), reads=[("f", nm)], writes=[("trig", nm)])
        trig[nm] = v
    if getattr(C, 'pro_lim', 99) < 1:
        return
    Lr = al([32]); Li = al([32])
    tt_("dve", Lr, mag, trig["cos"], ALU.mult, ["mag", ("trig", "cos")], ["Lr"])
    tt_("dve", Li, mag, trig["sin"], ALU.mult, ["mag", ("trig", "sin")], ["Li"])
    nr = al([32]); den = al([32]); t1 = al([32]); t2 = al([32]); cr = al([32]); ci = al([32])
    P.op("dve", lambda h: h.tensor_scalar(out=nr, in0=Lr, scalar1=-1.0, scalar2=None, op0=ALU.add), reads=["Lr"], writes=["nr"])
    tt_("dve", t1, a_re, a_re, ALU.mult, ["S_"], ["t1"])
    tt_("dve", t2, a_im, a_im, ALU.mult, ["S_"], ["t2"])
    tt_("dve", den, t1, t2, ALU.add, ["t1", "t2"], ["den"])
    P.op("dve", lambda h: h.reciprocal(out=den, in_=den), reads=["den"], writes=["den"])
    tt_("dve", t1, nr, a_re, ALU.mult, ["nr", "S_", "den"], ["t1"])
    tt_("dve", t2, Li, a_im, ALU.mult, ["Li", "S_", "den"], ["t2"])
    tt_("dve", cr, t1, t2, ALU.add, ["t1", "t2"], ["cr0"])
    tt_("dve", cr, cr, den, ALU.mult, ["cr0", "den"], ["cr"])
    tt_("dve", t1, Li, a_re, ALU.mult, ["Li", "S_", "cr0"], ["t1"])
    tt_("dve", t2, nr, a_im, ALU.mult, ["nr", "S_", "cr0"], ["t2"])
    tt_("dve", ci, t1, t2, ALU.subtract, ["t1", "t2"], ["ci0"])
    tt_("dve", ci, ci, den, ALU.mult, ["ci0", "den"], ["ci"])
    if getattr(C, 'pro_lim', 99) < 2:
        return
    PW = al([2, 9, 32])
    P.op("dve", lambda h: h.memset(PW[:, 0, 0, :], 1.0), reads=["ci"], writes=[("pw", 0)])
    P.op("dve", lambda h: h.memset(PW[:, 1, 0, :], 0.0), reads=[("pw", 0)], writes=[("pw", 0)])
    for j in range(1, 9):
        pr, pi_ = PW[:, 0, j - 1, :], PW[:, 1, j - 1, :]
        tt_("dve", t1, pr, Lr, ALU.mult, [("pw", j - 1), "Lr", "ci", ("pw", j - 2)], ["t1"])
        tt_("dve", t2, pi_, Li, ALU.mult, [("pw", j - 1), "Li", "ci", ("pw", j - 2)], ["t2"])
        tt_("dve", PW[:, 0, j, :], t1, t2, ALU.subtract, ["t1", "t2"], [("pwr", j)])
        tt_("dve", t1, pr, Li, ALU.mult, [("pw", j - 1), "Li", ("pwr", j)], ["t1"])
        tt_("dve", t2, pi_, Lr, ALU.mult, [("pw", j - 1), "Lr", ("pwr", j)], ["t2"])
        tt_("dve", PW[:, 1, j, :], t1, t2, ALU.add, ["t1", "t2", ("pwr", j)], [("pw", j)])
    pwk = [("pw", j) for j in range(9)]
    P.op("dve", lambda h: h.tensor_copy(out=C.ssD[:, 0, :], in_=PW[:, 0, 8, :]), reads=pwk, writes=["ssD0"])
    P.op("dve", lambda h: h.tensor_copy(out=C.ssD[:, 1, :], in_=PW[:, 1, 8, :]), reads=pwk, writes=["ssD"])
    m2 = al([8, 32]); m3 = al([8, 32]); ivr = al([8, 32]); ivi = al([8, 32]); br = al([8, 32]); bi = al([8, 32])
    pr8, pi8 = PW[:, 0, 0:8, :], PW[:, 1, 0:8, :]
    tt_("dve", m2, pr8, pr8, ALU.mult, pwk, ["m2"])
    tt_("dve", m3, pi8, pi8, ALU.mult, pwk, ["m3"])
    tt_("dve", m2, m2, m3, ALU.add, ["m2", "m3"], ["m2s"])
    P.op("dve", lambda h: h.reciprocal(out=m2, in_=m2), reads=["m2s"], writes=["rm"])
    tt_("dve", ivr, pr8, m2, ALU.mult, pwk + ["rm"], ["ivr"])
    tt_("dve", ivi, pi8, m2, ALU.mult, pwk + ["rm"], ["ivi0"])
    P.op("dve", lambda h: h.tensor_scalar(out=ivi, in0=ivi, scalar1=-1.0, scalar2=None, op0=ALU.mult), reads=["ivi0"], writes=["ivi"])
    crb = cr.unsqueeze(1).to_broadcast([128, 8, 32])
    cib = ci.unsqueeze(1).to_broadcast([128, 8, 32])
    tt_("dve", m2, ivr, crb, ALU.mult, ["ivr", "cr", "ivi"], ["q1"])
    tt_("dve", m3, ivi, cib, ALU.mult, ["ivi", "ci", "ivr"], ["q2"])
    tt_("dve", br, m2, m3, ALU.subtract, ["q1", "q2"], ["br"])
    tt_("dve", m2, ivr, cib, ALU.mult, ["ivr", "ci", "br"], ["q1"])
    tt_("dve", m3, ivi, crb, ALU.mult, ["ivi", "cr", "br"], ["q2"])
    tt_("dve", bi, m2, m3, ALU.add, ["q1", "q2"], ["bi"])
    if getattr(C, 'pro_lim', 99) < 3:
        return
    bar_ = lambda: P.barrier(keep=lambda k: isinstance(k, tuple) and k[0] == "wb")
    off_keep = off[0]
    off[0] = 8192 + 49152
    QMr = al([32, 2, 128], BF16); QMi = al([32, 2, 128], BF16); Kr = al([32, 128], BF16); Ki = al([32, 128], BF16)
    persist_end = off[0]
    off[0] = off_keep
    assert off_keep <= 8192 + 49152 - 8192, off_keep
    u1 = al([32, 16]); u2 = al([32, 16])
    b_re, b_im, c_re, c_im = BC[:, 0], BC[:, 1], BC[:, 2], BC[:, 3]
    P.op("pool", lambda h: h.memset(QMr, 0.0), writes=["QMr0"])
    P.op("pool", lambda h: h.memset(QMi, 0.0), writes=["QMi0"])
    lo, hi = slice(0, 64), slice(64, 128)
    for j in range(8):
        bc = lambda ap: ap.unsqueeze(2).to_broadcast([128, 32, 16])
        sl = slice(16 * j, 16 * j + 16)
        prj, pij = bc(PW[:, 0, j, :]), bc(PW[:, 1, j, :])
        brj, bij = bc(br[:, j, :]), bc(bi[:, j, :])
        dep = ["BC", "br", "bi"] + pwk
        tt_("dve", u1, c_re, prj, ALU.mult, dep + [("Q", j - 1)], ["u1"])
        tt_("dve", u2, c_im, pij, ALU.mult, dep + [("Q", j - 1)], ["u2"])
        tt_("dve", QMr[lo, :, 0, sl], u1[lo], u2[lo], ALU.subtract, ["u1", "u2", "QMr0"], [("Qr0", j)])
        tt_("dve", QMr[hi, :, 1, sl], u1[hi], u2[hi], ALU.subtract, ["u1", "u2", "QMr0"], [("Qr", j)])
        tt_("dve", u1, c_re, pij, ALU.mult, dep + [("Qr", j), ("Qr0", j)], ["u1"])
        tt_("dve", u2, c_im, prj, ALU.mult, dep + [("Qr", j), ("Qr0", j)], ["u2"])
        P.op("dve", lambda h, sl=sl: h.scalar_tensor_tensor(out=QMi[lo, :, 0, sl], in0=u1[lo], scalar=-1.0, in1=u2[lo], op0=ALU.mult, op1=ALU.subtract),
             reads=["u1", "u2", "QMi0"], writes=[("Qi0", j)])
        P.op("dve", lambda h, sl=sl: h.scalar_tensor_tensor(out=QMi[hi, :, 1, sl], in0=u1[hi], scalar=-1.0, in1=u2[hi], op0=ALU.mult, op1=ALU.subtract),
             reads=["u1", "u2", "QMi0"], writes=[("Qi", j)])
        tt_("dve", u1, b_re, brj, ALU.mult, dep + [("Qi", j), ("Qi0", j)], ["u1"])
        tt_("dve", u2, b_im, bij, ALU.mult, dep + [("Qi", j), ("Qi0", j)], ["u2"])
        tt_("dve", Kr[:, :, sl], u1, u2, ALU.subtract, ["u1", "u2"], [("Kr", j)])
        tt_("dve", u1, b_re, bij, ALU.mult, dep + [("Kr", j)], ["u1"])
        tt_("dve", u2, b_im, brj, ALU.mult, dep + [("Kr", j)], ["u2"])
        tt_("dve", Ki[:, :, sl], u1, u2, ALU.add, ["u1", "u2"], [("Q", j)])
    bar_()
    off[0] = 8192
    allq = []
    if getattr(C, 'pro_lim', 99) < 4:
        return
    KTMr = al([32, 2, 128], BF16); KTMi = al([32, 2, 128], BF16)
    assert off[0] <= 8192 + 49152
    tmpK = [carve(C, 110592, [8, 128], BF16) for i_ in range(2)]
    P.op("pool", lambda h: h.memset(KTMr, 0.0), writes=["KTM0"])
    P.op("pool", lambda h: h.memset(KTMi, 0.0), writes=["KTM1"])
    for ri, (Ksrc, KTdst) in enumerate(((Kr, KTMr), (Ki, KTMi))):
        for q in range(4):
            bk = 2 * ri + (q % 2)
            psb = C.bank(bk).bitcast(BF16)
            for e in range(8):
                g2 = 8 * q + e
                P.op("pe", lambda h, Ksrc=Ksrc, g2=g2, e=e, psb=psb: h.transpose(psb[:, 128 * e:128 * e + 128], Ksrc[:, g2, :], C.ident[:]),
                     reads=["ident"], writes=[("ps", bk)])
            tk = tmpK[q % 2]
            P.op("act", lambda h, tk=tk, psb=psb: h.activation(out=tk, in_=psb.rearrange("p (a b) -> p a b", a=8), func=AF.Copy),
                 reads=[("ps", bk)], writes=[("tmpK", 0)])
            P.op("dve", lambda h, KTdst=KTdst, q=q, tk=tk: h.tensor_copy(out=KTdst[:, 8 * q:8 * q + 8, 0, 0:64], in_=tk[:, :, 0:64]),
                 reads=[("tmpK", 0), "KTM0", "KTM1"], writes=[("KT", ri, q, 0)])
            P.op("dve", lambda h, KTdst=KTdst, q=q, tk=tk: h.tensor_copy(out=KTdst[:, 8 * q:8 * q + 8, 1, 64:128], in_=tk[:, :, 64:128]),
                 reads=[("tmpK", 0), "KTM0", "KTM1"], writes=[("KT", ri, q, 1)])
    if getattr(C, 'pro_lim', 99) < 5:
        return
    TT = al([64, 128], BF16)
    tmpT = [carve(C, 106496 + 2048 * i_, [4, 128], F32) for i_ in range(2)]
    assert off[0] <= 8192 + 49152 and persist_end <= 106496, (off[0], persist_end)
    for q in range(16):
        bk = 4 + (q % 2)
        ps = C.bank(bk)
        for e in range(2):
            g2 = 2 * q + e
            P.op("pe", lambda h, g2=g2, e=e, ps=ps: h.matmul(
                ps[:, 256 * e:256 * e + 256], lhsT=Kr[:, g2, :], rhs=QMr[:, g2, :, :].rearrange("p a b -> p (a b)"), start=True, stop=False),
                reads=[], writes=[("ps", bk)])
            P.op("pe", lambda h, g2=g2, e=e, ps=ps: h.matmul(
                ps[:, 256 * e:256 * e + 256], lhsT=Ki[:, g2, :], rhs=QMi[:, g2, :, :].rearrange("p a b -> p (a b)"), start=False, stop=True),
                reads=[], writes=[("ps", bk)])
        tm = tmpT[q % 2]
        P.op("dve", lambda h, tm=tm, ps=ps: h.tensor_tensor(
            out=tm, in0=ps.rearrange("p (a b) -> p a b", a=4), in1=C.bmask[:].unsqueeze(1).to_broadcast([128, 4, 128]), op=ALU.mult),
            reads=[("ps", bk), "bmask"], writes=[("tmT", q % 2)])
        for e in range(4):
            g = 4 * q + e
            P.op("dve", lambda h, tm=tm, g=g, e=e: h.scalar_tensor_tensor(
                out=TT[:, g, :], in0=C.identf[:], scalar=C.vec[:, V_DSK + g:V_DSK + g + 1], in1=tm[:, e, :], op0=ALU.mult, op1=ALU.add),
                reads=[("tmT", q % 2), "identf", "vec"], writes=[("TT", g)])
    if getattr(C, 'pro_lim', 99) < 6:
        return
    bar_()
    flat4 = lambda ap: ap.rearrange("p a b c -> p (a b c)")
    for m_, src in enumerate((QMr, QMi, KTMr, KTMi)):
        P.dma("sp", lambda h, m_=m_, src=src: h.dma_start(out=C.scr_mats[m_], in_=flat4(src)), writes=[("scr_mats", m_)])
    P.dma("sp", lambda h: h.dma_start(out=C.scr_tt, in_=TT.rearrange("p a b -> p (a b)")), writes=["scr_tt"])


def ssm_layer(C, b):
    P = C.P
    keepw = lambda k: isinstance(k, tuple) and k[0] in ("wb", "x")
    bar = lambda: P.barrier(keep=keepw)
    A0, B0, C0, M0 = 0, 32768, 65536, 98304
    h = carve(C, A0, [KC, S], BF16)
    uD = carve(C, B0, [KC, 2, 1024], BF16)
    U = carve(C, A0, [NG, 128], BF16)
    Zbf = carve(C, A0 + 16384, [2, 32, 128], BF16)
    Wst = carve(C, C0, [2, 32 * 128], F32)
    Yg = carve(C, C0, [NG, 128], BF16)
    zt = carve(C, A0, [KC, 1024], BF16)
    gl = carve(C, A0 + 16384, [KC, 1024], BF16)
    tmpf = [carve(C, C0 + 16384 + 4096 * i, [1024], F32) for i in range(2)]
    tmps = [carve(C, C0 + 24576 + 1024 * i, [512], BF16) for i in range(2)]
    ring = [dict(Qr=carve(C, M0 + 6144 * r, [4, 2, 128], BF16), Qi=carve(C, M0 + 6144 * r + 2048, [4, 2, 128], BF16),
                 KTr=carve(C, M0 + 6144 * r, [4, 2, 128], BF16), KTi=carve(C, M0 + 6144 * r + 2048, [4, 2, 128], BF16),
                 TT=carve(C, M0 + 6144 * r + 4096, [8, 128], BF16)) for r in range(2)]
    X0 = M0 + 12288
    sA = carve(C, X0, [2, 32], F32)
    sM1 = carve(C, X0 + 256, [2, 32], F32)
    sM2 = carve(C, X0 + 512, [2, 32], F32)
    DD = carve(C, X0 + 768, [2, 32], F32)
    DX = carve(C, X0 + 1024, [2, 32], F32)
    g1col = 48 + 16

    def hout(k, tt):
        return h[:, k, 512 * tt:512 * tt + 512], ("h", k, tt)

    norm_mod(C, b, 2, 48, hout)
    bar()
    for sl_ in range(2):
        (wi,), wik = C.W.next(lambda C_, sl_=sl_: [wslab(C_.w_in, 512 * sl_, 512)])
        for q in range(4):
            oc = 4 * sl_ + q
            for tt in range(4):
                bk = 6 + (tt % 2)
                ps = C.bank(bk)
                for k in range(KC):
                    P.op("pe", lambda h_, wi=wi, k=k, q=q, tt=tt, ps=ps: h_.matmul(
                        ps, lhsT=wi[:, k, 128 * q:128 * q + 128], rhs=h[:, k, 512 * tt:512 * tt + 512], start=(k == 0), stop=(k == KC - 1)),
                        reads=[wik, ("h", k, tt)], writes=[("ps", bk)])
                hf, c0 = tt // 2, 64 * (tt % 2)
                dst = uD[:, oc, hf, :].rearrange("p (i c) -> p i c", i=8)[:, :, c0:c0 + 64]
                src = ps.rearrange("p (c i) -> p i c", i=8)
                if tt % 2 == 0:
                    P.op("act", lambda h_, dst=dst, src=src: h_.activation(out=dst, in_=src, func=AF.Copy),
                         reads=[("ps", bk)], writes=[("uD", oc, hf, tt % 2)])
                else:
                    P.op("dve", lambda h_, dst=dst, src=src: h_.tensor_copy(out=dst, in_=src),
                         reads=[("ps", bk)], writes=[("uD", oc, hf, tt % 2)])
    P.op("dve", lambda h_: h_.tensor_copy(out=DD[:, 0, :], in_=C.ssD[:, 0, :]), writes=["DD0"])
    P.op("dve", lambda h_: h_.tensor_copy(out=DD[:, 1, :], in_=C.ssD[:, 0, :]), reads=["DD0"], writes=["DD1"])
    P.op("dve", lambda h_: h_.tensor_scalar(out=DX[:, 0, :], in0=C.ssD[:, 1, :], scalar1=-1.0, scalar2=None, op0=ALU.mult), reads=["DD1"], writes=["DX0"])
    P.op("dve", lambda h_: h_.tensor_copy(out=DX[:, 1, :], in_=C.ssD[:, 1, :]), reads=["DX0"], writes=["DX"])
    P.op("dve", lambda h_: h_.memset(C.sscar[:], 0.0), reads=["DX"], writes=["sscar"])
    bar()

    def load_mats(gb, r, which):
        R = ring[r]
        fns = []
        f3 = lambda ap: ap.rearrange("p a b c -> p (a b c)")
        if which == "K":
            fns.append(lambda h_, R=R, gb=gb: h_.dma_start(out=f3(R["KTr"]), in_=C.scr_mats[2][:, 1024 * gb:1024 * gb + 1024]))
            fns.append(lambda h_, R=R, gb=gb: h_.dma_start(out=f3(R["KTi"]), in_=C.scr_mats[3][:, 1024 * gb:1024 * gb + 1024]))
        else:
            fns.append(lambda h_, R=R, gb=gb: h_.dma_start(out=f3(R["Qr"]), in_=C.scr_mats[0][:, 1024 * gb:1024 * gb + 1024]))
            fns.append(lambda h_, R=R, gb=gb: h_.dma_start(out=f3(R["Qi"]), in_=C.scr_mats[1][:, 1024 * gb:1024 * gb + 1024]))
            fns.append(lambda h_, R=R, gb=gb: h_.dma_start(out=R["TT"].rearrange("p a b -> p (a b)"), in_=C.scr_tt[:, 1024 * gb:1024 * gb + 1024]))
        P.dma("sp", fns, writes=[("ring", r)])

    for hf in range(2):
        for k in range(KC):
            P.dma("sp", lambda h_, k=k, hf=hf: h_.dma_start(
                out=C.scr_u[:, 128 * k:128 * k + 128, :].rearrange("i p c -> p i c"),
                in_=uD[:, k, hf, :].rearrange("p (i c) -> p i c", i=8)),
                reads=[("uD", k, hf, 0), ("uD", k, hf, 1)], writes=[("scr_u", k)])
        for i in range(8):
            P.dma("sp", lambda h_, i=i: h_.dma_start(
                out=U[16 * i:16 * i + 16, :, :], in_=C.scr_u[i].rearrange("(g h) c -> h g c", h=16)),
                reads=[("scr_u", k) for k in range(KC)], writes=[("U", i)])
        Ukeys = [("U", i) for i in range(8)]
        load_mats(0, 0, "K")
        for gb in range(8):
            if gb + 1 < 8:
                load_mats(gb + 1, (gb + 1) % 2, "K")
            R = ring[gb % 2]
            br_, bi_ = 2 * (gb % 2), 2 * (gb % 2) + 1
            for g2l in range(4):
                g2 = 4 * gb + g2l
                for bk_, KT in ((br_, R["KTr"]), (bi_, R["KTi"])):
                    for gp in range(2):
                        P.op("pe", lambda h_, bk_=bk_, KT=KT, g2l=g2l, gp=gp, g2=g2: h_.matmul(
                            C.bank(bk_)[:, 128 * g2l:128 * g2l + 128], lhsT=KT[:, g2l, gp, :], rhs=U[:, 2 * g2 + gp, :],
                            start=(gp == 0), stop=(gp == 1)),
                            reads=[("ring", gb % 2)] + Ukeys, writes=[("ps", bk_)])
            for ri, bk_ in ((0, br_), (1, bi_)):
                eng = "act" if ri == 0 else "dve"
                dst = Wst[:, ri, :].rearrange("p (c g) -> p c g", g=32)[:, :, 4 * gb:4 * gb + 4]
                src = C.bank(bk_).rearrange("p (g c) -> p c g", g=4)
                if eng == "act":
                    P.op("act", lambda h_, dst=dst, src=src: h_.activation(out=dst, in_=src, func=AF.Copy),
                         reads=[("ps", bk_)], writes=[("W", gb)])
                else:
                    P.op("dve", lambda h_, dst=dst, src=src: h_.tensor_copy(out=dst, in_=src),
                         reads=[("ps", bk_)], writes=[("W2", gb)])
        bar()
        Wv = Wst.rearrange("p r (c g) -> p r c g", g=32)
        for c in range(128):
            zprev = C.sscar[:] if c == 0 else Wv[:, :, c - 1, :]
            wc = Wv[:, :, c, :]
            ns = c > 0
            P.op("dve", lambda h_, zprev=zprev, wc=wc: h_.tensor_tensor(out=sA, in0=zprev, in1=wc, op=ALU.add), reads=["rec"], writes=["rec"], nosync=ns)
            P.op("dve", lambda h_: h_.tensor_tensor(out=sM1, in0=DD, in1=sA, op=ALU.mult), reads=["rec"], writes=["rec"], nosync=True)
            P.op("dve", lambda h_: h_.tensor_tensor(out=sM2[:, 0, :], in0=DX[:, 0, :], in1=sA[:, 1, :], op=ALU.mult), reads=["rec"], writes=["rec"], nosync=True)
            P.op("dve", lambda h_: h_.tensor_tensor(out=sM2[:, 1, :], in0=DX[:, 1, :], in1=sA[:, 0, :], op=ALU.mult), reads=["rec"], writes=["rec"], nosync=True)
            P.op("dve", lambda h_, wc=wc: h_.tensor_tensor(out=wc, in0=sM1, in1=sM2, op=ALU.add), reads=["rec"], writes=["rec"], nosync=True)
        Zv = Zbf
        P.op("dve", lambda h_: h_.tensor_copy(out=Zv[:, :, :, 0], in_=C.sscar[:]), reads=["rec"], writes=["rec"])
        P.op("dve", lambda h_: h_.tensor_copy(out=Zv[:, 0, :, 1:128], in_=Wv[:, 0, 0:127, :].rearrange("p c g -> p g c")), reads=["rec"], writes=["rec"])
        P.op("act", lambda h_: h_.activation(out=Zv[:, 1, :, 1:128], in_=Wv[:, 1, 0:127, :].rearrange("p c g -> p g c"), func=AF.Copy), reads=["rec"], writes=["rec2"])
        P.op("dve", lambda h_: h_.tensor_copy(out=C.sscar[:], in_=Wv[:, :, 127, :]), reads=["rec"], writes=["rec"])
        bar()
        load_mats(0, 0, "Q")
        for gb in range(8):
            if gb + 1 < 8:
                load_mats(gb + 1, (gb + 1) % 2, "Q")
            R = ring[gb % 2]
            for half in range(2):
                bk_ = 4 + (2 * gb + half) % 2
                for e in range(4):
                    gi = 4 * half + e
                    g = 8 * gb + gi
                    g2l, gp = gi // 2, gi % 2
                    g2 = g // 2
                    rows = slice(64 * gp, 64 * gp + 64)
                    o_ = C.bank(bk_)[:, 128 * e:128 * e + 128]
                    P.op("pe", lambda h_, o_=o_, R=R, gi=gi, g=g: h_.matmul(o_, lhsT=R["TT"][:, gi, :], rhs=U[:, g, :], start=True, stop=False),
                         reads=[("ring", gb % 2)] + Ukeys, writes=[("ps", bk_)])
                    P.op("pe", lambda h_, o_=o_, R=R, g2l=g2l, gp=gp, g2=g2: h_.matmul(
                        o_, lhsT=R["Qr"][:, g2l, gp, :], rhs=Zbf[:, 0, g2, :], start=False, stop=False),
                        reads=[("ring", gb % 2)], writes=[("ps", bk_)])
                    P.op("pe", lambda h_, o_=o_, R=R, g2l=g2l, gp=gp, g2=g2: h_.matmul(
                        o_, lhsT=R["Qi"][:, g2l, gp, :], rhs=Zbf[:, 1, g2, :], start=False, stop=True),
                        reads=[("ring", gb % 2)], writes=[("ps", bk_)])
                g0 = 8 * gb + 4 * half
                dst = Yg[:, g0:g0 + 4, :]
                src = C.bank(bk_).rearrange("p (a b) -> p a b", a=4)
                if half == 0:
                    P.op("act", lambda h_, dst=dst, src=src: h_.activation(out=dst, in_=src, func=AF.Copy), reads=[("ps", bk_)], writes=[("Yg", gb, half)])
                else:
                    P.op("dve", lambda h_, dst=dst, src=src: h_.tensor_copy(out=dst, in_=src), reads=[("ps", bk_)], writes=[("Yg", gb, half)])
        Ygkeys = [("Yg", gb, hh) for gb in range(8) for hh in range(2)]
        for j in range(8):
            P.dma("sp", lambda h_, j=j: h_.dma_start(
                out=C.scr_y[j].rearrange("(g h) c -> h g c", h=16), in_=Yg[16 * j:16 * j + 16, :, :]),
                reads=Ygkeys, writes=[("scr_y", j)])
        bar()
        for k in range(KC):
            P.dma("sp", lambda h_, k=k, hf=hf: h_.dma_start(
                out=uD[:, k, hf, :].rearrange("p (j c) -> p j c", j=8),
                in_=C.scr_y[:, 128 * k:128 * k + 128, :].rearrange("j p c -> p j c")),
                writes=[("yD", k)])
        for k in range(KC):
            yv = uD[:, k, hf, :]
            tf = tmpf[k % 2]
            tkey = ("tf", k % 2)
            P.op("act", lambda h_, tf=tf, yv=yv: h_.activation(out=tf, in_=yv, func=AF.Square), reads=[("yD", k)], writes=[tkey])
            P.op("dve", lambda h_, tf=tf: h_.tensor_scalar(out=tf, in0=tf, scalar1=0.044715, scalar2=1.0, op0=ALU.mult, op1=ALU.add),
                 reads=[tkey], writes=[tkey])
            P.op("dve", lambda h_, tf=tf, yv=yv: h_.tensor_tensor(out=tf, in0=tf, in1=yv, op=ALU.mult), reads=[tkey, ("yD", k)], writes=[tkey])
            P.op("act", lambda h_, tf=tf: h_.activation(out=tf, in_=tf, func=AF.Sigmoid, scale=1.5957691216057308), reads=[tkey], writes=[tkey])
            P.op("dve", lambda h_, tf=tf, yv=yv, k=k: h_.tensor_tensor(out=zt[:, k, :], in0=tf, in1=yv, op=ALU.mult),
                 reads=[tkey, ("yD", k)], writes=[("zt", k)])
        for sl_ in range(2):
            (wg_,), wgk = C.W.next(lambda C_, sl_=sl_: [wslab(C_.w_glu, 512 * sl_, 512)])
            for q in range(4):
                oc = 4 * sl_ + q
                for tl in range(2):
                    bk = 6 + (tl % 2)
                    ps = C.bank(bk)
                    for k in range(KC):
                        P.op("pe", lambda h_, wg_=wg_, k=k, q=q, tl=tl, ps=ps: h_.matmul(
                            ps, lhsT=wg_[:, k, 128 * q:128 * q + 128], rhs=zt[:, k, 512 * tl:512 * tl + 512], start=(k == 0), stop=(k == KC - 1)),
                            reads=[wgk, ("zt", k)], writes=[("ps", bk)])
                    ts_ = tmps[tl % 2]
                    P.op("act", lambda h_, ts_=ts_, ps=ps, oc=oc: h_.activation(
                        out=ts_, in_=ps, func=AF.Sigmoid, bias=C.vec[:, V_BGLU + oc:V_BGLU + oc + 1], scale=1.0),
                        reads=[("ps", bk), "vec"], writes=[("tmps", tl % 2)])
                    P.op("dve", lambda h_, ts_=ts_, oc=oc, tl=tl: h_.tensor_tensor(
                        out=gl[:, oc, 512 * tl:512 * tl + 512], in0=zt[:, oc, 512 * tl:512 * tl + 512], in1=ts_, op=ALU.mult),
                        reads=[("tmps", tl % 2), ("zt", oc)], writes=[("gl", oc)])
        for sl_ in range(2):
            (wo_,), wok = C.W.next(lambda C_, sl_=sl_: [wslab(C_.w_o_ssm, 512 * sl_, 512)])
            for q in range(4):
                oc = 4 * sl_ + q
                for tl in range(2):
                    bk = 6 + (tl % 2)
                    ps = C.bank(bk)
                    for k in range(KC):
                        P.op("pe", lambda h_, wo_=wo_, k=k, q=q, tl=tl, ps=ps: h_.matmul(
                            ps, lhsT=wo_[:, k, 128 * q:128 * q + 128], rhs=gl[:, k, 512 * tl:512 * tl + 512], start=(k == 0), stop=(k == KC - 1)),
                            reads=[wok] + [("gl", kk) for kk in range(KC)], writes=[("ps", bk)])
                    xv = C.x[:, oc, 1024 * hf:1024 * hf + 1024].rearrange("p (c j) -> p j c", j=8)[:, 4 * tl:4 * tl + 4, :]
                    pv = ps.rearrange("p (j c) -> p j c", j=4)
                    P.op("dve", lambda h_, xv=xv, pv=pv, oc=oc: h_.scalar_tensor_tensor(
                        out=xv, in0=pv, scalar=C.mod[:, g1col + oc, b:b + 1], in1=xv, op0=ALU.mult, op1=ALU.add),
                        reads=[("ps", bk), ("x", oc, 2 * hf), ("x", oc, 2 * hf + 1), "mod"], writes=[("x", oc, 2 * hf), ("x", oc, 2 * hf + 1)])
        bar()
```

```python
import contextlib
import numpy as np
import concourse.bass as bass
import concourse.mybir as mybir
from concourse.bass_utils import run_bass_kernel_spmd

F32 = mybir.dt.float32
BF16 = mybir.dt.bfloat16
AF = mybir.ActivationFunctionType
ALU = mybir.AluOpType

ENGS = ("pe", "act", "dve", "pool", "sp")
N_DMA_SEMS = 24


class Prog:
    def __init__(self, nc):
        self.nc = nc
        self.ops = {e: [] for e in ENGS}
        self.reg = {}
        self.dma_use = [0] * N_DMA_SEMS
        self.dma_rr = 0
        self.out_tokens = []
        self.arena_dma = {}
        self.last_compute = {}

    def _collect(self, eng, reads, writes):
        need = {}

        def add(k, v):
            if need.get(k, -1) < v:
                need[k] = v

        for r in reads:
            e = self.reg.get(r)
            if e is not None:
                for k, v in e[0].items():
                    add(k, v)
        for w in writes:
            e = self.reg.get(w)
            if e is not None:
                for k, v in e[0].items():
                    add(k, v)
                for k, v in e[1].items():
                    if k == eng:
                        continue
                    add(k, v)
        if eng == "pe":
            need.pop("pe", None)
        return need

    def _commit(self, toks, reads, writes):
        for r in reads:
            e = self.reg.setdefault(r, [{}, {}])
            for k, v in toks:
                if e[1].get(k, -1) < v:
                    e[1][k] = v
        for w in writes:
            self.reg[w] = [dict(toks), {}]

    def op(self, eng, fn, reads=(), writes=(), nosync=False):
        need = self._collect(eng, reads, writes)
        if nosync:
            need.pop(eng, None)
        idx = len(self.ops[eng])
        self.ops[eng].append([fn, list(need.items()), False, None])
        self._commit([(eng, idx)], reads, writes)
        self.last_compute[eng] = idx
        return idx

    def dma(self, eng, fns, reads=(), writes=(), arena=True, out=False):
        if not isinstance(fns, (list, tuple)):
            fns = [fns]
        need = self._collect("x", reads, writes)
        toks = []
        for n, fn in enumerate(fns):
            i = self.dma_rr
            self.dma_rr = (self.dma_rr + 1) % N_DMA_SEMS
            nd = dict(need) if n == 0 else {}
            if self.dma_use[i] > 0:
                k = ("d", i)
                v = 16 * self.dma_use[i]
                if nd.get(k, -1) < v:
                    nd[k] = v
            self.dma_use[i] += 1
            tok = (("d", i), 16 * self.dma_use[i])
            self.ops[eng].append([fn, list(nd.items()), False, tok])
            toks.append(tok)
            if arena:
                self.arena_dma[tok[0]] = tok[1]
            if out:
                self.out_tokens.append(tok)
        self._commit(toks, reads, writes)
        return toks

    def barrier(self, keep=lambda key: False):
        toks = [(e, i) for e, i in self.last_compute.items()]
        toks += list(self.arena_dma.items())
        for e in ENGS:
            self.ops[e].append([None, [t for t in toks if t[0] != e], False, None])
        self.arena_dma = {}
        self.reg = {k: v for k, v in self.reg.items() if keep(k)}

    def finish(self):
        self.ops["sp"].append([None, list(self.out_tokens), False, None])

    def emit(self):
        nc = self.nc
        for e in ENGS:
            for rec in self.ops[e]:
                for k, v in rec[1]:
                    if isinstance(k, str):
                        assert self.ops[k][v][0] is not None and self.ops[k][v][3] is None
                        self.ops[k][v][2] = True
        sig = {}
        for e in ENGS:
            c = 0
            s = []
            for rec in self.ops[e]:
                if rec[2]:
                    c += 1
                s.append(c)
            sig[e] = s
            assert c < 60000, (e, c)
        self.stats = {e: (len(self.ops[e]), sig[e][-1] if sig[e] else 0) for e in ENGS}
        with contextlib.ExitStack() as st:
            esem = {e: st.enter_context(nc.semaphore("s_" + e)) for e in ENGS}
            dsem = [st.enter_context(nc.semaphore("d%d" % i)) for i in range(N_DMA_SEMS)]
            block = st.enter_context(nc.Block())

            def run(e):
                def body(h):
                    water = {}
                    for fn, deps, signal, dtok in self.ops[e]:
                        for k, v in deps:
                            if isinstance(k, str):
                                sem = esem[k]
                                val = sig[k][v]
                            else:
                                sem = dsem[k[1]]
                                val = v
                            if water.get(k, 0) >= val:
                                continue
                            water[k] = val
                            h.wait_ge(sem, val)
                        if fn is None:
                            continue
                        ins = fn(h)
                        if dtok is not None:
                            ins.then_inc(dsem[dtok[0][1]], 16)
                        elif signal:
                            ins.then_inc(esem[e], 1)
                return body

            block.tensor(run("pe"))
            block.scalar(run("act"))
            block.vector(run("dve"))
            block.gpsimd(run("pool"))
            block.sync(run("sp"))


D = 1024
KC = 8
S = 2048
NB = 2
NH = 16
DH = 64
FF = 2816
FJ = 22
NG = 64
EPS = 1e-6
TWO_PI = 6.283185307179586

V_NMIX, V_NFFN, V_NOUT, V_BMOD, V_BFIN, V_BGLU, V_CW, V_CB, V_DSK, NV = 0, 16, 32, 40, 136, 152, 160, 424, 512, 576
NMOD = 112


class WStream:
    def __init__(self, P, bufs, schedule=None):
        self.P = P
        self.bufs = bufs
        self.n = len(bufs)
        self.schedule = schedule
        self.req = []
        self.issued = 0
        self.cur = 0

    def _issue(self, idx):
        parts = (self.schedule[idx] if self.schedule is not None else self.req[idx])(self.C)
        slot = idx % self.n
        buf = self.bufs[slot]
        fns = []
        off = 0
        for (src, shape) in parts:
            n = int(np.prod(shape))
            dst = buf[:, off:off + n]
            if len(shape) == 2:
                dst = dst.rearrange("p (k n) -> p k n", k=shape[0])
            fns.append(lambda h, dst=dst, src=src: h.dma_start(out=dst, in_=src))
            off += n
        self.P.dma("pool", fns, writes=[("wb", slot)], arena=False)

    def next(self, parts_fn, ahead=2):
        idx = self.cur
        self.cur += 1
        self.req.append(parts_fn)
        parts = parts_fn(self.C)
        lim = idx + ahead if self.schedule is not None else idx
        while self.issued <= lim and (self.schedule is None or self.issued < len(self.schedule)):
            self._issue(self.issued)
            self.issued += 1
        slot = idx % self.n
        buf = self.bufs[slot]
        views = []
        off = 0
        for (src, shape) in parts:
            n = int(np.prod(shape))
            v = buf[:, off:off + n]
            if len(shape) == 2:
                v = v.rearrange("p (k n) -> p k n", k=shape[0])
            views.append(v)
            off += n
        return views, ("wb", slot)


def wslab(w2d, c0, n, r0=0, kk=KC):
    return (w2d[r0:r0 + kk * 128, c0:c0 + n].rearrange("(k p) n -> p k n", p=128), (kk, n))


class Ctx:
    pass


def build(stages=("attn", "ffn0", "ssm", "ffn1"), schedule=None, nseq=NB, debug=None):
    nc = bass.Bass("TRN2", target_bir_lowering=False)
    C = Ctx()
    C.debug = debug
    if isinstance(debug, str) and debug.startswith('pro'):
        C.pro_lim = int(debug[3:4])
        C.tt_lim = int(debug[4:5]) if len(debug) > 4 else 9
    if isinstance(debug, str) and debug.startswith('ng'):
        C.ngroups = int(debug[2:])
    C.nc = nc
    C.stages = stages
    di = lambda name, shape: nc.dram_tensor(name, shape, F32, kind="ExternalInput").ap()
    C.xT = di("xT", [NB, D, S])
    C.cT = di("cT", [D, NB])
    C.vecs = di("vecs", [128, NV])
    C.w_mod = di("w_mod", [2, D, 6 * D])
    C.w_fin = di("w_fin", [D, 2 * D])
    C.w_qkv = di("w_qkv", [D, 3 * D])
    C.w_o_attn = di("w_o_attn", [D, D])
    C.w_in = di("w_in_ssm", [D, D])
    C.w_glu = di("w_glu", [D, D])
    C.w_o_ssm = di("w_o_ssm", [D, D])
    C.w_up = di("w_up", [2, D, 2 * FF])
    C.w_down = di("w_down", [2, FF, D])
    C.ssm_s = di("ssm_s", [128, 3, 32])
    C.ssm_bc = di("ssm_bc", [128, 4, 32, 16])
    C.outT = nc.dram_tensor("outT", [NB, D, S], F32, kind="ExternalOutput").ap()
    C.scr_mats = nc.dram_tensor("scr_mats", [4, 128, 32 * 256], BF16, kind="Internal").ap()
    C.scr_tt = nc.dram_tensor("scr_tt", [128, 64 * 128], BF16, kind="Internal").ap()
    C.scr_u = nc.dram_tensor("scr_u", [8, 64 * 16, 128], BF16, kind="Internal").ap()
    C.scr_y = nc.dram_tensor("scr_y", [8, 64 * 16, 128], BF16, kind="Internal").ap()

    P = Prog(nc)
    C.P = P
    with contextlib.ExitStack() as st:
        sb = lambda name, shape, dtype: st.enter_context(nc.sbuf_tensor(name, shape, dtype))
        C.x = sb("x_sb", [128, KC, S], F32)
        C.vec = sb("vec_sb", [128, NV], F32)
        C.mod = sb("mod_sb", [128, NMOD, NB], F32)
        C.A = sb("A_sb", [128, 5, KC, NB], F32)
        C.cact = sb("cact", [128, KC, NB], BF16)
        C.cin = sb("cin", [128, KC, NB], F32)
        C.ones = sb("ones_bf", [128, 128], BF16)
        C.tri_i = sb("tri_i", [128, 128], BF16)
        C.tri_r = sb("tri_r", [128, 128], BF16)
        C.ident = sb("ident", [128, 128], BF16)
        C.maskbig = sb("maskbig", [128, 896], BF16)
        C.bmask = sb("bmask", [128, 128], F32)
        C.identf = sb("identf", [128, 128], F32)
        C.pid_i = sb("pid_i", [128, 2], mybir.dt.int32)
        C.pid_f = sb("pid_f", [128, 2], F32)
        C.halo = sb("halo", [128, 2 * FJ, 2], F32)
        C.sscar = sb("sscar", [128, 2, 32], F32)
        C.ssD = sb("ssD", [128, 2, 32], F32)
        wb = [sb("wbuf%d" % i, [128, 4096], BF16) for i in range(3)]
        C.W = WStream(P, wb, schedule)
        C.W.C = C
        C.ARENA_BYTES = 110 * 1024
        C.arena = sb("arena", [128, C.ARENA_BYTES // 2], BF16)
        C.iot_i = C.arena[:, 0:1792].bitcast(mybir.dt.int32)
        C.iot_f = C.arena[:, 2048:2048 + 1792].bitcast(F32)
        C.NT_OFF = C.ARENA_BYTES - 8192
        pq = [st.enter_context(nc.psum_tensor("pq%d" % i, [128, 1024], F32)) for i in range(4)]
        C.pq = pq
        C.bank = lambda i: pq[i // 2][:, 512 * (i % 2):512 * (i % 2) + 512]

        prologue(C)
        for b in range(nseq):
            run_sequence(C, b)
        P.finish()
        P.emit()
    C.requests = C.W.req
    return nc, C


def carve(C, off, shape, dtype):
    n = int(np.prod(shape))
    if dtype == F32:
        ap = C.arena[:, off // 2: off // 2 + 2 * n].bitcast(F32)
    else:
        ap = C.arena[:, off // 2: off // 2 + n]
    if len(shape) == 2:
        ap = ap.rearrange("p (a b) -> p a b", a=shape[0])
    elif len(shape) == 3:
        ap = ap.rearrange("p (a b c) -> p a b c", a=shape[0], b=shape[1])
    return ap


def prologue(C):
    P, nc = C.P, C.nc
    I32 = mybir.dt.int32
    P.dma("sp", lambda h: h.dma_start(out=C.vec[:], in_=C.vecs), writes=["vec"])
    P.dma("sp", lambda h: h.dma_start(out=C.cin[:], in_=C.cT.rearrange("(k p) b -> p k b", p=128)), writes=["cin"])
    P.op("pool", lambda h: h.iota(C.iot_i[:], pattern=[[1, 896]], base=-384, channel_multiplier=0), writes=["iot_i"])
    P.op("pool", lambda h: h.iota(C.pid_i[:, 0:1], pattern=[[0, 1]], base=0, channel_multiplier=1), writes=["pid_i"])
    P.op("dve", lambda h: h.tensor_copy(out=C.iot_f[:], in_=C.iot_i[:]), reads=["iot_i"], writes=["iot_f"])
    P.op("dve", lambda h: h.tensor_copy(out=C.pid_f[:, 0:1], in_=C.pid_i[:, 0:1]), reads=["pid_i"], writes=["pid_f"])
    P.op("dve", lambda h: h.tensor_single_scalar(out=C.pid_i[:, 1:2], in_=C.pid_i[:, 0:1], scalar=4, op=ALU.arith_shift_right),
         reads=["pid_i"], writes=["pid_i2"])
    P.op("dve", lambda h: h.tensor_copy(out=C.pid_f[:, 1:2], in_=C.pid_i[:, 1:2]), reads=["pid_i2"], writes=["pid_f2"])
    pidx = C.pid_f[:, 0:1]
    col128 = C.iot_f[:, 384:512]
    rd = ["iot_f", "pid_f"]
    P.op("dve", lambda h: h.tensor_single_scalar(out=C.maskbig[:], in_=C.iot_f[:], scalar=pidx, op=ALU.is_gt), reads=rd, writes=["maskbig"])
    P.op("dve", lambda h: h.tensor_single_scalar(out=C.tri_i[:], in_=col128, scalar=pidx, op=ALU.is_le), reads=rd, writes=["tri_i"])
    P.op("dve", lambda h: h.tensor_single_scalar(out=C.tri_r[:], in_=col128, scalar=pidx, op=ALU.is_gt), reads=rd, writes=["tri_r"])
    P.op("dve", lambda h: h.tensor_single_scalar(out=C.ident[:], in_=col128, scalar=pidx, op=ALU.is_equal), reads=rd, writes=["ident"])
    P.op("dve", lambda h: h.tensor_single_scalar(out=C.identf[:], in_=col128, scalar=pidx, op=ALU.is_equal), reads=rd, writes=["identf"])
    P.op("dve", lambda h: h.memset(C.ones[:], 1.0 / D), writes=["ones"])
    P.op("dve", lambda h: h.tensor_single_scalar(out=C.iot_i[:, 0:128], in_=C.iot_i[:, 384:512], scalar=4, op=ALU.arith_shift_right),
         reads=["iot_i", "iot_f"], writes=["iot_i"])
    P.op("dve", lambda h: h.tensor_copy(out=C.iot_f[:, 0:128], in_=C.iot_i[:, 0:128]), reads=["iot_i", "maskbig"], writes=["iot_f"])
    P.op("dve", lambda h: h.tensor_single_scalar(out=C.bmask[:], in_=C.iot_f[:, 0:128], scalar=C.pid_f[:, 1:2], op=ALU.is_ge),
         reads=["iot_f", "pid_f2"], writes=["bmask"])
    P.op("act", lambda h: h.activation(out=C.cact[:], in_=C.cin[:], func=AF.Silu), reads=["cin"], writes=["cact"])
    def mod_tasks(w2d_fn, ncols, col0, bias0, bank):
        tasks = []
        for s_ in range(ncols // 512):
            def task(s_=s_):
                ps = C.bank(bank)
                (wv,), wk = C.W.next(lambda C_, w2d_fn=w2d_fn, s_=s_: [wslab(w2d_fn(C_), 512 * s_, 512)])
                for q in range(4):
                    for k in range(KC):
                        P.op("pe", lambda h, wv=wv, q=q, k=k, ps=ps: h.matmul(
                            ps[:, NB * q:NB * q + NB], lhsT=wv[:, k, 128 * q:128 * q + 128], rhs=C.cact[:, k, :],
                            start=(k == 0), stop=(k == KC - 1)),
                            reads=[wk, "cact"], writes=[("ps", bank)])
                c_ = col0 + 4 * s_
                b_ = bias0 + 4 * s_
                P.op("dve", lambda h, ps=ps, c_=c_, b_=b_: h.tensor_tensor(
                    out=C.mod[:, c_:c_ + 4, :], in0=ps[:, 0:NB * 4].rearrange("p (c b) -> p c b", b=NB),
                    in1=C.vec[:, b_:b_ + 4].unsqueeze(2).to_broadcast([128, 4, NB]), op=ALU.add),
                    reads=[("ps", bank), "vec"], writes=["mod"])
            tasks.append(task)
        return tasks

    sites = [(V_NMIX, 8), (V_NFFN, 32), (V_NMIX + 8, 48 + 8), (V_NFFN + 8, 48 + 32), (V_NOUT, 96 + 8)]

    def site_task(si):
        gcol, sccol = sites[si]
        P.op("dve", lambda h, si=si, gcol=gcol, sccol=sccol: h.scalar_tensor_tensor(
            out=C.A[:, si, :, :], in0=C.mod[:, sccol:sccol + 8, :], scalar=1.0,
            in1=C.vec[:, gcol:gcol + 8].unsqueeze(2).to_broadcast([128, 8, NB]), op0=ALU.add, op1=ALU.mult),
            reads=["mod", "vec"], writes=[("A", si)])

    for t_ in mod_tasks(lambda C_: C_.w_mod[0], 6 * D, 0, V_BMOD, 7):
        t_()
    site_task(0)
    site_task(1)
    C.deferred = (mod_tasks(lambda C_: C_.w_mod[1], 6 * D, 48, V_BMOD + 48, 7)
                  + mod_tasks(lambda C_: C_.w_fin, 2 * D, 96, V_BFIN, 7)
                  + [lambda: site_task(2), lambda: site_task(3), lambda: site_task(4)])
    if "attn" not in C.stages:
        run_deferred(C, 10 ** 6)
    if "ssm" in C.stages:
        ssm_prologue(C)
    P.barrier(keep=lambda k: isinstance(k, tuple) and k[0] == "wb")


def run_deferred(C, n):
    while n > 0 and C.deferred:
        C.deferred.pop(0)()
        n -= 1


def norm_mod(C, b, site, shcol, out_fn, out_dtype_bf16=True):
    P = C.P
    sq = [carve(C, C.NT_OFF + 1024 * i, [512], BF16) for i in range(2)]
    rs = carve(C, C.NT_OFF + 2048, [512], F32)
    tmp = [carve(C, C.NT_OFF + 4096 + 2048 * i, [512], F32) for i in range(2)]
    for tt in range(4):
        ts = slice(512 * tt, 512 * tt + 512)
        ps = C.bank(tt % 2)
        for k in range(KC):
            P.op("act", lambda h, k=k, ts=ts: h.activation(out=sq[k % 2], in_=C.x[:, k, ts], func=AF.Square),
                 reads=[("x", k, tt)], writes=[("nsq", k % 2)])
            P.op("pe", lambda h, k=k, ps=ps: h.matmul(ps, lhsT=C.ones[:], rhs=sq[k % 2], start=(k == 0), stop=(k == KC - 1)),
                 reads=[("nsq", k % 2), "ones"], writes=[("ps", tt % 2)])
        P.op("act", lambda h, ps=ps: h.activation(out=rs, in_=ps, func=AF.Sqrt, bias=EPS, scale=1.0),
             reads=[("ps", tt % 2)], writes=["nrs"])
        P.op("dve", lambda h: h.reciprocal(out=rs, in_=rs), reads=["nrs"], writes=["nrs"])
        for k in range(KC):
            o, okey = out_fn(k, tt)
            P.op("dve", lambda h, k=k, ts=ts: h.tensor_tensor(out=tmp[k % 2], in0=C.x[:, k, ts], in1=rs, op=ALU.mult),
                 reads=[("x", k, tt), "nrs"], writes=[("ntmp", k % 2)])
            P.op("act", lambda h, k=k, o=o: h.activation(out=o, in_=tmp[k % 2], func=AF.Identity,
                                                         bias=C.mod[:, shcol + k, b:b + 1], scale=C.A[:, site, k, b:b + 1]),
                 reads=[("ntmp", k % 2), "mod", ("A", site)], writes=[okey])


def load_x(C, b):
    P = C.P
    for k in range(KC):
        P.dma("sp", lambda h, k=k: h.dma_start(out=C.x[:, k, :], in_=C.xT[b, 128 * k:128 * k + 128, :]),
              writes=[("x", k, tt) for tt in range(4)])


def final_out(C, b):
    obuf = [carve(C, 16384 + 2048 * i, [512], F32) for i in range(4)]
    cnt = [0]

    def out_fn(k, tt):
        i = cnt[0] % 4
        cnt[0] += 1
        return obuf[i], ("obuf", i)

    norm_mod_out(C, b, 4, 96, out_fn, obuf)


def norm_mod_out(C, b, site, shcol, out_fn, obuf):
    P = C.P
    state = {"n": 0}

    def wrapped(k, tt):
        return out_fn(k, tt)

    sq = [carve(C, C.NT_OFF + 1024 * i, [512], BF16) for i in range(2)]
    rs = carve(C, C.NT_OFF + 2048, [512], F32)
    tmp = [carve(C, C.NT_OFF + 4096 + 2048 * i, [512], F32) for i in range(2)]
    for tt in range(4):
        ts = slice(512 * tt, 512 * tt + 512)
        ps = C.bank(tt % 2)
        for k in range(KC):
            P.op("act", lambda h, k=k, ts=ts: h.activation(out=sq[k % 2], in_=C.x[:, k, ts], func=AF.Square),
                 reads=[("x", k, tt)], writes=[("nsq", k % 2)])
            P.op("pe", lambda h, k=k, ps=ps: h.matmul(ps, lhsT=C.ones[:], rhs=sq[k % 2], start=(k == 0), stop=(k == KC - 1)),
                 reads=[("nsq", k % 2), "ones"], writes=[("ps", tt % 2)])
        P.op("act", lambda h, ps=ps: h.activation(out=rs, in_=ps, func=AF.Sqrt, bias=EPS, scale=1.0),
             reads=[("ps", tt % 2)], writes=["nrs"])
        P.op("dve", lambda h: h.reciprocal(out=rs, in_=rs), reads=["nrs"], writes=["nrs"])
        for k in range(KC):
            o, okey = wrapped(k, tt)
            P.op("dve", lambda h, k=k, ts=ts: h.tensor_tensor(out=tmp[k % 2], in0=C.x[:, k, ts], in1=rs, op=ALU.mult),
                 reads=[("x", k, tt), "nrs"], writes=[("ntmp", k % 2)])
            P.op("act", lambda h, k=k, o=o: h.activation(out=o, in_=tmp[k % 2], func=AF.Identity,
                                                         bias=C.mod[:, shcol + k, b:b + 1], scale=C.A[:, site, k, b:b + 1]),
                 reads=[("ntmp", k % 2), "mod", ("A", site)], writes=[okey])
            P.dma("sp", lambda h, k=k, ts=ts, o=o: h.dma_start(out=C.outT[b, 128 * k:128 * k + 128, ts], in_=o),
                  reads=[okey], writes=[("out", b, k, tt)], out=True)


def dump_x(C, b):
    P = C.P
    for k in range(KC):
        P.dma("sp", lambda h, k=k: h.dma_start(out=C.outT[b, 128 * k:128 * k + 128, :], in_=C.x[:, k, :]),
              reads=[("x", k, tt) for tt in range(4)], writes=[("out", b, k)], out=True)


def run_sequence(C, b):
    P = C.P
    keepw = lambda k: isinstance(k, tuple) and k[0] == "wb"
    load_x(C, b)
    for stg in C.stages:
        if stg == "attn":
            attn_layer(C, b)
            run_deferred(C, 10 ** 6)
            if isinstance(C.debug, str) and (C.debug.startswith("attn") or C.debug == "wodump"):
                P.barrier(keep=keepw)
                return
        elif stg == "ffn0":
            ffn_layer(C, b, 0)
        elif stg == "ssm":
            ssm_layer(C, b)
        elif stg == "ffn1":
            ffn_layer(C, b, 1)
        elif stg == "dump":
            dump_x(C, b)
            P.barrier(keep=keepw)
            return
        P.barrier(keep=keepw)
    final_out(C, b)
    P.barrier(keep=keepw)


def _colT(v):
    return np.ascontiguousarray(v.reshape(-1, 128).T)


def prep_inputs(inp, core):
    f = lambda a: np.ascontiguousarray(np.asarray(a, dtype=np.float32))
    bs = slice(NB * core, NB * core + NB)
    m = {}
    m["xT"] = f(np.asarray(inp["x"])[bs].transpose(0, 2, 1))
    m["cT"] = f(np.asarray(inp["c"])[bs].T)
    vec = np.zeros((128, NV), np.float32)
    for i in range(2):
        vec[:, V_NMIX + 8 * i:V_NMIX + 8 * i + 8] = _colT(np.asarray(inp["norm_mix"])[i])
        vec[:, V_NFFN + 8 * i:V_NFFN + 8 * i + 8] = _colT(np.asarray(inp["norm_ffn"])[i])
        vec[:, V_BMOD + 48 * i:V_BMOD + 48 * i + 48] = _colT(np.asarray(inp["b_mod"])[i])
        cw = np.asarray(inp["conv_w"])[i].reshape(3, 2 * FJ, 128).transpose(2, 1, 0).reshape(128, 2 * FJ * 3)
        vec[:, V_CW + 132 * i:V_CW + 132 * i + 132] = cw
        vec[:, V_CB + 44 * i:V_CB + 44 * i + 44] = _colT(np.asarray(inp["conv_b"])[i])
    vec[:, V_NOUT:V_NOUT + 8] = _colT(np.asarray(inp["norm_out"]))
    vec[:, V_BFIN:V_BFIN + 16] = _colT(np.asarray(inp["b_fin"]))
    vec[:, V_BGLU:V_BGLU + 8] = _colT(np.asarray(inp["b_glu"])[0])
    dsk = np.asarray(inp["d_skip"])[0].reshape(NG, 16).T
    vec[:, V_DSK:V_DSK + 64] = np.tile(dsk, (8, 1))
    m["vecs"] = vec
    for k in ("w_mod", "w_fin", "w_up", "w_down"):
        m[k] = f(inp[k])
    for k in ("w_qkv", "w_o_attn", "w_in_ssm", "w_glu", "w_o_ssm"):
        m[k] = f(np.asarray(inp[k])[0])
    r2 = lambda a: np.asarray(a)[0].reshape(32, 2, 64).transpose(1, 2, 0).reshape(128, 32)
    ldt = np.broadcast_to(np.asarray(inp["log_dt"])[0].reshape(32, 2, 1), (32, 2, 64)).transpose(1, 2, 0).reshape(128, 32)
    m["ssm_s"] = f(np.stack([r2(inp["a_re"]), r2(inp["a_im"]), ldt], axis=1))
    bp = lambda a: np.asarray(a)[0].reshape(32, 2, 64, 16).transpose(1, 2, 0, 3).reshape(128, 32, 16)
    cp = lambda a: np.asarray(a)[0].reshape(32, 2, 16, 64).transpose(1, 3, 0, 2).reshape(128, 32, 16)
    m["ssm_bc"] = f(np.stack([bp(inp["b_re"]), bp(inp["b_im"]), cp(inp["c_re"]), cp(inp["c_im"])], axis=1))
    return m


_CACHE = {}


def get_program(stages=("attn", "ffn0", "ssm", "ffn1"), debug=None):
    key = (tuple(stages), debug)
    if key not in _CACHE:
        _, C0 = build(stages, schedule=None, debug=debug)
        nc, C = build(stages, schedule=C0.requests, debug=debug)
        _CACHE[key] = (nc, C)
    return _CACHE[key]


def kernel(**inputs):
    nc, C = get_program()
    in_maps = [prep_inputs(inputs, c) for c in range(8)]
    res = run_bass_kernel_spmd(nc, in_maps, core_ids=list(range(8)))
    outs = [np.asarray(r["outT"]).transpose(0, 2, 1) for r in res.results]
    return np.ascontiguousarray(np.concatenate(outs, axis=0).astype(np.float32))


def attn_layer(C, b):
    P = C.P
    h = carve(C, 0, [KC, S], BF16)
    qT = carve(C, 32768, [S], BF16)
    kT = carve(C, 36864, [S], BF16)
    V = carve(C, 40960, [16, 128], BF16)
    oT = carve(C, 45056, [S], BF16)
    NWK = 4
    wk = lambda nm, i: carve(C, 49152 + 4096 * {"e": 0, "ln": 1, "en": 2, "wt": 3}[nm] + 1024 * i, [512], BF16)

    def hout(k, tt):
        return h[:, k, 512 * tt:512 * tt + 512], ("h", k, tt)

    norm_mod(C, b, 0, 0, hout)
    hkeys_tt = lambda tt: [("h", k, tt) for k in range(KC)]
    scale = DH ** -0.5
    g1col = 16

    for g in range(getattr(C, 'ngroups', 8)):
        (wq, wkk, wv), wkey = C.W.next(lambda C_, g=g: [wslab(C_.w_qkv, 128 * g, 128), wslab(C_.w_qkv, D + 128 * g, 128),
                                                      wslab(C_.w_qkv, 2 * D + 128 * g, 128)])
        for which, wmat, dst, dkey in ((0, wq, qT, "qT"), (1, wkk, kT, "kT")):
            for tt in range(4):
                bk = 6 + (tt % 2)
                ps = C.bank(bk)
                for k in range(KC):
                    P.op("pe", lambda h_, wmat=wmat, k=k, tt=tt, ps=ps: h_.matmul(
                        ps, lhsT=wmat[:, k, :], rhs=h[:, k, 512 * tt:512 * tt + 512], start=(k == 0), stop=(k == KC - 1)),
                        reads=[wkey, ("h", k, tt)], writes=[("ps", bk)])
                eng = "act" if tt % 2 == 0 else "dve"
                if eng == "act":
                    P.op("act", lambda h_, dst=dst, tt=tt, ps=ps: h_.activation(out=dst[:, 512 * tt:512 * tt + 512], in_=ps, func=AF.Copy),
                         reads=[("ps", bk)], writes=[(dkey, tt)])
                else:
                    P.op("dve", lambda h_, dst=dst, tt=tt, ps=ps: h_.tensor_copy(out=dst[:, 512 * tt:512 * tt + 512], in_=ps),
                         reads=[("ps", bk)], writes=[(dkey, tt)])
        for q4 in range(4):
            bk = 6 + (q4 % 2)
            ps = C.bank(bk)
            for j in range(4):
                kb = 4 * q4 + j
                for k in range(KC):
                    P.op("pe", lambda h_, k=k, kb=kb, j=j, ps=ps, wv=wv: h_.matmul(
                        ps[:, 128 * j:128 * j + 128], lhsT=h[:, k, 128 * kb:128 * kb + 128], rhs=wv[:, k, :],
                        start=(k == 0), stop=(k == KC - 1)),
                        reads=[wkey, ("h", k, kb // 4)], writes=[("ps", bk)])
            P.op("act" if q4 % 2 == 0 else "dve",
                 (lambda h_, q4=q4, ps=ps: h_.activation(out=V[:, 4 * q4:4 * q4 + 4, :], in_=ps.rearrange("p (a b) -> p a b", a=4), func=AF.Copy))
                 if q4 % 2 == 0 else
                 (lambda h_, q4=q4, ps=ps: h_.tensor_copy(out=V[:, 4 * q4:4 * q4 + 4, :], in_=ps.rearrange("p (a b) -> p a b", a=4))),
                 reads=[("ps", bk)], writes=[("V", q4)])

        tiles = []
        for pair_i, (qa, qb) in enumerate(((0, 3), (1, 2))):
            chains = []
            for ci, (qc, hd) in enumerate(((qa, 0), (qa, 1), (qb, 0), (qb, 1))):
                nkb = 4 * qc + 4
                chains.append([dict(qc=qc, hd=hd, kb=kb, first=(n == 0), last=(kb == 0), diag=(kb >= 4 * qc), i=kb - 4 * qc,
                                    c0=(128 * (kb - 4 * qc) if kb >= 4 * qc else 0), ci=ci, ob=4 + (ci // 2))
                               for n, kb in enumerate(range(nkb - 1, -1, -1))])
            pos = [0] * 4
            while any(pos[c_] < len(chains[c_]) for c_ in range(4)):
                for c_ in range(4):
                    if pos[c_] < len(chains[c_]):
                        tiles.append(chains[c_][pos[c_]])
                        pos[c_] += 1
        NT = len(tiles)

        def s_qk(t):
            T = tiles[t]
            zb = t % 2
            c0 = T["c0"]
            hp = slice(64 * T["hd"], 64 * T["hd"] + 64)
            P.op("pe", lambda h_, T=T, zb=zb, hp=hp, c0=c0: h_.matmul(
                C.bank(zb)[:, c0:512], lhsT=kT[hp, 128 * T["kb"]:128 * T["kb"] + 128],
                rhs=qT[hp, 512 * T["qc"] + c0:512 * T["qc"] + 512], start=True, stop=True),
                reads=[("kT", T["kb"] // 4), ("qT", T["qc"])], writes=[("ps", zb)])

        def s_expa(t):
            T = tiles[t]
            zb = t % 2
            w = t % NWK
            c0 = T["c0"]
            P.op("act", lambda h_, zb=zb, w=w, c0=c0: h_.activation(out=wk("e", w)[:, c0:512], in_=C.bank(zb)[:, c0:512], func=AF.Exp, scale=scale),
                 reads=[("ps", zb)], writes=[("e", w)])
            if T["diag"]:
                P.op("dve", lambda h_, w=w, c0=c0: h_.tensor_tensor(
                    out=wk("e", w)[:, c0:c0 + 128], in0=wk("e", w)[:, c0:c0 + 128], in1=C.maskbig[:, 384:512], op=ALU.mult),
                    reads=[("e", w), "maskbig"], writes=[("e", w)])

        def s_expb(t):
            w = t % NWK
            c0 = tiles[t]["c0"]
            P.op("act", lambda h_, w=w, c0=c0: h_.activation(out=wk("ln", w)[:, c0:512], in_=wk("e", w)[:, c0:512], func=AF.Ln, bias=1.0, scale=1.0),
                 reads=[("e", w)], writes=[("ln", w)])

        def split_cols(T):
            return [(T["c0"], 512, T["first"])]

        def s_mm1(t):
            T = tiles[t]
            sb_ = (2, 3, 6, 7)[T["ci"]]
            w = t % NWK
            for (a_, b_, first) in split_cols(T):
                P.op("pe", lambda h_, sb_=sb_, w=w, a_=a_, b_=b_, first=first: h_.matmul(
                    C.bank(sb_)[:, a_:b_], lhsT=C.tri_i[:], rhs=wk("ln", w)[:, a_:b_], start=first, stop=True),
                    reads=[("ln", w), "tri_i"], writes=[("ps", sb_)])

        def s_en(t):
            T = tiles[t]
            sb_ = (2, 3, 6, 7)[T["ci"]]
            w = t % NWK
            c0 = T["c0"]
            P.op("act", lambda h_, sb_=sb_, w=w, c0=c0: h_.activation(out=wk("en", w)[:, c0:512], in_=C.bank(sb_)[:, c0:512], func=AF.Exp, scale=-1.0),
                 reads=[("ps", sb_)], writes=[("en", w)])

        def s_mm2(t):
            T = tiles[t]
            if T["last"]:
                return
            sb_ = (2, 3, 6, 7)[T["ci"]]
            w = t % NWK
            c0 = T["c0"]
            P.op("pe", lambda h_, sb_=sb_, w=w, c0=c0: h_.matmul(
                C.bank(sb_)[:, c0:512], lhsT=C.tri_r[:], rhs=wk("ln", w)[:, c0:512], start=False, stop=True),
                reads=[("ln", w), "tri_r"], writes=[("ps", sb_)])

        def s_w(t):
            w = t % NWK
            c0 = tiles[t]["c0"]
            P.op("dve", lambda h_, w=w, c0=c0: h_.tensor_tensor(out=wk("wt", w)[:, c0:512], in0=wk("e", w)[:, c0:512], in1=wk("en", w)[:, c0:512], op=ALU.mult),
                 reads=[("e", w), ("en", w)], writes=[("wt", w)])

        def s_pv(t):
            T = tiles[t]
            ob = T["ob"]
            w = t % NWK
            hp = slice(64 * T["hd"], 64 * T["hd"] + 64)
            for (a_, b_, first) in split_cols(T):
                P.op("pe", lambda h_, T=T, ob=ob, w=w, hp=hp, a_=a_, b_=b_, first=first: h_.matmul(
                    C.bank(ob)[hp, a_:b_], lhsT=V[:, T["kb"], hp], rhs=wk("wt", w)[:, a_:b_], start=first, stop=True),
                    reads=[("V", T["kb"] // 4), ("wt", w)], writes=[("ps", ob)])
            if T["last"] and T["hd"] == 1:
                qc = T["qc"]
                P.op("dve", lambda h_, ob=ob, qc=qc: h_.tensor_copy(out=oT[:, 512 * qc:512 * qc + 512], in_=C.bank(ob)),
                     reads=[("ps", ob)], writes=[("oT", qc)])

        s_qk(0)
        if NT > 1:
            s_qk(1)
        s_expa(0)
        s_expb(0)
        for t in range(NT):
            if t + 1 < NT:
                s_expa(t + 1)
                s_expb(t + 1)
            s_mm1(t)
            if t + 2 < NT:
                s_qk(t + 2)
            if t >= 1:
                s_pv(t - 1)
            s_en(t)
            s_mm2(t)
            s_w(t)
        s_pv(NT - 1)

        if isinstance(getattr(C, "debug", None), str) and C.debug.startswith("attn") and g == int(C.debug[4:]):
            dbg = carve(C, 65536, [4, S], F32)
            for i, (src, keys) in enumerate(((qT, [("qT", t_) for t_ in range(4)]), (kT, [("kT", t_) for t_ in range(4)]),
                                             (oT, [("oT", t_) for t_ in range(4)]))):
                P.op("dve", lambda h_, i=i, src=src: h_.tensor_copy(out=dbg[:, i, :], in_=src), reads=keys, writes=[("dbg", i)])
                P.dma("sp", lambda h_, i=i: h_.dma_start(out=C.outT[b, 128 * i:128 * i + 128, :], in_=dbg[:, i, :]),
                      reads=[("dbg", i)], writes=[("dbgo", i)], out=True)
            P.op("dve", lambda h_: h_.tensor_copy(out=dbg[:, 3, :], in_=V.rearrange("p a b -> p (a b)")),
                 reads=[("V", q_) for q_ in range(4)], writes=[("dbg", 3)])
            P.dma("sp", lambda h_: h_.dma_start(out=C.outT[b, 384:512, :], in_=dbg[:, 3, :]), reads=[("dbg", 3)], writes=[("dbgo", 3)], out=True)
            for k_ in range(4):
                P.op("dve", lambda h_, k_=k_: h_.tensor_copy(out=dbg[:, k_, :], in_=h[:, k_, :]),
                     reads=[("h", k_, t_) for t_ in range(4)] + [("dbgo", k_)], writes=[("dbg", k_)])
                P.dma("sp", lambda h_, k_=k_: h_.dma_start(out=C.outT[b, 512 + 128 * k_:640 + 128 * k_, :], in_=dbg[:, k_, :]),
                      reads=[("dbg", k_)], writes=[("dbgo2", k_)], out=True)
            return
        (wo,), wokey = C.W.next(lambda C_, g=g: [(C_.w_o_attn[128 * g:128 * g + 128, :], (D,))])
        if getattr(C, "debug", None) == "wodump" and g == 1:
            dbg = carve(C, 65536, [1024], F32)
            P.op("dve", lambda h_: h_.tensor_copy(out=dbg, in_=wo), reads=[wokey], writes=["dbgw"])
            P.dma("sp", lambda h_: h_.dma_start(out=C.outT[b, 0:128, 0:1024], in_=dbg), reads=["dbgw"], writes=["dbgwo"], out=True)
            return
        for oc in range(KC):
            for tt in range(4):
                bk = 6 + (tt % 2)
                ps = C.bank(bk)
                P.op("pe", lambda h_, oc=oc, tt=tt, ps=ps, wo=wo: h_.matmul(
                    ps, lhsT=wo[:, 128 * oc:128 * oc + 128], rhs=oT[:, 512 * tt:512 * tt + 512], start=True, stop=True),
                    reads=[wokey, ("oT", tt)], writes=[("ps", bk)])
                P.op("dve", lambda h_, oc=oc, tt=tt, ps=ps: h_.scalar_tensor_tensor(
                    out=C.x[:, oc, 512 * tt:512 * tt + 512], in0=ps, scalar=C.mod[:, g1col + oc, b:b + 1],
                    in1=C.x[:, oc, 512 * tt:512 * tt + 512], op0=ALU.mult, op1=ALU.add),
                    reads=[("ps", bk), ("x", oc, tt), "mod"], writes=[("x", oc, tt)])
        run_deferred(C, 3)


def ffn_layer(C, b, li):
    P = C.P
    h = carve(C, 0, [KC, S], BF16)
    gT = carve(C, 32768, [FJ, 1024], BF16)
    acc = [carve(C, 77824 + 4096 * i, [1024], F32) for i in range(4)]
    sg = [carve(C, 94208 + 2048 * i, [1024], BF16) for i in range(2)]
    site = 1 if li == 0 else 3
    shcol = 48 * li + 24
    g2col = 48 * li + 40

    def hout(k, tt):
        return h[:, k, 512 * tt:512 * tt + 512], ("h", k, tt)

    norm_mod(C, b, site, shcol, hout)
    cwv = lambda ch, tap: C.vec[:, V_CW + 132 * li + 3 * ch + tap:V_CW + 132 * li + 3 * ch + tap + 1]
    cbv = lambda ch: C.vec[:, V_CB + 44 * li + ch:V_CB + 44 * li + ch + 1]
    for hf in range(2):
        tok0 = 1024 * hf
        for J in range(6):
            nj = min(4, FJ - 4 * J)
            ncol = 128 * nj
            (wg,), wgk = C.W.next(lambda C_, J=J, ncol=ncol: [wslab(C_.w_up[li], 512 * J, ncol)], ahead=1)
            (wvv,), wvk = C.W.next(lambda C_, J=J, ncol=ncol: [wslab(C_.w_up[li], FF + 512 * J, ncol)], ahead=1)
            for jj in range(nj):
                j = 4 * J + jj
                a = (j % 2) * 2
                for which, wmat, wkey_ in ((0, wg, wgk), (1, wvv, wvk)):
                    pqi = a + which
                    pst = C.pq[pqi][:, :]
                    for tl in range(2):
                        bank = 2 * pqi + tl
                        for k in range(KC):
                            P.op("pe", lambda h_, wmat=wmat, k=k, jj=jj, tl=tl, pst=pst, tok0=tok0: h_.matmul(
                                pst[:, 512 * tl:512 * tl + 512], lhsT=wmat[:, k, 128 * jj:128 * jj + 128],
                                rhs=h[:, k, tok0 + 512 * tl:tok0 + 512 * tl + 512], start=(k == 0), stop=(k == KC - 1)),
                                reads=[wkey_, ("h", k, 2 * hf + tl)], writes=[("ps", bank)])
                    ch = j if which == 0 else FJ + j
                    ai = 2 * (j % 2) + which
                    accw = acc[ai]
                    akey = ("acc", ai)
                    pkeys = [("ps", 2 * pqi), ("ps", 2 * pqi + 1)]
                    P.op("act", lambda h_, accw=accw, pst=pst, ch=ch: h_.activation(
                        out=accw, in_=pst, func=AF.Identity, bias=cbv(ch), scale=cwv(ch, 2)),
                        reads=pkeys + ["vec"], writes=[akey])
                    P.op("dve", lambda h_, accw=accw, pst=pst, ch=ch: h_.scalar_tensor_tensor(
                        out=accw[:, 1:1024], in0=pst[:, 0:1023], scalar=cwv(ch, 1), in1=accw[:, 1:1024], op0=ALU.mult, op1=ALU.add),
                        reads=pkeys + ["vec", akey], writes=[akey])
                    P.op("dve", lambda h_, accw=accw, pst=pst, ch=ch: h_.scalar_tensor_tensor(
                        out=accw[:, 2:1024], in0=pst[:, 0:1022], scalar=cwv(ch, 0), in1=accw[:, 2:1024], op0=ALU.mult, op1=ALU.add),
                        reads=pkeys + ["vec", akey], writes=[akey])
                    if hf == 0:
                        P.op("act", lambda h_, pst=pst, ch=ch: h_.activation(out=C.halo[:, ch, :], in_=pst[:, 1022:1024], func=AF.Copy),
                             reads=pkeys, writes=[("halo", ch)])
                    else:
                        P.op("dve", lambda h_, accw=accw, ch=ch: h_.scalar_tensor_tensor(
                            out=accw[:, 0:1], in0=C.halo[:, ch, 1:2], scalar=cwv(ch, 1), in1=accw[:, 0:1], op0=ALU.mult, op1=ALU.add),
                            reads=[("halo", ch), "vec", akey], writes=[akey])
                        P.op("dve", lambda h_, accw=accw, ch=ch: h_.scalar_tensor_tensor(
                            out=accw[:, 0:2], in0=C.halo[:, ch, 0:2], scalar=cwv(ch, 0), in1=accw[:, 0:2], op0=ALU.mult, op1=ALU.add),
                            reads=[("halo", ch), "vec", akey], writes=[akey])
                sgw = sg[j % 2]
                ag, av = acc[2 * (j % 2)], acc[2 * (j % 2) + 1]
                P.op("act", lambda h_, sgw=sgw, ag=ag: h_.activation(out=sgw, in_=ag, func=AF.Silu),
                     reads=[("acc", 2 * (j % 2))], writes=[("sg", j % 2)])
                P.op("pool", lambda h_, sgw=sgw, av=av, j=j: h_.tensor_tensor(out=gT[:, j, :], in0=sgw, in1=av, op=ALU.mult),
                     reads=[("sg", j % 2), ("acc", 2 * (j % 2) + 1)], writes=[("gT", j)])
        for oc in range(KC):
            (wd,), wdk = C.W.next(lambda C_, oc=oc: [wslab(C_.w_down[li], 128 * oc, 128, kk=FJ)], ahead=2)
            for tl in range(2):
                bank = (2 * oc + tl) % 8
                ps = C.bank(bank)
                for j in range(FJ):
                    P.op("pe", lambda h_, wd=wd, j=j, tl=tl, ps=ps: h_.matmul(
                        ps, lhsT=wd[:, j, :], rhs=gT[:, j, 512 * tl:512 * tl + 512], start=(j == 0), stop=(j == FJ - 1)),
                        reads=[wdk, ("gT", j)], writes=[("ps", bank)])
                tt = 2 * hf + tl
                P.op("dve", lambda h_, oc=oc, tt=tt, ps=ps: h_.scalar_tensor_tensor(
                    out=C.x[:, oc, 512 * tt:512 * tt + 512], in0=ps, scalar=C.mod[:, g2col + oc, b:b + 1],
                    in1=C.x[:, oc, 512 * tt:512 * tt + 512], op0=ALU.mult, op1=ALU.add),
                    reads=[("ps", bank), ("x", oc, tt), "mod"], writes=[("x", oc, tt)])


def ssm_prologue(C):
    P = C.P
    I32 = mybir.dt.int32
    off = [8192]

    def al(shape, dtype=F32):
        n = int(np.prod(shape)) * (4 if dtype in (F32, I32) else 2)
        o = off[0]
        off[0] += (n + 63) // 64 * 64
        if dtype == I32:
            return C.arena[:, o // 2:o // 2 + 2 * int(np.prod(shape))].bitcast(I32)
        return carve(C, o, shape, dtype)

    cnt = [0]

    def tt_(eng, out, in0, in1, op, rd, wr):
        P.op(eng, lambda h, out=out, in0=in0, in1=in1, op=op: h.tensor_tensor(out=out, in0=in0, in1=in1, op=op), reads=rd, writes=wr)

    S_ = al([3, 32])
    BC = al([4, 32, 16])
    P.dma("sp", lambda h: h.dma_start(out=S_, in_=C.ssm_s), writes=["S_"])
    P.dma("sp", lambda h: h.dma_start(out=BC, in_=C.ssm_bc), writes=["BC"])
    a_re, a_im, ldt = S_[:, 0, :], S_[:, 1, :], S_[:, 2, :]
    dt_ = al([32]); ar = al([32]); th = al([32]); mag = al([32])
    P.op("act", lambda h: h.activation(out=dt_, in_=ldt, func=AF.Exp), reads=["S_"], writes=["dt"])
    tt_("dve", ar, a_re, dt_, ALU.mult, ["S_", "dt"], ["ar"])
    tt_("dve", th, a_im, dt_, ALU.mult, ["S_", "dt"], ["th"])
    P.op("act", lambda h: h.activation(out=mag, in_=ar, func=AF.Exp), reads=["ar"], writes=["mag"])
    trig = {}
    for nm, shift in (("sin", 0.0), ("cos", 0.25)):
        y = al([32]); ni = al([32], I32); nf = al([32]); f = al([32]); v = al([32])
        P.op("dve", lambda h, y=y, shift=shift: h.tensor_scalar(out=y, in0=th, scalar1=1.0 / TWO_PI, scalar2=shift, op0=ALU.mult, op1=ALU.add),
             reads=["th"], writes=[("y", nm)])
        P.op("dve", lambda h, y=y, ni=ni: h.tensor_copy(out=ni, in_=y), reads=[("y", nm)], writes=[("ni", nm)])
        P.op("dve", lambda h, nf=nf, ni=ni: h.tensor_copy(out=nf, in_=ni), reads=[("ni", nm)], writes=[("nf", nm)])
        tt_("dve", f, y, nf, ALU.subtract, [("y", nm), ("nf", nm)], [("f", nm)])
        P.op("act", lambda h, v=v, f=f: h.activation(out=v, in_=f, func=AF.Sin, scale=TWO_PI * (1.0 - 1e-6)), reads=[("f", nm)], writes=[("trig", nm)])
        trig[nm] = v
    if getattr(C, 'pro_lim', 99) < 1:
        return
    Lr = al([32]); Li = al([32])
    tt_("dve", Lr, mag, trig["cos"], ALU.mult, ["mag", ("trig", "cos")], ["Lr"])
    tt_("dve", Li, mag, trig["sin"], ALU.mult, ["mag", ("trig", "sin")], ["Li"])
    nr = al([32]); den = al([32]); t1 = al([32]); t2 = al([32]); cr = al([32]); ci = al([32])
    P.op("dve", lambda h: h.tensor_scalar(out=nr, in0=Lr, scalar1=-1.0, scalar2=None, op0=ALU.add), reads=["Lr"], writes=["nr"])
    tt_("dve", t1, a_re, a_re, ALU.mult, ["S_"], ["t1"])
    tt_("dve", t2, a_im, a_im, ALU.mult, ["S_"], ["t2"])
    tt_("dve", den, t1, t2, ALU.add, ["t1", "t2"], ["den"])
    P.op("dve", lambda h: h.reciprocal(out=den, in_=den), reads=["den"], writes=["den"])
    tt_("dve", t1, nr, a_re, ALU.mult, ["nr", "S_", "den"], ["t1"])
    tt_("dve", t2, Li, a_im, ALU.mult, ["Li", "S_", "den"], ["t2"])
    tt_("dve", cr, t1, t2, ALU.add, ["t1", "t2"], ["cr0"])
    tt_("dve", cr, cr, den, ALU.mult, ["cr0", "den"], ["cr"])
    tt_("dve", t1, Li, a_re, ALU.mult, ["Li", "S_", "cr0"], ["t1"])
    tt_("dve", t2, nr, a_im, ALU.mult, ["nr", "S_", "cr0"], ["t2"])
    tt_("dve", ci, t1, t2, ALU.subtract, ["t1", "t2"], ["ci0"])
    tt_("dve", ci, ci, den, ALU.mult, ["ci0", "den"], ["ci"])
    if getattr(C, 'pro_lim', 99) < 2:
        return
    PW = al([2, 9, 32])
    P.op("dve", lambda h: h.memset(PW[:, 0, 0, :], 1.0), reads=["ci"], writes=[("pw", 0)])
    P.op("dve", lambda h: h.memset(PW[:, 1, 0, :], 0.0), reads=[("pw", 0)], writes=[("pw", 0)])
    for j in range(1, 9):
        pr, pi_ = PW[:, 0, j - 1, :], PW[:, 1, j - 1, :]
        tt_("dve", t1, pr, Lr, ALU.mult, [("pw", j - 1), "Lr", "ci", ("pw", j - 2)], ["t1"])
        tt_("dve", t2, pi_, Li, ALU.mult, [("pw", j - 1), "Li", "ci", ("pw", j - 2)], ["t2"])
        tt_("dve", PW[:, 0, j, :], t1, t2, ALU.subtract, ["t1", "t2"], [("pwr", j)])
        tt_("dve", t1, pr, Li, ALU.mult, [("pw", j - 1), "Li", ("pwr", j)], ["t1"])
        tt_("dve", t2, pi_, Lr, ALU.mult, [("pw", j - 1), "Lr", ("pwr", j)], ["t2"])
        tt_("dve", PW[:, 1, j, :], t1, t2, ALU.add, ["t1", "t2", ("pwr", j)], [("pw", j)])
    pwk = [("pw", j) for j in range(9)]
    P.op("dve", lambda h: h.tensor_copy(out=C.ssD[:, 0, :], in_=PW[:, 0, 8, :]), reads=pwk, writes=["ssD0"])
    P.op("dve", lambda h: h.tensor_copy(out=C.ssD[:, 1, :], in_=PW[:, 1, 8, :]), reads=pwk, writes=["ssD"])
    m2 = al([8, 32]); m3 = al([8, 32]); ivr = al([8, 32]); ivi = al([8, 32]); br = al([8, 32]); bi = al([8, 32])
    pr8, pi8 = PW[:, 0, 0:8, :], PW[:, 1, 0:8, :]
    tt_("dve", m2, pr8, pr8, ALU.mult, pwk, ["m2"])
    tt_("dve", m3, pi8, pi8, ALU.mult, pwk, ["m3"])
    tt_("dve", m2, m2, m3, ALU.add, ["m2", "m3"], ["m2s"])
    P.op("dve", lambda h: h.reciprocal(out=m2, in_=m2), reads=["m2s"], writes=["rm"])
    tt_("dve", ivr, pr8, m2, ALU.mult, pwk + ["rm"], ["ivr"])
    tt_("dve", ivi, pi8, m2, ALU.mult, pwk + ["rm"], ["ivi0"])
    P.op("dve", lambda h: h.tensor_scalar(out=ivi, in0=ivi, scalar1=-1.0, scalar2=None, op0=ALU.mult), reads=["ivi0"], writes=["ivi"])
    crb = cr.unsqueeze(1).to_broadcast([128, 8, 32])
    cib = ci.unsqueeze(1).to_broadcast([128, 8, 32])
    tt_("dve", m2, ivr, crb, ALU.mult, ["ivr", "cr", "ivi"], ["q1"])
    tt_("dve", m3, ivi, cib, ALU.mult, ["ivi", "ci", "ivr"], ["q2"])
    tt_("dve", br, m2, m3, ALU.subtract, ["q1", "q2"], ["br"])
    tt_("dve", m2, ivr, cib, ALU.mult, ["ivr", "ci", "br"], ["q1"])
    tt_("dve", m3, ivi, crb, ALU.mult, ["ivi", "cr", "br"], ["q2"])
    tt_("dve", bi, m2, m3, ALU.add, ["q1", "q2"], ["bi"])
    if getattr(C, 'pro_lim', 99) < 3:
        return
    bar_ = lambda: P.barrier(keep=lambda k: isinstance(k, tuple) and k[0] == "wb")
    off_keep = off[0]
    off[0] = 8192 + 49152
    QMr = al([32, 2, 128], BF16); QMi = al([32, 2, 128], BF16); Kr = al([32, 128], BF16); Ki = al([32, 128], BF16)
    persist_end = off[0]
    off[0] = off_keep
    assert off_keep <= 8192 + 49152 - 8192, off_keep
    u1 = al([32, 16]); u2 = al([32, 16])
    b_re, b_im, c_re, c_im = BC[:, 0], BC[:, 1], BC[:, 2], BC[:, 3]
    P.op("pool", lambda h: h.memset(QMr, 0.0), writes=["QMr0"])
    P.op("pool", lambda h: h.memset(QMi, 0.0), writes=["QMi0"])
    lo, hi = slice(0, 64), slice(64, 128)
    for j in range(8):
        bc = lambda ap: ap.unsqueeze(2).to_broadcast([128, 32, 16])
        sl = slice(16 * j, 16 * j + 16)
        prj, pij = bc(PW[:, 0, j, :]), bc(PW[:, 1, j, :])
        brj, bij = bc(br[:, j, :]), bc(bi[:, j, :])
        dep = ["BC", "br", "bi"] + pwk
        tt_("dve", u1, c_re, prj, ALU.mult, dep + [("Q", j - 1)], ["u1"])
        tt_("dve", u2, c_im, pij, ALU.mult, dep + [("Q", j - 1)], ["u2"])
        tt_("dve", QMr[lo, :, 0, sl], u1[lo], u2[lo], ALU.subtract, ["u1", "u2", "QMr0"], [("Qr0", j)])
        tt_("dve", QMr[hi, :, 1, sl], u1[hi], u2[hi], ALU.subtract, ["u1", "u2", "QMr0"], [("Qr", j)])
        tt_("dve", u1, c_re, pij, ALU.mult, dep + [("Qr", j), ("Qr0", j)], ["u1"])
        tt_("dve", u2, c_im, prj, ALU.mult, dep + [("Qr", j), ("Qr0", j)], ["u2"])
        P.op("dve", lambda h, sl=sl: h.scalar_tensor_tensor(out=QMi[lo, :, 0, sl], in0=u1[lo], scalar=-1.0, in1=u2[lo], op0=ALU.mult, op1=ALU.subtract),
             reads=["u1", "u2", "QMi0"], writes=[("Qi0", j)])
        P.op("dve", lambda h, sl=sl: h.scalar_tensor_tensor(out=QMi[hi, :, 1, sl], in0=u1[hi], scalar=-1.0, in1=u2[hi], op0=ALU.mult, op1=ALU.subtract),
             reads=["u1", "u2", "QMi0"], writes=[("Qi", j)])
        tt_("dve", u1, b_re, brj, ALU.mult, dep + [("Qi", j), ("Qi0", j)], ["u1"])
        tt_("dve", u2, b_im, bij, ALU.mult, dep + [("Qi", j), ("Qi0", j)], ["u2"])
        tt_("dve", Kr[:, :, sl], u1, u2, ALU.subtract, ["u1", "u2"], [("Kr", j)])
        tt_("dve", u1, b_re, bij, ALU.mult, dep + [("Kr", j)], ["u1"])
        tt_("dve", u2, b_im, brj, ALU.mult, dep + [("Kr", j)], ["u2"])
        tt_("dve", Ki[:, :, sl], u1, u2, ALU.add, ["u1", "u2"], [("Q", j)])
    bar_()
    off[0] = 8192
    allq = []
    if getattr(C, 'pro_lim', 99) < 4:
        return
    KTMr = al([32, 2, 128], BF16); KTMi = al([32, 2, 128], BF16)
    assert off[0] <= 8192 + 49152
    tmpK = [carve(C, 110592, [8, 128], BF16) for i_ in range(2)]
    P.op("pool", lambda h: h.memset(KTMr, 0.0), writes=["KTM0"])
    P.op("pool", lambda h: h.memset(KTMi, 0.0), writes=["KTM1"])
    for ri, (Ksrc, KTdst) in enumerate(((Kr, KTMr), (Ki, KTMi))):
        for q in range(4):
            bk = 2 * ri + (q % 2)
            psb = C.bank(bk).bitcast(BF16)
            for e in range(8):
                g2 = 8 * q + e
                P.op("pe", lambda h, Ksrc=Ksrc, g2=g2, e=e, psb=psb: h.transpose(psb[:, 128 * e:128 * e + 128], Ksrc[:, g2, :], C.ident[:]),
                     reads=["ident"], writes=[("ps", bk)])
            tk = tmpK[q % 2]
            P.op("act", lambda h, tk=tk, psb=psb: h.activation(out=tk, in_=psb.rearrange("p (a b) -> p a b", a=8), func=AF.Copy),
                 reads=[("ps", bk)], writes=[("tmpK", 0)])
            P.op("dve", lambda h, KTdst=KTdst, q=q, tk=tk: h.tensor_copy(out=KTdst[:, 8 * q:8 * q + 8, 0, 0:64], in_=tk[:, :, 0:64]),
                 reads=[("tmpK", 0), "KTM0", "KTM1"], writes=[("KT", ri, q, 0)])
            P.op("dve", lambda h, KTdst=KTdst, q=q, tk=tk: h.tensor_copy(out=KTdst[:, 8 * q:8 * q + 8, 1, 64:128], in_=tk[:, :, 64:128]),
                 reads=[("tmpK", 0), "KTM0", "KTM1"], writes=[("KT", ri, q, 1)])
    if getattr(C, 'pro_lim', 99) < 5:
        return
    TT = al([64, 128], BF16)
    tmpT = [carve(C, 106496 + 2048 * i_, [4, 128], F32) for i_ in range(2)]
    assert off[0] <= 8192 + 49152 and persist_end <= 106496, (off[0], persist_end)
    for q in range(16):
        bk = 4 + (q % 2)
        ps = C.bank(bk)
        for e in range(2):
            g2 = 2 * q + e
            P.op("pe", lambda h, g2=g2, e=e, ps=ps: h.matmul(
                ps[:, 256 * e:256 * e + 256], lhsT=Kr[:, g2, :], rhs=QMr[:, g2, :, :].rearrange("p a b -> p (a b)"), start=True, stop=False),
                reads=[], writes=[("ps", bk)])
            P.op("pe", lambda h, g2=g2, e=e, ps=ps: h.matmul(
                ps[:, 256 * e:256 * e + 256], lhsT=Ki[:, g2, :], rhs=QMi[:, g2, :, :].rearrange("p a b -> p (a b)"), start=False, stop=True),
                reads=[], writes=[("ps", bk)])
        tm = tmpT[q % 2]
        P.op("dve", lambda h, tm=tm, ps=ps: h.tensor_tensor(
            out=tm, in0=ps.rearrange("p (a b) -> p a b", a=4), in1=C.bmask[:].unsqueeze(1).to_broadcast([128, 4, 128]), op=ALU.mult),
            reads=[("ps", bk), "bmask"], writes=[("tmT", q % 2)])
        for e in range(4):
            g = 4 * q + e
            P.op("dve", lambda h, tm=tm, g=g, e=e: h.scalar_tensor_tensor(
                out=TT[:, g, :], in0=C.identf[:], scalar=C.vec[:, V_DSK + g:V_DSK + g + 1], in1=tm[:, e, :], op0=ALU.mult, op1=ALU.add),
                reads=[("tmT", q % 2), "identf", "vec"], writes=[("TT", g)])
    if getattr(C, 'pro_lim', 99) < 6:
        return
    bar_()
    flat4 = lambda ap: ap.rearrange("p a b c -> p (a b c)")
    for m_, src in enumerate((QMr, QMi, KTMr, KTMi)):
        P.dma("sp", lambda h, m_=m_, src=src: h.dma_start(out=C.scr_mats[m_], in_=flat4(src)), writes=[("scr_mats", m_)])
    P.dma("sp", lambda h: h.dma_start(out=C.scr_tt, in_=TT.rearrange("p a b -> p (a b)")), writes=["scr_tt"])


def ssm_layer(C, b):
    P = C.P
    keepw = lambda k: isinstance(k, tuple) and k[0] in ("wb", "x")
    bar = lambda: P.barrier(keep=keepw)
    A0, B0, C0, M0 = 0, 32768, 65536, 98304
    h = carve(C, A0, [KC, S], BF16)
    uD = carve(C, B0, [KC, 2, 1024], BF16)
    U = carve(C, A0, [NG, 128], BF16)
    Zbf = carve(C, A0 + 16384, [2, 32, 128], BF16)
    Wst = carve(C, C0, [2, 32 * 128], F32)
    Yg = carve(C, C0, [NG, 128], BF16)
    zt = carve(C, A0, [KC, 1024], BF16)
    gl = carve(C, A0 + 16384, [KC, 1024], BF16)
    tmpf = [carve(C, C0 + 16384 + 4096 * i, [1024], F32) for i in range(2)]
    tmps = [carve(C, C0 + 24576 + 1024 * i, [512], BF16) for i in range(2)]
    ring = [dict(Qr=carve(C, M0 + 6144 * r, [4, 2, 128], BF16), Qi=carve(C, M0 + 6144 * r + 2048, [4, 2, 128], BF16),
                 KTr=carve(C, M0 + 6144 * r, [4, 2, 128], BF16), KTi=carve(C, M0 + 6144 * r + 2048, [4, 2, 128], BF16),
                 TT=carve(C, M0 + 6144 * r + 4096, [8, 128], BF16)) for r in range(2)]
    X0 = M0 + 12288
    sA = carve(C, X0, [2, 32], F32)
    sM1 = carve(C, X0 + 256, [2, 32], F32)
    sM2 = carve(C, X0 + 512, [2, 32], F32)
    DD = carve(C, X0 + 768, [2, 32], F32)
    DX = carve(C, X0 + 1024, [2, 32], F32)
    g1col = 48 + 16

    def hout(k, tt):
        return h[:, k, 512 * tt:512 * tt + 512], ("h", k, tt)

    norm_mod(C, b, 2, 48, hout)
    bar()
    for sl_ in range(2):
        (wi,), wik = C.W.next(lambda C_, sl_=sl_: [wslab(C_.w_in, 512 * sl_, 512)])
        for q in range(4):
            oc = 4 * sl_ + q
            for tt in range(4):
                bk = 6 + (tt % 2)
                ps = C.bank(bk)
                for k in range(KC):
                    P.op("pe", lambda h_, wi=wi, k=k, q=q, tt=tt, ps=ps: h_.matmul(
                        ps, lhsT=wi[:, k, 128 * q:128 * q + 128], rhs=h[:, k, 512 * tt:512 * tt + 512], start=(k == 0), stop=(k == KC - 1)),
                        reads=[wik, ("h", k, tt)], writes=[("ps", bk)])
                hf, c0 = tt // 2, 64 * (tt % 2)
                dst = uD[:, oc, hf, :].rearrange("p (i c) -> p i c", i=8)[:, :, c0:c0 + 64]
                src = ps.rearrange("p (c i) -> p i c", i=8)
                if tt % 2 == 0:
                    P.op("act", lambda h_, dst=dst, src=src: h_.activation(out=dst, in_=src, func=AF.Copy),
                         reads=[("ps", bk)], writes=[("uD", oc, hf, tt % 2)])
                else:
                    P.op("dve", lambda h_, dst=dst, src=src: h_.tensor_copy(out=dst, in_=src),
                         reads=[("ps", bk)], writes=[("uD", oc, hf, tt % 2)])
    P.op("dve", lambda h_: h_.tensor_copy(out=DD[:, 0, :], in_=C.ssD[:, 0, :]), writes=["DD0"])
    P.op("dve", lambda h_: h_.tensor_copy(out=DD[:, 1, :], in_=C.ssD[:, 0, :]), reads=["DD0"], writes=["DD1"])
    P.op("dve", lambda h_: h_.tensor_scalar(out=DX[:, 0, :], in0=C.ssD[:, 1, :], scalar1=-1.0, scalar2=None, op0=ALU.mult), reads=["DD1"], writes=["DX0"])
    P.op("dve", lambda h_: h_.tensor_copy(out=DX[:, 1, :], in_=C.ssD[:, 1, :]), reads=["DX0"], writes=["DX"])
    P.op("dve", lambda h_: h_.memset(C.sscar[:], 0.0), reads=["DX"], writes=["sscar"])
    bar()

    def load_mats(gb, r, which):
        R = ring[r]
        fns = []
        f3 = lambda ap: ap.rearrange("p a b c -> p (a b c)")
        if which == "K":
            fns.append(lambda h_, R=R, gb=gb: h_.dma_start(out=f3(R["KTr"]), in_=C.scr_mats[2][:, 1024 * gb:1024 * gb + 1024]))
            fns.append(lambda h_, R=R, gb=gb: h_.dma_start(out=f3(R["KTi"]), in_=C.scr_mats[3][:, 1024 * gb:1024 * gb + 1024]))
        else:
            fns.append(lambda h_, R=R, gb=gb: h_.dma_start(out=f3(R["Qr"]), in_=C.scr_mats[0][:, 1024 * gb:1024 * gb + 1024]))
            fns.append(lambda h_, R=R, gb=gb: h_.dma_start(out=f3(R["Qi"]), in_=C.scr_mats[1][:, 1024 * gb:1024 * gb + 1024]))
            fns.append(lambda h_, R=R, gb=gb: h_.dma_start(out=R["TT"].rearrange("p a b -> p (a b)"), in_=C.scr_tt[:, 1024 * gb:1024 * gb + 1024]))
        P.dma("sp", fns, writes=[("ring", r)])

    for hf in range(2):
        for k in range(KC):
            P.dma("sp", lambda h_, k=k, hf=hf: h_.dma_start(
                out=C.scr_u[:, 128 * k:128 * k + 128, :].rearrange("i p c -> p i c"),
                in_=uD[:, k, hf, :].rearrange("p (i c) -> p i c", i=8)),
                reads=[("uD", k, hf, 0), ("uD", k, hf, 1)], writes=[("scr_u", k)])
        for i in range(8):
            P.dma("sp", lambda h_, i=i: h_.dma_start(
                out=U[16 * i:16 * i + 16, :, :], in_=C.scr_u[i].rearrange("(g h) c -> h g c", h=16)),
                reads=[("scr_u", k) for k in range(KC)], writes=[("U", i)])
        Ukeys = [("U", i) for i in range(8)]
        load_mats(0, 0, "K")
        for gb in range(8):
            if gb + 1 < 8:
                load_mats(gb + 1, (gb + 1) % 2, "K")
            R = ring[gb % 2]
            br_, bi_ = 2 * (gb % 2), 2 * (gb % 2) + 1
            for g2l in range(4):
                g2 = 4 * gb + g2l
                for bk_, KT in ((br_, R["KTr"]), (bi_, R["KTi"])):
                    for gp in range(2):
                        P.op("pe", lambda h_, bk_=bk_, KT=KT, g2l=g2l, gp=gp, g2=g2: h_.matmul(
                            C.bank(bk_)[:, 128 * g2l:128 * g2l + 128], lhsT=KT[:, g2l, gp, :], rhs=U[:, 2 * g2 + gp, :],
                            start=(gp == 0), stop=(gp == 1)),
                            reads=[("ring", gb % 2)] + Ukeys, writes=[("ps", bk_)])
            for ri, bk_ in ((0, br_), (1, bi_)):
                eng = "act" if ri == 0 else "dve"
                dst = Wst[:, ri, :].rearrange("p (c g) -> p c g", g=32)[:, :, 4 * gb:4 * gb + 4]
                src = C.bank(bk_).rearrange("p (g c) -> p c g", g=4)
                if eng == "act":
                    P.op("act", lambda h_, dst=dst, src=src: h_.activation(out=dst, in_=src, func=AF.Copy),
                         reads=[("ps", bk_)], writes=[("W", gb)])
                else:
                    P.op("dve", lambda h_, dst=dst, src=src: h_.tensor_copy(out=dst, in_=src),
                         reads=[("ps", bk_)], writes=[("W2", gb)])
        bar()
        Wv = Wst.rearrange("p r (c g) -> p r c g", g=32)
        for c in range(128):
            zprev = C.sscar[:] if c == 0 else Wv[:, :, c - 1, :]
            wc = Wv[:, :, c, :]
            ns = c > 0
            P.op("dve", lambda h_, zprev=zprev, wc=wc: h_.tensor_tensor(out=sA, in0=zprev, in1=wc, op=ALU.add), reads=["rec"], writes=["rec"], nosync=ns)
            P.op("dve", lambda h_: h_.tensor_tensor(out=sM1, in0=DD, in1=sA, op=ALU.mult), reads=["rec"], writes=["rec"], nosync=True)
            P.op("dve", lambda h_: h_.tensor_tensor(out=sM2[:, 0, :], in0=DX[:, 0, :], in1=sA[:, 1, :], op=ALU.mult), reads=["rec"], writes=["rec"], nosync=True)
            P.op("dve", lambda h_: h_.tensor_tensor(out=sM2[:, 1, :], in0=DX[:, 1, :], in1=sA[:, 0, :], op=ALU.mult), reads=["rec"], writes=["rec"], nosync=True)
            P.op("dve", lambda h_, wc=wc: h_.tensor_tensor(out=wc, in0=sM1, in1=sM2, op=ALU.add), reads=["rec"], writes=["rec"], nosync=True)
        Zv = Zbf
        P.op("dve", lambda h_: h_.tensor_copy(out=Zv[:, :, :, 0], in_=C.sscar[:]), reads=["rec"], writes=["rec"])
        P.op("dve", lambda h_: h_.tensor_copy(out=Zv[:, 0, :, 1:128], in_=Wv[:, 0, 0:127, :].rearrange("p c g -> p g c")), reads=["rec"], writes=["rec"])
        P.op("act", lambda h_: h_.activation(out=Zv[:, 1, :, 1:128], in_=Wv[:, 1, 0:127, :].rearrange("p c g -> p g c"), func=AF.Copy), reads=["rec"], writes=["rec2"])
        P.op("dve", lambda h_: h_.tensor_copy(out=C.sscar[:], in_=Wv[:, :, 127, :]), reads=["rec"], writes=["rec"])
        bar()
        load_mats(0, 0, "Q")
        for gb in range(8):
            if gb + 1 < 8:
                load_mats(gb + 1, (gb + 1) % 2, "Q")
            R = ring[gb % 2]
            for half in range(2):
                bk_ = 4 + (2 * gb + half) % 2
                for e in range(4):
                    gi = 4 * half + e
                    g = 8 * gb + gi
                    g2l, gp = gi // 2, gi % 2
                    g2 = g // 2
                    rows = slice(64 * gp, 64 * gp + 64)
                    o_ = C.bank(bk_)[:, 128 * e:128 * e + 128]
                    P.op("pe", lambda h_, o_=o_, R=R, gi=gi, g=g: h_.matmul(o_, lhsT=R["TT"][:, gi, :], rhs=U[:, g, :], start=True, stop=False),
                         reads=[("ring", gb % 2)] + Ukeys, writes=[("ps", bk_)])
                    P.op("pe", lambda h_, o_=o_, R=R, g2l=g2l, gp=gp, g2=g2: h_.matmul(
                        o_, lhsT=R["Qr"][:, g2l, gp, :], rhs=Zbf[:, 0, g2, :], start=False, stop=False),
                        reads=[("ring", gb % 2)], writes=[("ps", bk_)])
                    P.op("pe", lambda h_, o_=o_, R=R, g2l=g2l, gp=gp, g2=g2: h_.matmul(
                        o_, lhsT=R["Qi"][:, g2l, gp, :], rhs=Zbf[:, 1, g2, :], start=False, stop=True),
                        reads=[("ring", gb % 2)], writes=[("ps", bk_)])
                g0 = 8 * gb + 4 * half
                dst = Yg[:, g0:g0 + 4, :]
                src = C.bank(bk_).rearrange("p (a b) -> p a b", a=4)
                if half == 0:
                    P.op("act", lambda h_, dst=dst, src=src: h_.activation(out=dst, in_=src, func=AF.Copy), reads=[("ps", bk_)], writes=[("Yg", gb, half)])
                else:
                    P.op("dve", lambda h_, dst=dst, src=src: h_.tensor_copy(out=dst, in_=src), reads=[("ps", bk_)], writes=[("Yg", gb, half)])
        Ygkeys = [("Yg", gb, hh) for gb in range(8) for hh in range(2)]
        for j in range(8):
            P.dma("sp", lambda h_, j=j: h_.dma_start(
                out=C.scr_y[j].rearrange("(g h) c -> h g c", h=16), in_=Yg[16 * j:16 * j + 16, :, :]),
                reads=Ygkeys, writes=[("scr_y", j)])
        bar()
        for k in range(KC):
            P.dma("sp", lambda h_, k=k, hf=hf: h_.dma_start(
                out=uD[:, k, hf, :].rearrange("p (j c) -> p j c", j=8),
                in_=C.scr_y[:, 128 * k:128 * k + 128, :].rearrange("j p c -> p j c")),
                writes=[("yD", k)])
        for k in range(KC):
            yv = uD[:, k, hf, :]
            tf = tmpf[k % 2]
            tkey = ("tf", k % 2)
            P.op("act", lambda h_, tf=tf, yv=yv: h_.activation(out=tf, in_=yv, func=AF.Square), reads=[("yD", k)], writes=[tkey])
            P.op("dve", lambda h_, tf=tf: h_.tensor_scalar(out=tf, in0=tf, scalar1=0.044715, scalar2=1.0, op0=ALU.mult, op1=ALU.add),
                 reads=[tkey], writes=[tkey])
            P.op("dve", lambda h_, tf=tf, yv=yv: h_.tensor_tensor(out=tf, in0=tf, in1=yv, op=ALU.mult), reads=[tkey, ("yD", k)], writes=[tkey])
            P.op("act", lambda h_, tf=tf: h_.activation(out=tf, in_=tf, func=AF.Sigmoid, scale=1.5957691216057308), reads=[tkey], writes=[tkey])
            P.op("dve", lambda h_, tf=tf, yv=yv, k=k: h_.tensor_tensor(out=zt[:, k, :], in0=tf, in1=yv, op=ALU.mult),
                 reads=[tkey, ("yD", k)], writes=[("zt", k)])
        for sl_ in range(2):
            (wg_,), wgk = C.W.next(lambda C_, sl_=sl_: [wslab(C_.w_glu, 512 * sl_, 512)])
            for q in range(4):
                oc = 4 * sl_ + q
                for tl in range(2):
                    bk = 6 + (tl % 2)
                    ps = C.bank(bk)
                    for k in range(KC):
                        P.op("pe", lambda h_, wg_=wg_, k=k, q=q, tl=tl, ps=ps: h_.matmul(
                            ps, lhsT=wg_[:, k, 128 * q:128 * q + 128], rhs=zt[:, k, 512 * tl:512 * tl + 512], start=(k == 0), stop=(k == KC - 1)),
                            reads=[wgk, ("zt", k)], writes=[("ps", bk)])
                    ts_ = tmps[tl % 2]
                    P.op("act", lambda h_, ts_=ts_, ps=ps, oc=oc: h_.activation(
                        out=ts_, in_=ps, func=AF.Sigmoid, bias=C.vec[:, V_BGLU + oc:V_BGLU + oc + 1], scale=1.0),
                        reads=[("ps", bk), "vec"], writes=[("tmps", tl % 2)])
                    P.op("dve", lambda h_, ts_=ts_, oc=oc, tl=tl: h_.tensor_tensor(
                        out=gl[:, oc, 512 * tl:512 * tl + 512], in0=zt[:, oc, 512 * tl:512 * tl + 512], in1=ts_, op=ALU.mult),
                        reads=[("tmps", tl % 2), ("zt", oc)], writes=[("gl", oc)])
        for sl_ in range(2):
            (wo_,), wok = C.W.next(lambda C_, sl_=sl_: [wslab(C_.w_o_ssm, 512 * sl_, 512)])
            for q in range(4):
                oc = 4 * sl_ + q
                for tl in range(2):
                    bk = 6 + (tl % 2)
                    ps = C.bank(bk)
                    for k in range(KC):
                        P.op("pe", lambda h_, wo_=wo_, k=k, q=q, tl=tl, ps=ps: h_.matmul(
                            ps, lhsT=wo_[:, k, 128 * q:128 * q + 128], rhs=gl[:, k, 512 * tl:512 * tl + 512], start=(k == 0), stop=(k == KC - 1)),
                            reads=[wok] + [("gl", kk) for kk in range(KC)], writes=[("ps", bk)])
                    xv = C.x[:, oc, 1024 * hf:1024 * hf + 1024].rearrange("p (c j) -> p j c", j=8)[:, 4 * tl:4 * tl + 4, :]
                    pv = ps.rearrange("p (j c) -> p j c", j=4)
                    P.op("dve", lambda h_, xv=xv, pv=pv, oc=oc: h_.scalar_tensor_tensor(
                        out=xv, in0=pv, scalar=C.mod[:, g1col + oc, b:b + 1], in1=xv, op0=ALU.mult, op1=ALU.add),
                        reads=[("ps", bk), ("x", oc, 2 * hf), ("x", oc, 2 * hf + 1), "mod"], writes=[("x", oc, 2 * hf), ("x", oc, 2 * hf + 1)])
        bar()
```

```python
import contextlib
import numpy as np
import concourse.bass as bass
import concourse.mybir as mybir
from concourse.bass_utils import run_bass_kernel_spmd

F32 = mybir.dt.float32
BF16 = mybir.dt.bfloat16
AF = mybir.ActivationFunctionType
ALU = mybir.AluOpType

ENGS = ("pe", "act", "dve", "pool", "sp")
N_DMA_SEMS = 24


class Prog:
    def __init__(self, nc):
        self.nc = nc
        self.ops = {e: [] for e in ENGS}
        self.reg = {}
        self.dma_use = [0] * N_DMA_SEMS
        self.dma_rr = 0
        self.out_tokens = []
        self.arena_dma = {}
        self.last_compute = {}

    def _collect(self, eng, reads, writes):
        need = {}

        def add(k, v):
            if need.get(k, -1) < v:
                need[k] = v

        for r in reads:
            e = self.reg.get(r)
            if e is not None:
                for k, v in e[0].items():
                    add(k, v)
        for w in writes:
            e = self.reg.get(w)
            if e is not None:
                for k, v in e[0].items():
                    add(k, v)
                for k, v in e[1].items():
                    if k == eng:
                        continue
                    add(k, v)
        if eng == "pe":
            need.pop("pe", None)
        return need

    def _commit(self, toks, reads, writes):
        for r in reads:
            e = self.reg.setdefault(r, [{}, {}])
            for k, v in toks:
                if e[1].get(k, -1) < v:
                    e[1][k] = v
        for w in writes:
            self.reg[w] = [dict(toks), {}]

    def op(self, eng, fn, reads=(), writes=(), nosync=False):
        need = self._collect(eng, reads, writes)
        if nosync:
            need.pop(eng, None)
        idx = len(self.ops[eng])
        self.ops[eng].append([fn, list(need.items()), False, None])
        self._commit([(eng, idx)], reads, writes)
        self.last_compute[eng] = idx
        return idx

    def dma(self, eng, fns, reads=(), writes=(), arena=True, out=False):
        if not isinstance(fns, (list, tuple)):
            fns = [fns]
        need = self._collect("x", reads, writes)
        toks = []
        for n, fn in enumerate(fns):
            i = self.dma_rr
            self.dma_rr = (self.dma_rr + 1) % N_DMA_SEMS
            nd = dict(need) if n == 0 else {}
            if self.dma_use[i] > 0:
                k = ("d", i)
                v = 16 * self.dma_use[i]
                if nd.get(k, -1) < v:
                    nd[k] = v
            self.dma_use[i] += 1
            tok = (("d", i), 16 * self.dma_use[i])
            self.ops[eng].append([fn, list(nd.items()), False, tok])
            toks.append(tok)
            if arena:
                self.arena_dma[tok[0]] = tok[1]
            if out:
                self.out_tokens.append(tok)
        self._commit(toks, reads, writes)
        return toks

    def barrier(self, keep=lambda key: False):
        toks = [(e, i) for e, i in self.last_compute.items()]
        toks += list(self.arena_dma.items())
        for e in ENGS:
            self.ops[e].append([None, [t for t in toks if t[0] != e], False, None])
        self.arena_dma = {}
        self.reg = {k: v for k, v in self.reg.items() if keep(k)}

    def finish(self):
        self.ops["sp"].append([None, list(self.out_tokens), False, None])

    def emit(self):
        nc = self.nc
        for e in ENGS:
            for rec in self.ops[e]:
                for k, v in rec[1]:
                    if isinstance(k, str):
                        assert self.ops[k][v][0] is not None and self.ops[k][v][3] is None
                        self.ops[k][v][2] = True
        sig = {}
        for e in ENGS:
            c = 0
            s = []
            for rec in self.ops[e]:
                if rec[2]:
                    c += 1
                s.append(c)
            sig[e] = s
            assert c < 60000, (e, c)
        self.stats = {e: (len(self.ops[e]), sig[e][-1] if sig[e] else 0) for e in ENGS}
        with contextlib.ExitStack() as st:
            esem = {e: st.enter_context(nc.semaphore("s_" + e)) for e in ENGS}
            dsem = [st.enter_context(nc.semaphore("d%d" % i)) for i in range(N_DMA_SEMS)]
            block = st.enter_context(nc.Block())

            def run(e):
                def body(h):
                    water = {}
                    for fn, deps, signal, dtok in self.ops[e]:
                        for k, v in deps:
                            if isinstance(k, str):
                                sem = esem[k]
                                val = sig[k][v]
                            else:
                                sem = dsem[k[1]]
                                val = v
                            if water.get(k, 0) >= val:
                                continue
                            water[k] = val
                            h.wait_ge(sem, val)
                        if fn is None:
                            continue
                        ins = fn(h)
                        if dtok is not None:
                            ins.then_inc(dsem[dtok[0][1]], 16)
                        elif signal:
                            ins.then_inc(esem[e], 1)
                return body

            block.tensor(run("pe"))
            block.scalar(run("act"))
            block.vector(run("dve"))
            block.gpsimd(run("pool"))
            block.sync(run("sp"))


D = 1024
KC = 8
S = 2048
NB = 2
NH = 16
DH = 64
FF = 2816
FJ = 22
NG = 64
EPS = 1e-6
TWO_PI = 6.283185307179586

V_NMIX, V_NFFN, V_NOUT, V_BMOD, V_BFIN, V_BGLU, V_CW, V_CB, V_DSK, NV = 0, 16, 32, 40, 136, 152, 160, 424, 512, 576
NMOD = 112


class WStream:
    def __init__(self, P, bufs, schedule=None):
        self.P = P
        self.bufs = bufs
        self.n = len(bufs)
        self.schedule = schedule
        self.req = []
        self.issued = 0
        self.cur = 0

    def _issue(self, idx):
        parts = (self.schedule[idx] if self.schedule is not None else self.req[idx])(self.C)
        slot = idx % self.n
        buf = self.bufs[slot]
        fns = []
        off = 0
        for (src, shape) in parts:
            n = int(np.prod(shape))
            dst = buf[:, off:off + n]
            if len(shape) == 2:
                dst = dst.rearrange("p (k n) -> p k n", k=shape[0])
            fns.append(lambda h, dst=dst, src=src: h.dma_start(out=dst, in_=src))
            off += n
        self.P.dma("pool", fns, writes=[("wb", slot)], arena=False)

    def next(self, parts_fn, ahead=2):
        idx = self.cur
        self.cur += 1
        self.req.append(parts_fn)
        parts = parts_fn(self.C)
        lim = idx + ahead if self.schedule is not None else idx
        while self.issued <= lim and (self.schedule is None or self.issued < len(self.schedule)):
            self._issue(self.issued)
            self.issued += 1
        slot = idx % self.n
        buf = self.bufs[slot]
        views = []
        off = 0
        for (src, shape) in parts:
            n = int(np.prod(shape))
            v = buf[:, off:off + n]
            if len(shape) == 2:
                v = v.rearrange("p (k n) -> p k n", k=shape[0])
            views.append(v)
            off += n
        return views, ("wb", slot)


def wslab(w2d, c0, n, r0=0, kk=KC):
    return (w2d[r0:r0 + kk * 128, c0:c0 + n].rearrange("(k p) n -> p k n", p=128), (kk, n))


class Ctx:
    pass


def build(stages=("attn", "ffn0", "ssm", "ffn1"), schedule=None, nseq=NB, debug=None):
    nc = bass.Bass("TRN2", target_bir_lowering=False)
    C = Ctx()
    C.debug = debug
    if isinstance(debug, str) and debug.startswith('pro'):
        C.pro_lim = int(debug[3:4])
        C.tt_lim = int(debug[4:5]) if len(debug) > 4 else 9
    if isinstance(debug, str) and debug.startswith('ng'):
        C.ngroups = int(debug[2:])
    C.nc = nc
    C.stages = stages
    di = lambda name, shape: nc.dram_tensor(name, shape, F32, kind="ExternalInput").ap()
    C.xT = di("xT", [NB, D, S])
    C.cT = di("cT", [D, NB])
    C.vecs = di("vecs", [128, NV])
    C.w_mod = di("w_mod", [2, D, 6 * D])
    C.w_fin = di("w_fin", [D, 2 * D])
    C.w_qkv = di("w_qkv", [D, 3 * D])
    C.w_o_attn = di("w_o_attn", [D, D])
    C.w_in = di("w_in_ssm", [D, D])
    C.w_glu = di("w_glu", [D, D])
    C.w_o_ssm = di("w_o_ssm", [D, D])
    C.w_up = di("w_up", [2, D, 2 * FF])
    C.w_down = di("w_down", [2, FF, D])
    C.ssm_s = di("ssm_s", [128, 3, 32])
    C.ssm_bc = di("ssm_bc", [128, 4, 32, 16])
    C.outT = nc.dram_tensor("outT", [NB, D, S], F32, kind="ExternalOutput").ap()
    C.scr_mats = nc.dram_tensor("scr_mats", [4, 128, 32 * 256], BF16, kind="Internal").ap()
    C.scr_tt = nc.dram_tensor("scr_tt", [128, 64 * 128], BF16, kind="Internal").ap()
    C.scr_u = nc.dram_tensor("scr_u", [8, 64 * 16, 128], BF16, kind="Internal").ap()
    C.scr_y = nc.dram_tensor("scr_y", [8, 64 * 16, 128], BF16, kind="Internal").ap()

    P = Prog(nc)
    C.P = P
    with contextlib.ExitStack() as st:
        sb = lambda name, shape, dtype: st.enter_context(nc.sbuf_tensor(name, shape, dtype))
        C.x = sb("x_sb", [128, KC, S], F32)
        C.vec = sb("vec_sb", [128, NV], F32)
        C.mod = sb("mod_sb", [128, NMOD, NB], F32)
        C.A = sb("A_sb", [128, 5, KC, NB], F32)
        C.cact = sb("cact", [128, KC, NB], BF16)
        C.cin = sb("cin", [128, KC, NB], F32)
        C.ones = sb("ones_bf", [128, 128], BF16)
        C.tri_i = sb("tri_i", [128, 128], BF16)
        C.tri_r = sb("tri_r", [128, 128], BF16)
        C.ident = sb("ident", [128, 128], BF16)
        C.maskbig = sb("maskbig", [128, 896], BF16)
        C.bmask = sb("bmask", [128, 128], F32)
        C.identf = sb("identf", [128, 128], F32)
        C.pid_i = sb("pid_i", [128, 2], mybir.dt.int32)
        C.pid_f = sb("pid_f", [128, 2], F32)
        C.halo = sb("halo", [128, 2 * FJ, 2], F32)
        C.sscar = sb("sscar", [128, 2, 32], F32)
        C.ssD = sb("ssD", [128, 2, 32], F32)
        wb = [sb("wbuf%d" % i, [128, 4096], BF16) for i in range(3)]
        C.W = WStream(P, wb, schedule)
        C.W.C = C
        C.ARENA_BYTES = 110 * 1024
        C.arena = sb("arena", [128, C.ARENA_BYTES // 2], BF16)
        C.iot_i = C.arena[:, 0:1792].bitcast(mybir.dt.int32)
        C.iot_f = C.arena[:, 2048:2048 + 1792].bitcast(F32)
        C.NT_OFF = C.ARENA_BYTES - 8192
        pq = [st.enter_context(nc.psum_tensor("pq%d" % i, [128, 1024], F32)) for i in range(4)]
        C.pq = pq
        C.bank = lambda i: pq[i // 2][:, 512 * (i % 2):512 * (i % 2) + 512]

        prologue(C)
        for b in range(nseq):
            run_sequence(C, b)
        P.finish()
        P.emit()
    C.requests = C.W.req
    return nc, C


def carve(C, off, shape, dtype):
    n = int(np.prod(shape))
    if dtype == F32:
        ap = C.arena[:, off // 2: off // 2 + 2 * n].bitcast(F32)
    else:
        ap = C.arena[:, off // 2: off // 2 + n]
    if len(shape) == 2:
        ap = ap.rearrange("p (a b) -> p a b", a=shape[0])
    elif len(shape) == 3:
        ap = ap.rearrange("p (a b c) -> p a b c", a=shape[0], b=shape[1])
    return ap


def prologue(C):
    P, nc = C.P, C.nc
    I32 = mybir.dt.int32
    P.dma("sp", lambda h: h.dma_start(out=C.vec[:], in_=C.vecs), writes=["vec"])
    P.dma("sp", lambda h: h.dma_start(out=C.cin[:], in_=C.cT.rearrange("(k p) b -> p k b", p=128)), writes=["cin"])
    P.op("pool", lambda h: h.iota(C.iot_i[:], pattern=[[1, 896]], base=-384, channel_multiplier=0), writes=["iot_i"])
    P.op("pool", lambda h: h.iota(C.pid_i[:, 0:1], pattern=[[0, 1]], base=0, channel_multiplier=1), writes=["pid_i"])
    P.op("dve", lambda h: h.tensor_copy(out=C.iot_f[:], in_=C.iot_i[:]), reads=["iot_i"], writes=["iot_f"])
    P.op("dve", lambda h: h.tensor_copy(out=C.pid_f[:, 0:1], in_=C.pid_i[:, 0:1]), reads=["pid_i"], writes=["pid_f"])
    P.op("dve", lambda h: h.tensor_single_scalar(out=C.pid_i[:, 1:2], in_=C.pid_i[:, 0:1], scalar=4, op=ALU.arith_shift_right),
         reads=["pid_i"], writes=["pid_i2"])
    P.op("dve", lambda h: h.tensor_copy(out=C.pid_f[:, 1:2], in_=C.pid_i[:, 1:2]), reads=["pid_i2"], writes=["pid_f2"])
    pidx = C.pid_f[:, 0:1]
    col128 = C.iot_f[:, 384:512]
    rd = ["iot_f", "pid_f"]
    P.op("dve", lambda h: h.tensor_single_scalar(out=C.maskbig[:], in_=C.iot_f[:], scalar=pidx, op=ALU.is_gt), reads=rd, writes=["maskbig"])
    P.op("dve", lambda h: h.tensor_single_scalar(out=C.tri_i[:], in_=col128, scalar=pidx, op=ALU.is_le), reads=rd, writes=["tri_i"])
    P.op("dve", lambda h: h.tensor_single_scalar(out=C.tri_r[:], in_=col128, scalar=pidx, op=ALU.is_gt), reads=rd, writes=["tri_r"])
    P.op("dve", lambda h: h.tensor_single_scalar(out=C.ident[:], in_=col128, scalar=pidx, op=ALU.is_equal), reads=rd, writes=["ident"])
    P.op("dve", lambda h: h.tensor_single_scalar(out=C.identf[:], in_=col128, scalar=pidx, op=ALU.is_equal), reads=rd, writes=["identf"])
    P.op("dve", lambda h: h.memset(C.ones[:], 1.0 / D), writes=["ones"])
    P.op("dve", lambda h: h.tensor_single_scalar(out=C.iot_i[:, 0:128], in_=C.iot_i[:, 384:512], scalar=4, op=ALU.arith_shift_right),
         reads=["iot_i", "iot_f"], writes=["iot_i"])
    P.op("dve", lambda h: h.tensor_copy(out=C.iot_f[:, 0:128], in_=C.iot_i[:, 0:128]), reads=["iot_i", "maskbig"], writes=["iot_f"])
    P.op("dve", lambda h: h.tensor_single_scalar(out=C.bmask[:], in_=C.iot_f[:, 0:128], scalar=C.pid_f[:, 1:2], op=ALU.is_ge),
         reads=["iot_f", "pid_f2"], writes=["bmask"])
    P.op("act", lambda h: h.activation(out=C.cact[:], in_=C.cin[:], func=AF.Silu), reads=["cin"], writes=["cact"])
    def mod_tasks(w2d_fn, ncols, col0, bias0, bank):
        tasks = []
        for s_ in range(ncols // 512):
            def task(s_=s_):
                ps = C.bank(bank)
                (wv,), wk = C.W.next(lambda C_, w2d_fn=w2d_fn, s_=s_: [wslab(w2d_fn(C_), 512 * s_, 512)])
                for q in range(4):
                    for k in range(KC):
                        P.op("pe", lambda h, wv=wv, q=q, k=k, ps=ps: h.matmul(
                            ps[:, NB * q:NB * q + NB], lhsT=wv[:, k, 128 * q:128 * q + 128], rhs=C.cact[:, k, :],
                            start=(k == 0), stop=(k == KC - 1)),
                            reads=[wk, "cact"], writes=[("ps", bank)])
                c_ = col0 + 4 * s_
                b_ = bias0 + 4 * s_
                P.op("dve", lambda h, ps=ps, c_=c_, b_=b_: h.tensor_tensor(
                    out=C.mod[:, c_:c_ + 4, :], in0=ps[:, 0:NB * 4].rearrange("p (c b) -> p c b", b=NB),
                    in1=C.vec[:, b_:b_ + 4].unsqueeze(2).to_broadcast([128, 4, NB]), op=ALU.add),
                    reads=[("ps", bank), "vec"], writes=["mod"])
            tasks.append(task)
        return tasks

    sites = [(V_NMIX, 8), (V_NFFN, 32), (V_NMIX + 8, 48 + 8), (V_NFFN + 8, 48 + 32), (V_NOUT, 96 + 8)]

    def site_task(si):
        gcol, sccol = sites[si]
        P.op("dve", lambda h, si=si, gcol=gcol, sccol=sccol: h.scalar_tensor_tensor(
            out=C.A[:, si, :, :], in0=C.mod[:, sccol:sccol + 8, :], scalar=1.0,
            in1=C.vec[:, gcol:gcol + 8].unsqueeze(2).to_broadcast([128, 8, NB]), op0=ALU.add, op1=ALU.mult),
            reads=["mod", "vec"], writes=[("A", si)])

    for t_ in mod_tasks(lambda C_: C_.w_mod[0], 6 * D, 0, V_BMOD, 7):
        t_()
    site_task(0)
    site_task(1)
    C.deferred = (mod_tasks(lambda C_: C_.w_mod[1], 6 * D, 48, V_BMOD + 48, 7)
                  + mod_tasks(lambda C_: C_.w_fin, 2 * D, 96, V_BFIN, 7)
                  + [lambda: site_task(2), lambda: site_task(3), lambda: site_task(4)])
    if "attn" not in C.stages:
        run_deferred(C, 10 ** 6)
    if "ssm" in C.stages:
        ssm_prologue(C)
    P.barrier(keep=lambda k: isinstance(k, tuple) and k[0] == "wb")


def run_deferred(C, n):
    while n > 0 and C.deferred:
        C.deferred.pop(0)()
        n -= 1


def norm_mod(C, b, site, shcol, out_fn, out_dtype_bf16=True):
    P = C.P
    sq = [carve(C, C.NT_OFF + 1024 * i, [512], BF16) for i in range(2)]
    rs = carve(C, C.NT_OFF + 2048, [512], F32)
    tmp = [carve(C, C.NT_OFF + 4096 + 2048 * i, [512], F32) for i in range(2)]
    for tt in range(4):
        ts = slice(512 * tt, 512 * tt + 512)
        ps = C.bank(tt % 2)
        for k in range(KC):
            P.op("act", lambda h, k=k, ts=ts: h.activation(out=sq[k % 2], in_=C.x[:, k, ts], func=AF.Square),
                 reads=[("x", k, tt)], writes=[("nsq", k % 2)])
            P.op("pe", lambda h, k=k, ps=ps: h.matmul(ps, lhsT=C.ones[:], rhs=sq[k % 2], start=(k == 0), stop=(k == KC - 1)),
                 reads=[("nsq", k % 2), "ones"], writes=[("ps", tt % 2)])
        P.op("act", lambda h, ps=ps: h.activation(out=rs, in_=ps, func=AF.Sqrt, bias=EPS, scale=1.0),
             reads=[("ps", tt % 2)], writes=["nrs"])
        P.op("dve", lambda h: h.reciprocal(out=rs, in_=rs), reads=["nrs"], writes=["nrs"])
        for k in range(KC):
            o, okey = out_fn(k, tt)
            P.op("dve", lambda h, k=k, ts=ts: h.tensor_tensor(out=tmp[k % 2], in0=C.x[:, k, ts], in1=rs, op=ALU.mult),
                 reads=[("x", k, tt), "nrs"], writes=[("ntmp", k % 2)])
            P.op("act", lambda h, k=k, o=o: h.activation(out=o, in_=tmp[k % 2], func=AF.Identity,
                                                         bias=C.mod[:, shcol + k, b:b + 1], scale=C.A[:, site, k, b:b + 1]),
                 reads=[("ntmp", k % 2), "mod", ("A", site)], writes=[okey])


def load_x(C, b):
    P = C.P
    for k in range(KC):
        P.dma("sp", lambda h, k=k: h.dma_start(out=C.x[:, k, :], in_=C.xT[b, 128 * k:128 * k + 128, :]),
              writes=[("x", k, tt) for tt in range(4)])


def final_out(C, b):
    obuf = [carve(C, 16384 + 2048 * i, [512], F32) for i in range(4)]
    cnt = [0]

    def out_fn(k, tt):
        i = cnt[0] % 4
        cnt[0] += 1
        return obuf[i], ("obuf", i)

    norm_mod_out(C, b, 4, 96, out_fn, obuf)


def norm_mod_out(C, b, site, shcol, out_fn, obuf):
    P = C.P
    state = {"n": 0}

    def wrapped(k, tt):
        return out_fn(k, tt)

    sq = [carve(C, C.NT_OFF + 1024 * i, [512], BF16) for i in range(2)]
    rs = carve(C, C.NT_OFF + 2048, [512], F32)
    tmp = [carve(C, C.NT_OFF + 4096 + 2048 * i, [512], F32) for i in range(2)]
    for tt in range(4):
        ts = slice(512 * tt, 512 * tt + 512)
        ps = C.bank(tt % 2)
        for k in range(KC):
            P.op("act", lambda h, k=k, ts=ts: h.activation(out=sq[k % 2], in_=C.x[:, k, ts], func=AF.Square),
                 reads=[("x", k, tt)], writes=[("nsq", k % 2)])
            P.op("pe", lambda h, k=k, ps=ps: h.matmul(ps, lhsT=C.ones[:], rhs=sq[k % 2], start=(k == 0), stop=(k == KC - 1)),
                 reads=[("nsq", k % 2), "ones"], writes=[("ps", tt % 2)])
        P.op("act", lambda h, ps=ps: h.activation(out=rs, in_=ps, func=AF.Sqrt, bias=EPS, scale=1.0),
             reads=[("ps", tt % 2)], writes=["nrs"])
        P.op("dve", lambda h: h.reciprocal(out=rs, in_=rs), reads=["nrs"], writes=["nrs"])
        for k in range(KC):
            o, okey = wrapped(k, tt)
            P.op("dve", lambda h, k=k, ts=ts: h.tensor_tensor(out=tmp[k % 2], in0=C.x[:, k, ts], in1=rs, op=ALU.mult),
                 reads=[("x", k, tt), "nrs"], writes=[("ntmp", k % 2)])
            P.op("act", lambda h, k=k, o=o: h.activation(out=o, in_=tmp[k % 2], func=AF.Identity,
                                                         bias=C.mod[:, shcol + k, b:b + 1], scale=C.A[:, site, k, b:b + 1]),
                 reads=[("ntmp", k % 2), "mod", ("A", site)], writes=[okey])
            P.dma("sp", lambda h, k=k, ts=ts, o=o: h.dma_start(out=C.outT[b, 128 * k:128 * k + 128, ts], in_=o),
                  reads=[okey], writes=[("out", b, k, tt)], out=True)


def dump_x(C, b):
    P = C.P
    for k in range(KC):
        P.dma("sp", lambda h, k=k: h.dma_start(out=C.outT[b, 128 * k:128 * k + 128, :], in_=C.x[:, k, :]),
              reads=[("x", k, tt) for tt in range(4)], writes=[("out", b, k)], out=True)


def run_sequence(C, b):
    P = C.P
    keepw = lambda k: isinstance(k, tuple) and k[0] == "wb"
    load_x(C, b)
    for stg in C.stages:
        if stg == "attn":
            attn_layer(C, b)
            run_deferred(C, 10 ** 6)
            if isinstance(C.debug, str) and (C.debug.startswith("attn") or C.debug == "wodump"):
                P.barrier(keep=keepw)
                return
        elif stg == "ffn0":
            ffn_layer(C, b, 0)
        elif stg == "ssm":
            ssm_layer(C, b)
        elif stg == "ffn1":
            ffn_layer(C, b, 1)
        elif stg == "dump":
            dump_x(C, b)
            P.barrier(keep=keepw)
            return
        P.barrier(keep=keepw)
    final_out(C, b)
    P.barrier(keep=keepw)


def _colT(v):
    return np.ascontiguousarray(v.reshape(-1, 128).T)


def prep_inputs(inp, core):
    f = lambda a: np.ascontiguousarray(np.asarray(a, dtype=np.float32))
    bs = slice(NB * core, NB * core + NB)
    m = {}
    m["xT"] = f(np.asarray(inp["x"])[bs].transpose(0, 2, 1))
    m["cT"] = f(np.asarray(inp["c"])[bs].T)
    vec = np.zeros((128, NV), np.float32)
    for i in range(2):
        vec[:, V_NMIX + 8 * i:V_NMIX + 8 * i + 8] = _colT(np.asarray(inp["norm_mix"])[i])
        vec[:, V_NFFN + 8 * i:V_NFFN + 8 * i + 8] = _colT(np.asarray(inp["norm_ffn"])[i])
        vec[:, V_BMOD + 48 * i:V_BMOD + 48 * i + 48] = _colT(np.asarray(inp["b_mod"])[i])
        cw = np.asarray(inp["conv_w"])[i].reshape(3, 2 * FJ, 128).transpose(2, 1, 0).reshape(128, 2 * FJ * 3)
        vec[:, V_CW + 132 * i:V_CW + 132 * i + 132] = cw
        vec[:, V_CB + 44 * i:V_CB + 44 * i + 44] = _colT(np.asarray(inp["conv_b"])[i])
    vec[:, V_NOUT:V_NOUT + 8] = _colT(np.asarray(inp["norm_out"]))
    vec[:, V_BFIN:V_BFIN + 16] = _colT(np.asarray(inp["b_fin"]))
    vec[:, V_BGLU:V_BGLU + 8] = _colT(np.asarray(inp["b_glu"])[0])
    dsk = np.asarray(inp["d_skip"])[0].reshape(NG, 16).T
    vec[:, V_DSK:V_DSK + 64] = np.tile(dsk, (8, 1))
    m["vecs"] = vec
    for k in ("w_mod", "w_fin", "w_up", "w_down"):
        m[k] = f(inp[k])
    for k in ("w_qkv", "w_o_attn", "w_in_ssm", "w_glu", "w_o_ssm"):
        m[k] = f(np.asarray(inp[k])[0])
    r2 = lambda a: np.asarray(a)[0].reshape(32, 2, 64).transpose(1, 2, 0).reshape(128, 32)
    ldt = np.broadcast_to(np.asarray(inp["log_dt"])[0].reshape(32, 2, 1), (32, 2, 64)).transpose(1, 2, 0).reshape(128, 32)
    m["ssm_s"] = f(np.stack([r2(inp["a_re"]), r2(inp["a_im"]), ldt], axis=1))
    bp = lambda a: np.asarray(a)[0].reshape(32, 2, 64, 16).transpose(1, 2, 0, 3).reshape(128, 32, 16)
    cp = lambda a: np.asarray(a)[0].reshape(32, 2, 16, 64).transpose(1, 3, 0, 2).reshape(128, 32, 16)
    m["ssm_bc"] = f(np.stack([bp(inp["b_re"]), bp(inp["b_im"]), cp(inp["c_re"]), cp(inp["c_im"])], axis=1))
    return m


_CACHE = {}


def get_program(stages=("attn", "ffn0", "ssm", "ffn1"), debug=None):
    key = (tuple(stages), debug)
    if key not in _CACHE:
        _, C0 = build(stages, schedule=None, debug=debug)
        nc, C = build(stages, schedule=C0.requests, debug=debug)
        _CACHE[key] = (nc, C)
    return _CACHE[key]


def kernel(**inputs):
    nc, C = get_program()
    in_maps = [prep_inputs(inputs, c) for c in range(8)]
    res = run_bass_kernel_spmd(nc, in_maps, core_ids=list(range(8)))
    outs = [np.asarray(r["outT"]).transpose(0, 2, 1) for r in res.results]
    return np.ascontiguousarray(np.concatenate(outs, axis=0).astype(np.float32))


def attn_layer(C, b):
    P = C.P
    h = carve(C, 0, [KC, S], BF16)
    qT = carve(C, 32768, [S], BF16)
    kT = carve(C, 36864, [S], BF16)
    V = carve(C, 40960, [16, 128], BF16)
    oT = carve(C, 45056, [S], BF16)
    NWK = 4
    wk = lambda nm, i: carve(C, 49152 + 4096 * {"e": 0, "ln": 1, "en": 2, "wt": 3}[nm] + 1024 * i, [512], BF16)

    def hout(k, tt):
        return h[:, k, 512 * tt:512 * tt + 512], ("h", k, tt)

    norm_mod(C, b, 0, 0, hout)
    hkeys_tt = lambda tt: [("h", k, tt) for k in range(KC)]
    scale = DH ** -0.5
    g1col = 16

    for g in range(getattr(C, 'ngroups', 8)):
        (wq, wkk, wv), wkey = C.W.next(lambda C_, g=g: [wslab(C_.w_qkv, 128 * g, 128), wslab(C_.w_qkv, D + 128 * g, 128),
                                                      wslab(C_.w_qkv, 2 * D + 128 * g, 128)])
        for which, wmat, dst, dkey in ((0, wq, qT, "qT"), (1, wkk, kT, "kT")):
            for tt in range(4):
                bk = 6 + (tt % 2)
                ps = C.bank(bk)
                for k in range(KC):
                    P.op("pe", lambda h_, wmat=wmat, k=k, tt=tt, ps=ps: h_.matmul(
                        ps, lhsT=wmat[:, k, :], rhs=h[:, k, 512 * tt:512 * tt + 512], start=(k == 0), stop=(k == KC - 1)),
                        reads=[wkey, ("h", k, tt)], writes=[("ps", bk)])
                eng = "act" if tt % 2 == 0 else "dve"
                if eng == "act":
                    P.op("act", lambda h_, dst=dst, tt=tt, ps=ps: h_.activation(out=dst[:, 512 * tt:512 * tt + 512], in_=ps, func=AF.Copy),
                         reads=[("ps", bk)], writes=[(dkey, tt)])
                else:
                    P.op("dve", lambda h_, dst=dst, tt=tt, ps=ps: h_.tensor_copy(out=dst[:, 512 * tt:512 * tt + 512], in_=ps),
                         reads=[("ps", bk)], writes=[(dkey, tt)])
        for q4 in range(4):
            bk = 6 + (q4 % 2)
            ps = C.bank(bk)
            for j in range(4):
                kb = 4 * q4 + j
                for k in range(KC):
                    P.op("pe", lambda h_, k=k, kb=kb, j=j, ps=ps, wv=wv: h_.matmul(
                        ps[:, 128 * j:128 * j + 128], lhsT=h[:, k, 128 * kb:128 * kb + 128], rhs=wv[:, k, :],
                        start=(k == 0), stop=(k == KC - 1)),
                        reads=[wkey, ("h", k, kb // 4)], writes=[("ps", bk)])
            P.op("act" if q4 % 2 == 0 else "dve",
                 (lambda h_, q4=q4, ps=ps: h_.activation(out=V[:, 4 * q4:4 * q4 + 4, :], in_=ps.rearrange("p (a b) -> p a b", a=4), func=AF.Copy))
                 if q4 % 2 == 0 else
                 (lambda h_, q4=q4, ps=ps: h_.tensor_copy(out=V[:, 4 * q4:4 * q4 + 4, :], in_=ps.rearrange("p (a b) -> p a b", a=4))),
                 reads=[("ps", bk)], writes=[("V", q4)])

        tiles = []
        for pair_i, (qa, qb) in enumerate(((0, 3), (1, 2))):
            chains = []
            for ci, (qc, hd) in enumerate(((qa, 0), (qa, 1), (qb, 0), (qb, 1))):
                nkb = 4 * qc + 4
                chains.append([dict(qc=qc, hd=hd, kb=kb, first=(n == 0), last=(kb == 0), diag=(kb >= 4 * qc), i=kb - 4 * qc,
                                    c0=(128 * (kb - 4 * qc) if kb >= 4 * qc else 0), ci=ci, ob=4 + (ci // 2))
                               for n, kb in enumerate(range(nkb - 1, -1, -1))])
            pos = [0] * 4
            while any(pos[c_] < len(chains[c_]) for c_ in range(4)):
                for c_ in range(4):
                    if pos[c_] < len(chains[c_]):
                        tiles.append(chains[c_][pos[c_]])
                        pos[c_] += 1
        NT = len(tiles)

        def s_qk(t):
            T = tiles[t]
            zb = t % 2
            c0 = T["c0"]
            hp = slice(64 * T["hd"], 64 * T["hd"] + 64)
            P.op("pe", lambda h_, T=T, zb=zb, hp=hp, c0=c0: h_.matmul(
                C.bank(zb)[:, c0:512], lhsT=kT[hp, 128 * T["kb"]:128 * T["kb"] + 128],
                rhs=qT[hp, 512 * T["qc"] + c0:512 * T["qc"] + 512], start=True, stop=True),
                reads=[("kT", T["kb"] // 4), ("qT", T["qc"])], writes=[("ps", zb)])

        def s_expa(t):
            T = tiles[t]
            zb = t % 2
            w = t % NWK
            c0 = T["c0"]
            P.op("act", lambda h_, zb=zb, w=w, c0=c0: h_.activation(out=wk("e", w)[:, c0:512], in_=C.bank(zb)[:, c0:512], func=AF.Exp, scale=scale),
                 reads=[("ps", zb)], writes=[("e", w)])
            if T["diag"]:
                P.op("dve", lambda h_, w=w, c0=c0: h_.tensor_tensor(
                    out=wk("e", w)[:, c0:c0 + 128], in0=wk("e", w)[:, c0:c0 + 128], in1=C.maskbig[:, 384:512], op=ALU.mult),
                    reads=[("e", w), "maskbig"], writes=[("e", w)])

        def s_expb(t):
            w = t % NWK
            c0 = tiles[t]["c0"]
            P.op("act", lambda h_, w=w, c0=c0: h_.activation(out=wk("ln", w)[:, c0:512], in_=wk("e", w)[:, c0:512], func=AF.Ln, bias=1.0, scale=1.0),
                 reads=[("e", w)], writes=[("ln", w)])

        def split_cols(T):
            return [(T["c0"], 512, T["first"])]

        def s_mm1(t):
            T = tiles[t]
            sb_ = (2, 3, 6, 7)[T["ci"]]
            w = t % NWK
            for (a_, b_, first) in split_cols(T):
                P.op("pe", lambda h_, sb_=sb_, w=w, a_=a_, b_=b_, first=first: h_.matmul(
                    C.bank(sb_)[:, a_:b_], lhsT=C.tri_i[:], rhs=wk("ln", w)[:, a_:b_], start=first, stop=True),
                    reads=[("ln", w), "tri_i"], writes=[("ps", sb_)])

        def s_en(t):
            T = tiles[t]
            sb_ = (2, 3, 6, 7)[T["ci"]]
            w = t % NWK
            c0 = T["c0"]
            P.op("act", lambda h_, sb_=sb_, w=w, c0=c0: h_.activation(out=wk("en", w)[:, c0:512], in_=C.bank(sb_)[:, c0:512], func=AF.Exp, scale=-1.0),
                 reads=[("ps", sb_)], writes=[("en", w)])

        def s_mm2(t):
            T = tiles[t]
            if T["last"]:
                return
            sb_ = (2, 3, 6, 7)[T["ci"]]
            w = t % NWK
            c0 = T["c0"]
            P.op("pe", lambda h_, sb_=sb_, w=w, c0=c0: h_.matmul(
                C.bank(sb_)[:, c0:512], lhsT=C.tri_r[:], rhs=wk("ln", w)[:, c0:512], start=False, stop=True),
                reads=[("ln", w), "tri_r"], writes=[("ps", sb_)])

        def s_w(t):
            w = t % NWK
            c0 = tiles[t]["c0"]
            P.op("dve", lambda h_, w=w, c0=c0: h_.tensor_tensor(out=wk("wt", w)[:, c0:512], in0=wk("e", w)[:, c0:512], in1=wk("en", w)[:, c0:512], op=ALU.mult),
                 reads=[("e", w), ("en", w)], writes=[("wt", w)])

        def s_pv(t):
            T = tiles[t]
            ob = T["ob"]
            w = t % NWK
            hp = slice(64 * T["hd"], 64 * T["hd"] + 64)
            for (a_, b_, first) in split_cols(T):
                P.op("pe", lambda h_, T=T, ob=ob, w=w, hp=hp, a_=a_, b_=b_, first=first: h_.matmul(
                    C.bank(ob)[hp, a_:b_], lhsT=V[:, T["kb"], hp], rhs=wk("wt", w)[:, a_:b_], start=first, stop=True),
                    reads=[("V", T["kb"] // 4), ("wt", w)], writes=[("ps", ob)])
            if T["last"] and T["hd"] == 1:
                qc = T["qc"]
                P.op("dve", lambda h_, ob=ob, qc=qc: h_.tensor_copy(out=oT[:, 512 * qc:512 * qc + 512], in_=C.bank(ob)),
                     reads=[("ps", ob)], writes=[("oT", qc)])

        s_qk(0)
        if NT > 1:
            s_qk(1)
        s_expa(0)
        s_expb(0)
        for t in range(NT):
            if t + 1 < NT:
                s_expa(t + 1)
                s_expb(t + 1)
            s_mm1(t)
            if t + 2 < NT:
                s_qk(t + 2)
            if t >= 1:
                s_pv(t - 1)
            s_en(t)
            s_mm2(t)
            s_w(t)
        s_pv(NT - 1)

        if isinstance(getattr(C, "debug", None), str) and C.debug.startswith("attn") and g == int(C.debug[4:]):
            dbg = carve(C, 65536, [4, S], F32)
            for i, (src, keys) in enumerate(((qT, [("qT", t_) for t_ in range(4)]), (kT, [("kT", t_) for t_ in range(4)]),
                                             (oT, [("oT", t_) for t_ in range(4)]))):
                P.op("dve", lambda h_, i=i, src=src: h_.tensor_copy(out=dbg[:, i, :], in_=src), reads=keys, writes=[("dbg", i)])
                P.dma("sp", lambda h_, i=i: h_.dma_start(out=C.outT[b, 128 * i:128 * i + 128, :], in_=dbg[:, i, :]),
                      reads=[("dbg", i)], writes=[("dbgo", i)], out=True)
            P.op("dve", lambda h_: h_.tensor_copy(out=dbg[:, 3, :], in_=V.rearrange("p a b -> p (a b)")),
                 reads=[("V", q_) for q_ in range(4)], writes=[("dbg", 3)])
            P.dma("sp", lambda h_: h_.dma_start(out=C.outT[b, 384:512, :], in_=dbg[:, 3, :]), reads=[("dbg", 3)], writes=[("dbgo", 3)], out=True)
            for k_ in range(4):
                P.op("dve", lambda h_, k_=k_: h_.tensor_copy(out=dbg[:, k_, :], in_=h[:, k_, :]),
                     reads=[("h", k_, t_) for t_ in range(4)] + [("dbgo", k_)], writes=[("dbg", k_)])
                P.dma("sp", lambda h_, k_=k_: h_.dma_start(out=C.outT[b, 512 + 128 * k_:640 + 128 * k_, :], in_=dbg[:, k_, :]),
                      reads=[("dbg", k_)], writes=[("dbgo2", k_)], out=True)
            return
        (wo,), wokey = C.W.next(lambda C_, g=g: [(C_.w_o_attn[128 * g:128 * g + 128, :], (D,))])
        if getattr(C, "debug", None) == "wodump" and g == 1:
            dbg = carve(C, 65536, [1024], F32)
            P.op("dve", lambda h_: h_.tensor_copy(out=dbg, in_=wo), reads=[wokey], writes=["dbgw"])
            P.dma("sp", lambda h_: h_.dma_start(out=C.outT[b, 0:128, 0:1024], in_=dbg), reads=["dbgw"], writes=["dbgwo"], out=True)
            return
        for oc in range(KC):
            for tt in range(4):
                bk = 6 + (tt % 2)
                ps = C.bank(bk)
                P.op("pe", lambda h_, oc=oc, tt=tt, ps=ps, wo=wo: h_.matmul(
                    ps, lhsT=wo[:, 128 * oc:128 * oc + 128], rhs=oT[:, 512 * tt:512 * tt + 512], start=True, stop=True),
                    reads=[wokey, ("oT", tt)], writes=[("ps", bk)])
                P.op("dve", lambda h_, oc=oc, tt=tt, ps=ps: h_.scalar_tensor_tensor(
                    out=C.x[:, oc, 512 * tt:512 * tt + 512], in0=ps, scalar=C.mod[:, g1col + oc, b:b + 1],
                    in1=C.x[:, oc, 512 * tt:512 * tt + 512], op0=ALU.mult, op1=ALU.add),
                    reads=[("ps", bk), ("x", oc, tt), "mod"], writes=[("x", oc, tt)])
        run_deferred(C, 3)


def ffn_layer(C, b, li):
    P = C.P
    h = carve(C, 0, [KC, S], BF16)
    gT = carve(C, 32768, [FJ, 1024], BF16)
    acc = [carve(C, 77824 + 4096 * i, [1024], F32) for i in range(4)]
    sg = [carve(C, 94208 + 2048 * i, [1024], BF16) for i in range(2)]
    site = 1 if li == 0 else 3
    shcol = 48 * li + 24
    g2col = 48 * li + 40

    def hout(k, tt):
        return h[:, k, 512 * tt:512 * tt + 512], ("h", k, tt)

    norm_mod(C, b, site, shcol, hout)
    cwv = lambda ch, tap: C.vec[:, V_CW + 132 * li + 3 * ch + tap:V_CW + 132 * li + 3 * ch + tap + 1]
    cbv = lambda ch: C.vec[:, V_CB + 44 * li + ch:V_CB + 44 * li + ch + 1]
    for hf in range(2):
        tok0 = 1024 * hf
        for J in range(6):
            nj = min(4, FJ - 4 * J)
            ncol = 128 * nj
            (wg,), wgk = C.W.next(lambda C_, J=J, ncol=ncol: [wslab(C_.w_up[li], 512 * J, ncol)], ahead=1)
            (wvv,), wvk = C.W.next(lambda C_, J=J, ncol=ncol: [wslab(C_.w_up[li], FF + 512 * J, ncol)], ahead=1)
            for jj in range(nj):
                j = 4 * J + jj
                a = (j % 2) * 2
                for which, wmat, wkey_ in ((0, wg, wgk), (1, wvv, wvk)):
                    pqi = a + which
                    pst = C.pq[pqi][:, :]
                    for tl in range(2):
                        bank = 2 * pqi + tl
                        for k in range(KC):
                            P.op("pe", lambda h_, wmat=wmat, k=k, jj=jj, tl=tl, pst=pst, tok0=tok0: h_.matmul(
                                pst[:, 512 * tl:512 * tl + 512], lhsT=wmat[:, k, 128 * jj:128 * jj + 128],
                                rhs=h[:, k, tok0 + 512 * tl:tok0 + 512 * tl + 512], start=(k == 0), stop=(k == KC - 1)),
                                reads=[wkey_, ("h", k, 2 * hf + tl)], writes=[("ps", bank)])
                    ch = j if which == 0 else FJ + j
                    ai = 2 * (j % 2) + which
                    accw = acc[ai]
                    akey = ("acc", ai)
                    pkeys = [("ps", 2 * pqi), ("ps", 2 * pqi + 1)]
                    P.op("act", lambda h_, accw=accw, pst=pst, ch=ch: h_.activation(
                        out=accw, in_=pst, func=AF.Identity, bias=cbv(ch), scale=cwv(ch, 2)),
                        reads=pkeys + ["vec"], writes=[akey])
                    P.op("dve", lambda h_, accw=accw, pst=pst, ch=ch: h_.scalar_tensor_tensor(
                        out=accw[:, 1:1024], in0=pst[:, 0:1023], scalar=cwv(ch, 1), in1=accw[:, 1:1024], op0=ALU.mult, op1=ALU.add),
                        reads=pkeys + ["vec", akey], writes=[akey])
                    P.op("dve", lambda h_, accw=accw, pst=pst, ch=ch: h_.scalar_tensor_tensor(
                        out=accw[:, 2:1024], in0=pst[:, 0:1022], scalar=cwv(ch, 0), in1=accw[:, 2:1024], op0=ALU.mult, op1=ALU.add),
                        reads=pkeys + ["vec", akey], writes=[akey])
                    if hf == 0:
                        P.op("act", lambda h_, pst=pst, ch=ch: h_.activation(out=C.halo[:, ch, :], in_=pst[:, 1022:1024], func=AF.Copy),
                             reads=pkeys, writes=[("halo", ch)])
                    else:
                        P.op("dve", lambda h_, accw=accw, ch=ch: h_.scalar_tensor_tensor(
                            out=accw[:, 0:1], in0=C.halo[:, ch, 1:2], scalar=cwv(ch, 1), in1=accw[:, 0:1], op0=ALU.mult, op1=ALU.add),
                            reads=[("halo", ch), "vec", akey], writes=[akey])
                        P.op("dve", lambda h_, accw=accw, ch=ch: h_.scalar_tensor_tensor(
                            out=accw[:, 0:2], in0=C.halo[:, ch, 0:2], scalar=cwv(ch, 0), in1=accw[:, 0:2], op0=ALU.mult, op1=ALU.add),
                            reads=[("halo", ch), "vec", akey], writes=[akey])
                sgw = sg[j % 2]
                ag, av = acc[2 * (j % 2)], acc[2 * (j % 2) + 1]
                P.op("act", lambda h_, sgw=sgw, ag=ag: h_.activation(out=sgw, in_=ag, func=AF.Silu),
                     reads=[("acc", 2 * (j % 2))], writes=[("sg", j % 2)])
                P.op("dve", lambda h_, sgw=sgw, av=av, j=j: h_.tensor_tensor(out=gT[:, j, :], in0=sgw, in1=av, op=ALU.mult),
                     reads=[("sg", j % 2), ("acc", 2 * (j % 2) + 1)], writes=[("gT", j)])
        for oc in range(KC):
            (wd,), wdk = C.W.next(lambda C_, oc=oc: [wslab(C_.w_down[li], 128 * oc, 128, kk=FJ)], ahead=2)
            for tl in range(2):
                bank = (2 * oc + tl) % 8
                ps = C.bank(bank)
                for j in range(FJ):
                    P.op("pe", lambda h_, wd=wd, j=j, tl=tl, ps=ps: h_.matmul(
                        ps, lhsT=wd[:, j, :], rhs=gT[:, j, 512 * tl:512 * tl + 512], start=(j == 0), stop=(j == FJ - 1)),
                        reads=[wdk, ("gT", j)], writes=[("ps", bank)])
                tt = 2 * hf + tl
                P.op("dve", lambda h_, oc=oc, tt=tt, ps=ps: h_.scalar_tensor_tensor(
                    out=C.x[:, oc, 512 * tt:512 * tt + 512], in0=ps, scalar=C.mod[:, g2col + oc, b:b + 1],
                    in1=C.x[:, oc, 512 * tt:512 * tt + 512], op0=ALU.mult, op1=ALU.add),
                    reads=[("ps", bank), ("x", oc, tt), "mod"], writes=[("x", oc, tt)])


def ssm_prologue(C):
    P = C.P
    I32 = mybir.dt.int32
    off = [8192]

    def al(shape, dtype=F32):
        n = int(np.prod(shape)) * (4 if dtype in (F32, I32) else 2)
        o = off[0]
        off[0] += (n + 63) // 64 * 64
        if dtype == I32:
            return C.arena[:, o // 2:o // 2 + 2 * int(np.prod(shape))].bitcast(I32)
        return carve(C, o, shape, dtype)

    cnt = [0]

    def tt_(eng, out, in0, in1, op, rd, wr):
        P.op(eng, lambda h, out=out, in0=in0, in1=in1, op=op: h.tensor_tensor(out=out, in0=in0, in1=in1, op=op), reads=rd, writes=wr)

    S_ = al([3, 32])
    BC = al([4, 32, 16])
    P.dma("sp", lambda h: h.dma_start(out=S_, in_=C.ssm_s), writes=["S_"])
    P.dma("sp", lambda h: h.dma_start(out=BC, in_=C.ssm_bc), writes=["BC"])
    a_re, a_im, ldt = S_[:, 0, :], S_[:, 1, :], S_[:, 2, :]
    dt_ = al([32]); ar = al([32]); th = al([32]); mag = al([32])
    P.op("act", lambda h: h.activation(out=dt_, in_=ldt, func=AF.Exp), reads=["S_"], writes=["dt"])
    tt_("dve", ar, a_re, dt_, ALU.mult, ["S_", "dt"], ["ar"])
    tt_("dve", th, a_im, dt_, ALU.mult, ["S_", "dt"], ["th"])
    P.op("act", lambda h: h.activation(out=mag, in_=ar, func=AF.Exp), reads=["ar"], writes=["mag"])
    trig = {}
    for nm, shift in (("sin", 0.0), ("cos", 0.25)):
        y = al([32]); ni = al([32], I32); nf = al([32]); f = al([32]); v = al([32])
        P.op("dve", lambda h, y=y, shift=shift: h.tensor_scalar(out=y, in0=th, scalar1=1.0 / TWO_PI, scalar2=shift, op0=ALU.mult, op1=ALU.add),
             reads=["th"], writes=[("y", nm)])
        P.op("dve", lambda h, y=y, ni=ni: h.tensor_copy(out=ni, in_=y), reads=[("y", nm)], writes=[("ni", nm)])
        P.op("dve", lambda h, nf=nf, ni=ni: h.tensor_copy(out=nf, in_=ni), reads=[("ni", nm)], writes=[("nf", nm)])
        tt_("dve", f, y, nf, ALU.subtract, [("y", nm), ("nf", nm)], [("f", nm)])
        P.op("act", lambda h, v=v, f=f: h.activation(out=v, in_=f, func=AF.Sin, scale=TWO_PI * (1.0 - 1e-6)), reads=[("f", nm)], writes=[("trig", nm)])
        trig[nm] = v
    if getattr(C, 'pro_lim', 99) < 1:
        return
    Lr = al([32]); Li = al([32])
    tt_("dve", Lr, mag, trig["cos"], ALU.mult, ["mag", ("trig", "cos")], ["Lr"])
    tt_("dve", Li, mag, trig["sin"], ALU.mult, ["mag", ("trig", "sin")], ["Li"])
    nr = al([32]); den = al([32]); t1 = al([32]); t2 = al([32]); cr = al([32]); ci = al([32])
    P.op("dve", lambda h: h.tensor_scalar(out=nr, in0=Lr, scalar1=-1.0, scalar2=None, op0=ALU.add), reads=["Lr"], writes=["nr"])
    tt_("dve", t1, a_re, a_re, ALU.mult, ["S_"], ["t1"])
    tt_("dve", t2, a_im, a_im, ALU.mult, ["S_"], ["t2"])
    tt_("dve", den, t1, t2, ALU.add, ["t1", "t2"], ["den"])
    P.op("dve", lambda h: h.reciprocal(out=den, in_=den), reads=["den"], writes=["den"])
    tt_("dve", t1, nr, a_re, ALU.mult, ["nr", "S_", "den"], ["t1"])
    tt_("dve", t2, Li, a_im, ALU.mult, ["Li", "S_", "den"], ["t2"])
    tt_("dve", cr, t1, t2, ALU.add, ["t1", "t2"], ["cr0"])
    tt_("dve", cr, cr, den, ALU.mult, ["cr0", "den"], ["cr"])
    tt_("dve", t1, Li, a_re, ALU.mult, ["Li", "S_", "cr0"], ["t1"])
    tt_("dve", t2, nr, a_im, ALU.mult, ["nr", "S_", "cr0"], ["t2"])
    tt_("dve", ci, t1, t2, ALU.subtract, ["t1", "t2"], ["ci0"])
    tt_("dve", ci, ci, den, ALU.mult, ["ci0", "den"], ["ci"])
    if getattr(C, 'pro_lim', 99) < 2:
        return
    PW = al([2, 9, 32])
    P.op("dve", lambda h: h.memset(PW[:, 0, 0, :], 1.0), reads=["ci"], writes=[("pw", 0)])
    P.op("dve", lambda h: h.memset(PW[:, 1, 0, :], 0.0), reads=[("pw", 0)], writes=[("pw", 0)])
    for j in range(1, 9):
        pr, pi_ = PW[:, 0, j - 1, :], PW[:, 1, j - 1, :]
        tt_("dve", t1, pr, Lr, ALU.mult, [("pw", j - 1), "Lr", "ci", ("pw", j - 2)], ["t1"])
        tt_("dve", t2, pi_, Li, ALU.mult, [("pw", j - 1), "Li", "ci", ("pw", j - 2)], ["t2"])
        tt_("dve", PW[:, 0, j, :], t1, t2, ALU.subtract, ["t1", "t2"], [("pwr", j)])
        tt_("dve", t1, pr, Li, ALU.mult, [("pw", j - 1), "Li", ("pwr", j)], ["t1"])
        tt_("dve", t2, pi_, Lr, ALU.mult, [("pw", j - 1), "Lr", ("pwr", j)], ["t2"])
        tt_("dve", PW[:, 1, j, :], t1, t2, ALU.add, ["t1", "t2", ("pwr", j)], [("pw", j)])
    pwk = [("pw", j) for j in range(9)]
    P.op("dve", lambda h: h.tensor_copy(out=C.ssD[:, 0, :], in_=PW[:, 0, 8, :]), reads=pwk, writes=["ssD0"])
    P.op("dve", lambda h: h.tensor_copy(out=C.ssD[:, 1, :], in_=PW[:, 1, 8, :]), reads=pwk, writes=["ssD"])
    m2 = al([8, 32]); m3 = al([8, 32]); ivr = al([8, 32]); ivi = al([8, 32]); br = al([8, 32]); bi = al([8, 32])
    pr8, pi8 = PW[:, 0, 0:8, :], PW[:, 1, 0:8, :]
    tt_("dve", m2, pr8, pr8, ALU.mult, pwk, ["m2"])
    tt_("dve", m3, pi8, pi8, ALU.mult, pwk, ["m3"])
    tt_("dve", m2, m2, m3, ALU.add, ["m2", "m3"], ["m2s"])
    P.op("dve", lambda h: h.reciprocal(out=m2, in_=m2), reads=["m2s"], writes=["rm"])
    tt_("dve", ivr, pr8, m2, ALU.mult, pwk + ["rm"], ["ivr"])
    tt_("dve", ivi, pi8, m2, ALU.mult, pwk + ["rm"], ["ivi0"])
    P.op("dve", lambda h: h.tensor_scalar(out=ivi, in0=ivi, scalar1=-1.0, scalar2=None, op0=ALU.mult), reads=["ivi0"], writes=["ivi"])
    crb = cr.unsqueeze(1).to_broadcast([128, 8, 32])
    cib = ci.unsqueeze(1).to_broadcast([128, 8, 32])
    tt_("dve", m2, ivr, crb, ALU.mult, ["ivr", "cr", "ivi"], ["q1"])
    tt_("dve", m3, ivi, cib, ALU.mult, ["ivi", "ci", "ivr"], ["q2"])
    tt_("dve", br, m2, m3, ALU.subtract, ["q1", "q2"], ["br"])
    tt_("dve", m2, ivr, cib, ALU.mult, ["ivr", "ci", "br"], ["q1"])
    tt_("dve", m3, ivi, crb, ALU.mult, ["ivi", "cr", "br"], ["q2"])
    tt_("dve", bi, m2, m3, ALU.add, ["q1", "q2"], ["bi"])
    if getattr(C, 'pro_lim', 99) < 3:
        return
    bar_ = lambda: P.barrier(keep=lambda k: isinstance(k, tuple) and k[0] == "wb")
    off_keep = off[0]
    off[0] = 8192 + 49152
    QMr = al([32, 2, 128], BF16); QMi = al([32, 2, 128], BF16); Kr = al([32, 128], BF16); Ki = al([32, 128], BF16)
    persist_end = off[0]
    off[0] = off_keep
    assert off_keep <= 8192 + 49152 - 8192, off_keep
    u1 = al([32, 16]); u2 = al([32, 16])
    b_re, b_im, c_re, c_im = BC[:, 0], BC[:, 1], BC[:, 2], BC[:, 3]
    P.op("pool", lambda h: h.memset(QMr, 0.0), writes=["QMr0"])
    P.op("pool", lambda h: h.memset(QMi, 0.0), writes=["QMi0"])
    lo, hi = slice(0, 64), slice(64, 128)
    for j in range(8):
        bc = lambda ap: ap.unsqueeze(2).to_broadcast([128, 32, 16])
        sl = slice(16 * j, 16 * j + 16)
        prj, pij = bc(PW[:, 0, j, :]), bc(PW[:, 1, j, :])
        brj, bij = bc(br[:, j, :]), bc(bi[:, j, :])
        dep = ["BC", "br", "bi"] + pwk
        tt_("dve", u1, c_re, prj, ALU.mult, dep + [("Q", j - 1)], ["u1"])
        tt_("dve", u2, c_im, pij, ALU.mult, dep + [("Q", j - 1)], ["u2"])
        tt_("dve", QMr[lo, :, 0, sl], u1[lo], u2[lo], ALU.subtract, ["u1", "u2", "QMr0"], [("Qr0", j)])
        tt_("dve", QMr[hi, :, 1, sl], u1[hi], u2[hi], ALU.subtract, ["u1", "u2", "QMr0"], [("Qr", j)])
        tt_("dve", u1, c_re, pij, ALU.mult, dep + [("Qr", j), ("Qr0", j)], ["u1"])
        tt_("dve", u2, c_im, prj, ALU.mult, dep + [("Qr", j), ("Qr0", j)], ["u2"])
        P.op("dve", lambda h, sl=sl: h.scalar_tensor_tensor(out=QMi[lo, :, 0, sl], in0=u1[lo], scalar=-1.0, in1=u2[lo], op0=ALU.mult, op1=ALU.subtract),
             reads=["u1", "u2", "QMi0"], writes=[("Qi0", j)])
        P.op("dve", lambda h, sl=sl: h.scalar_tensor_tensor(out=QMi[hi, :, 1, sl], in0=u1[hi], scalar=-1.0, in1=u2[hi], op0=ALU.mult, op1=ALU.subtract),
             reads=["u1", "u2", "QMi0"], writes=[("Qi", j)])
        tt_("dve", u1, b_re, brj, ALU.mult, dep + [("Qi", j), ("Qi0", j)], ["u1"])
        tt_("dve", u2, b_im, bij, ALU.mult, dep + [("Qi", j), ("Qi0", j)], ["u2"])
        tt_("dve", Kr[:, :, sl], u1, u2, ALU.subtract, ["u1", "u2"], [("Kr", j)])
        tt_("dve", u1, b_re, bij, ALU.mult, dep + [("Kr", j)], ["u1"])
        tt_("dve", u2, b_im, brj, ALU.mult, dep + [("Kr", j)], ["u2"])
        tt_("dve", Ki[:, :, sl], u1, u2, ALU.add, ["u1", "u2"], [("Q", j)])
    bar_()
    off[0] = 8192
    allq = []
    if getattr(C, 'pro_lim', 99) < 4:
        return
    KTMr = al([32, 2, 128], BF16); KTMi = al([32, 2, 128], BF16)
    assert off[0] <= 8192 + 49152
    tmpK = [carve(C, 110592, [8, 128], BF16) for i_ in range(2)]
    P.op("pool", lambda h: h.memset(KTMr, 0.0), writes=["KTM0"])
    P.op("pool", lambda h: h.memset(KTMi, 0.0), writes=["KTM1"])
    for ri, (Ksrc, KTdst) in enumerate(((Kr, KTMr), (Ki, KTMi))):
        for q in range(4):
            bk = 2 * ri + (q % 2)
            psb = C.bank(bk).bitcast(BF16)
            for e in range(8):
                g2 = 8 * q + e
                P.op("pe", lambda h, Ksrc=Ksrc, g2=g2, e=e, psb=psb: h.transpose(psb[:, 128 * e:128 * e + 128], Ksrc[:, g2, :], C.ident[:]),
                     reads=["ident"], writes=[("ps", bk)])
            tk = tmpK[q % 2]
            P.op("act", lambda h, tk=tk, psb=psb: h.activation(out=tk, in_=psb.rearrange("p (a b) -> p a b", a=8), func=AF.Copy),
                 reads=[("ps", bk)], writes=[("tmpK", 0)])
            P.op("dve", lambda h, KTdst=KTdst, q=q, tk=tk: h.tensor_copy(out=KTdst[:, 8 * q:8 * q + 8, 0, 0:64], in_=tk[:, :, 0:64]),
                 reads=[("tmpK", 0), "KTM0", "KTM1"], writes=[("KT", ri, q, 0)])
            P.op("dve", lambda h, KTdst=KTdst, q=q, tk=tk: h.tensor_copy(out=KTdst[:, 8 * q:8 * q + 8, 1, 64:128], in_=tk[:, :, 64:128]),
                 reads=[("tmpK", 0), "KTM0", "KTM1"], writes=[("KT", ri, q, 1)])
    if getattr(C, 'pro_lim', 99) < 5:
        return
    TT = al([64, 128], BF16)
    tmpT = [carve(C, 106496 + 2048 * i_, [4, 128], F32) for i_ in range(2)]
    assert off[0] <= 8192 + 49152 and persist_end <= 106496, (off[0], persist_end)
    for q in range(16):
        bk = 4 + (q % 2)
        ps = C.bank(bk)
        for e in range(2):
            g2 = 2 * q + e
            P.op("pe", lambda h, g2=g2, e=e, ps=ps: h.matmul(
                ps[:, 256 * e:256 * e + 256], lhsT=Kr[:, g2, :], rhs=QMr[:, g2, :, :].rearrange("p a b -> p (a b)"), start=True, stop=False),
                reads=[], writes=[("ps", bk)])
            P.op("pe", lambda h, g2=g2, e=e, ps=ps: h.matmul(
                ps[:, 256 * e:256 * e + 256], lhsT=Ki[:, g2, :], rhs=QMi[:, g2, :, :].rearrange("p a b -> p (a b)"), start=False, stop=True),
                reads=[], writes=[("ps", bk)])
        tm = tmpT[q % 2]
        P.op("dve", lambda h, tm=tm, ps=ps: h.tensor_tensor(
            out=tm, in0=ps.rearrange("p (a b) -> p a b", a=4), in1=C.bmask[:].unsqueeze(1).to_broadcast([128, 4, 128]), op=ALU.mult),
            reads=[("ps", bk), "bmask"], writes=[("tmT", q % 2)])
        for e in range(4):
            g = 4 * q + e
            P.op("dve", lambda h, tm=tm, g=g, e=e: h.scalar_tensor_tensor(
                out=TT[:, g, :], in0=C.identf[:], scalar=C.vec[:, V_DSK + g:V_DSK + g + 1], in1=tm[:, e, :], op0=ALU.mult, op1=ALU.add),
                reads=[("tmT", q % 2), "identf", "vec"], writes=[("TT", g)])
    if getattr(C, 'pro_lim', 99) < 6:
        return
    bar_()
    flat4 = lambda ap: ap.rearrange("p a b c -> p (a b c)")
    for m_, src in enumerate((QMr, QMi, KTMr, KTMi)):
        P.dma("sp", lambda h, m_=m_, src=src: h.dma_start(out=C.scr_mats[m_], in_=flat4(src)), writes=[("scr_mats", m_)])
    P.dma("sp", lambda h: h.dma_start(out=C.scr_tt, in_=TT.rearrange("p a b -> p (a b)")), writes=["scr_tt"])


def ssm_layer(C, b):
    P = C.P
    keepw = lambda k: isinstance(k, tuple) and k[0] in ("wb", "x")
    bar = lambda: P.barrier(keep=keepw)
    A0, B0, C0, M0 = 0, 32768, 65536, 98304
    h = carve(C, A0, [KC, S], BF16)
    uD = carve(C, B0, [KC, 2, 1024], BF16)
    U = carve(C, A0, [NG, 128], BF16)
    Zbf = carve(C, A0 + 16384, [2, 32, 128], BF16)
    Wst = carve(C, C0, [2, 32 * 128], F32)
    Yg = carve(C, C0, [NG, 128], BF16)
    zt = carve(C, A0, [KC, 1024], BF16)
    gl = carve(C, A0 + 16384, [KC, 1024], BF16)
    tmpf = [carve(C, C0 + 16384 + 4096 * i, [1024], F32) for i in range(2)]
    tmps = [carve(C, C0 + 24576 + 1024 * i, [512], BF16) for i in range(2)]
    ring = [dict(Qr=carve(C, M0 + 6144 * r, [4, 2, 128], BF16), Qi=carve(C, M0 + 6144 * r + 2048, [4, 2, 128], BF16),
                 KTr=carve(C, M0 + 6144 * r, [4, 2, 128], BF16), KTi=carve(C, M0 + 6144 * r + 2048, [4, 2, 128], BF16),
                 TT=carve(C, M0 + 6144 * r + 4096, [8, 128], BF16)) for r in range(2)]
    X0 = M0 + 12288
    sA = carve(C, X0, [2, 32], F32)
    sM1 = carve(C, X0 + 256, [2, 32], F32)
    sM2 = carve(C, X0 + 512, [2, 32], F32)
    DD = carve(C, X0 + 768, [2, 32], F32)
    DX = carve(C, X0 + 1024, [2, 32], F32)
    g1col = 48 + 16

    def hout(k, tt):
        return h[:, k, 512 * tt:512 * tt + 512], ("h", k, tt)

    norm_mod(C, b, 2, 48, hout)
    bar()
    for sl_ in range(2):
        (wi,), wik = C.W.next(lambda C_, sl_=sl_: [wslab(C_.w_in, 512 * sl_, 512)])
        for q in range(4):
            oc = 4 * sl_ + q
            for tt in range(4):
                bk = 6 + (tt % 2)
                ps = C.bank(bk)
                for k in range(KC):
                    P.op("pe", lambda h_, wi=wi, k=k, q=q, tt=tt, ps=ps: h_.matmul(
                        ps, lhsT=wi[:, k, 128 * q:128 * q + 128], rhs=h[:, k, 512 * tt:512 * tt + 512], start=(k == 0), stop=(k == KC - 1)),
                        reads=[wik, ("h", k, tt)], writes=[("ps", bk)])
                hf, c0 = tt // 2, 64 * (tt % 2)
                dst = uD[:, oc, hf, :].rearrange("p (i c) -> p i c", i=8)[:, :, c0:c0 + 64]
                src = ps.rearrange("p (c i) -> p i c", i=8)
                if tt % 2 == 0:
                    P.op("act", lambda h_, dst=dst, src=src: h_.activation(out=dst, in_=src, func=AF.Copy),
                         reads=[("ps", bk)], writes=[("uD", oc, hf, tt % 2)])
                else:
                    P.op("dve", lambda h_, dst=dst, src=src: h_.tensor_copy(out=dst, in_=src),
                         reads=[("ps", bk)], writes=[("uD", oc, hf, tt % 2)])
    P.op("dve", lambda h_: h_.tensor_copy(out=DD[:, 0, :], in_=C.ssD[:, 0, :]), writes=["DD0"])
    P.op("dve", lambda h_: h_.tensor_copy(out=DD[:, 1, :], in_=C.ssD[:, 0, :]), reads=["DD0"], writes=["DD1"])
    P.op("dve", lambda h_: h_.tensor_scalar(out=DX[:, 0, :], in0=C.ssD[:, 1, :], scalar1=-1.0, scalar2=None, op0=ALU.mult), reads=["DD1"], writes=["DX0"])
    P.op("dve", lambda h_: h_.tensor_copy(out=DX[:, 1, :], in_=C.ssD[:, 1, :]), reads=["DX0"], writes=["DX"])
    P.op("dve", lambda h_: h_.memset(C.sscar[:], 0.0), reads=["DX"], writes=["sscar"])
    bar()

    def load_mats(gb, r, which):
        R = ring[r]
        fns = []
        f3 = lambda ap: ap.rearrange("p a b c -> p (a b c)")
        if which == "K":
            fns.append(lambda h_, R=R, gb=gb: h_.dma_start(out=f3(R["KTr"]), in_=C.scr_mats[2][:, 1024 * gb:1024 * gb + 1024]))
            fns.append(lambda h_, R=R, gb=gb: h_.dma_start(out=f3(R["KTi"]), in_=C.scr_mats[3][:, 1024 * gb:1024 * gb + 1024]))
        else:
            fns.append(lambda h_, R=R, gb=gb: h_.dma_start(out=f3(R["Qr"]), in_=C.scr_mats[0][:, 1024 * gb:1024 * gb + 1024]))
            fns.append(lambda h_, R=R, gb=gb: h_.dma_start(out=f3(R["Qi"]), in_=C.scr_mats[1][:, 1024 * gb:1024 * gb + 1024]))
            fns.append(lambda h_, R=R, gb=gb: h_.dma_start(out=R["TT"].rearrange("p a b -> p (a b)"), in_=C.scr_tt[:, 1024 * gb:1024 * gb + 1024]))
        P.dma("sp", fns, writes=[("ring", r)])

    for hf in range(2):
        for k in range(KC):
            P.dma("sp", lambda h_, k=k, hf=hf: h_.dma_start(
                out=C.scr_u[:, 128 * k:128 * k + 128, :].rearrange("i p c -> p i c"),
                in_=uD[:, k, hf, :].rearrange("p (i c) -> p i c", i=8)),
                reads=[("uD", k, hf, 0), ("uD", k, hf, 1)], writes=[("scr_u", k)])
        for i in range(8):
            P.dma("sp", lambda h_, i=i: h_.dma_start(
                out=U[16 * i:16 * i + 16, :, :], in_=C.scr_u[i].rearrange("(g h) c -> h g c", h=16)),
                reads=[("scr_u", k) for k in range(KC)], writes=[("U", i)])
        Ukeys = [("U", i) for i in range(8)]
        load_mats(0, 0, "K")
        for gb in range(8):
            if gb + 1 < 8:
                load_mats(gb + 1, (gb + 1) % 2, "K")
            R = ring[gb % 2]
            br_, bi_ = 2 * (gb % 2), 2 * (gb % 2) + 1
            for g2l in range(4):
                g2 = 4 * gb + g2l
                for bk_, KT in ((br_, R["KTr"]), (bi_, R["KTi"])):
                    for gp in range(2):
                        P.op("pe", lambda h_, bk_=bk_, KT=KT, g2l=g2l, gp=gp, g2=g2: h_.matmul(
                            C.bank(bk_)[:, 128 * g2l:128 * g2l + 128], lhsT=KT[:, g2l, gp, :], rhs=U[:, 2 * g2 + gp, :],
                            start=(gp == 0), stop=(gp == 1)),
                            reads=[("ring", gb % 2)] + Ukeys, writes=[("ps", bk_)])
            for ri, bk_ in ((0, br_), (1, bi_)):
                eng = "act" if ri == 0 else "dve"
                dst = Wst[:, ri, :].rearrange("p (c g) -> p c g", g=32)[:, :, 4 * gb:4 * gb + 4]
                src = C.bank(bk_).rearrange("p (g c) -> p c g", g=4)
                if eng == "act":
                    P.op("act", lambda h_, dst=dst, src=src: h_.activation(out=dst, in_=src, func=AF.Copy),
                         reads=[("ps", bk_)], writes=[("W", gb)])
                else:
                    P.op("dve", lambda h_, dst=dst, src=src: h_.tensor_copy(out=dst, in_=src),
                         reads=[("ps", bk_)], writes=[("W2", gb)])
        bar()
        Wv = Wst.rearrange("p r (c g) -> p r c g", g=32)
        for c in range(128):
            zprev = C.sscar[:] if c == 0 else Wv[:, :, c - 1, :]
            wc = Wv[:, :, c, :]
            ns = c > 0
            P.op("dve", lambda h_, zprev=zprev, wc=wc: h_.tensor_tensor(out=sA, in0=zprev, in1=wc, op=ALU.add), reads=["rec"], writes=["rec"], nosync=ns)
            P.op("dve", lambda h_: h_.tensor_tensor(out=sM1, in0=DD, in1=sA, op=ALU.mult), reads=["rec"], writes=["rec"], nosync=True)
            P.op("dve", lambda h_: h_.tensor_tensor(out=sM2[:, 0, :], in0=DX[:, 0, :], in1=sA[:, 1, :], op=ALU.mult), reads=["rec"], writes=["rec"], nosync=True)
            P.op("dve", lambda h_: h_.tensor_tensor(out=sM2[:, 1, :], in0=DX[:, 1, :], in1=sA[:, 0, :], op=ALU.mult), reads=["rec"], writes=["rec"], nosync=True)
            P.op("dve", lambda h_, wc=wc: h_.tensor_tensor(out=wc, in0=sM1, in1=sM2, op=ALU.add), reads=["rec"], writes=["rec"], nosync=True)
        Zv = Zbf
        P.op("dve", lambda h_: h_.tensor_copy(out=Zv[:, :, :, 0], in_=C.sscar[:]), reads=["rec"], writes=["rec"])
        P.op("dve", lambda h_: h_.tensor_copy(out=Zv[:, 0, :, 1:128], in_=Wv[:, 0, 0:127, :].rearrange("p c g -> p g c")), reads=["rec"], writes=["rec"])
        P.op("act", lambda h_: h_.activation(out=Zv[:, 1, :, 1:128], in_=Wv[:, 1, 0:127, :].rearrange("p c g -> p g c"), func=AF.Copy), reads=["rec"], writes=["rec2"])
        P.op("dve", lambda h_: h_.tensor_copy(out=C.sscar[:], in_=Wv[:, :, 127, :]), reads=["rec"], writes=["rec"])
        bar()
        load_mats(0, 0, "Q")
        for gb in range(8):
            if gb + 1 < 8:
                load_mats(gb + 1, (gb + 1) % 2, "Q")
            R = ring[gb % 2]
            for half in range(2):
                bk_ = 4 + (2 * gb + half) % 2
                for e in range(4):
                    gi = 4 * half + e
                    g = 8 * gb + gi
                    g2l, gp = gi // 2, gi % 2
                    g2 = g // 2
                    rows = slice(64 * gp, 64 * gp + 64)
                    o_ = C.bank(bk_)[:, 128 * e:128 * e + 128]
                    P.op("pe", lambda h_, o_=o_, R=R, gi=gi, g=g: h_.matmul(o_, lhsT=R["TT"][:, gi, :], rhs=U[:, g, :], start=True, stop=False),
                         reads=[("ring", gb % 2)] + Ukeys, writes=[("ps", bk_)])
                    P.op("pe", lambda h_, o_=o_, R=R, g2l=g2l, gp=gp, g2=g2: h_.matmul(
                        o_, lhsT=R["Qr"][:, g2l, gp, :], rhs=Zbf[:, 0, g2, :], start=False, stop=False),
                        reads=[("ring", gb % 2)], writes=[("ps", bk_)])
                    P.op("pe", lambda h_, o_=o_, R=R, g2l=g2l, gp=gp, g2=g2: h_.matmul(
                        o_, lhsT=R["Qi"][:, g2l, gp, :], rhs=Zbf[:, 1, g2, :], start=False, stop=True),
                        reads=[("ring", gb % 2)], writes=[("ps", bk_)])
                g0 = 8 * gb + 4 * half
                dst = Yg[:, g0:g0 + 4, :]
                src = C.bank(bk_).rearrange("p (a b) -> p a b", a=4)
                if half == 0:
                    P.op("act", lambda h_, dst=dst, src=src: h_.activation(out=dst, in_=src, func=AF.Copy), reads=[("ps", bk_)], writes=[("Yg", gb, half)])
                else:
                    P.op("dve", lambda h_, dst=dst, src=src: h_.tensor_copy(out=dst, in_=src), reads=[("ps", bk_)], writes=[("Yg", gb, half)])
        Ygkeys = [("Yg", gb, hh) for gb in range(8) for hh in range(2)]
        for j in range(8):
            P.dma("sp", lambda h_, j=j: h_.dma_start(
                out=C.scr_y[j].rearrange("(g h) c -> h g c", h=16), in_=Yg[16 * j:16 * j + 16, :, :]),
                reads=Ygkeys, writes=[("scr_y", j)])
        bar()
        for k in range(KC):
            P.dma("sp", lambda h_, k=k, hf=hf: h_.dma_start(
                out=uD[:, k, hf, :].rearrange("p (j c) -> p j c", j=8),
                in_=C.scr_y[:, 128 * k:128 * k + 128, :].rearrange("j p c -> p j c")),
                writes=[("yD", k)])
        for k in range(KC):
            yv = uD[:, k, hf, :]
            tf = tmpf[k % 2]
            tkey = ("tf", k % 2)
            P.op("act", lambda h_, tf=tf, yv=yv: h_.activation(out=tf, in_=yv, func=AF.Square), reads=[("yD", k)], writes=[tkey])
            P.op("dve", lambda h_, tf=tf: h_.tensor_scalar(out=tf, in0=tf, scalar1=0.044715, scalar2=1.0, op0=ALU.mult, op1=ALU.add),
                 reads=[tkey], writes=[tkey])
            P.op("dve", lambda h_, tf=tf, yv=yv: h_.tensor_tensor(out=tf, in0=tf, in1=yv, op=ALU.mult), reads=[tkey, ("yD", k)], writes=[tkey])
            P.op("act", lambda h_, tf=tf: h_.activation(out=tf, in_=tf, func=AF.Sigmoid, scale=1.5957691216057308), reads=[tkey], writes=[tkey])
            P.op("dve", lambda h_, tf=tf, yv=yv, k=k: h_.tensor_tensor(out=zt[:, k, :], in0=tf, in1=yv, op=ALU.mult),
                 reads=[tkey, ("yD", k)], writes=[("zt", k)])
        for sl_ in range(2):
            (wg_,), wgk = C.W.next(lambda C_, sl_=sl_: [wslab(C_.w_glu, 512 * sl_, 512)])
            for q in range(4):
                oc = 4 * sl_ + q
                for tl in range(2):
                    bk = 6 + (tl % 2)
                    ps = C.bank(bk)
                    for k in range(KC):
                        P.op("pe", lambda h_, wg_=wg_, k=k, q=q, tl=tl, ps=ps: h_.matmul(
                            ps, lhsT=wg_[:, k, 128 * q:128 * q + 128], rhs=zt[:, k, 512 * tl:512 * tl + 512], start=(k == 0), stop=(k == KC - 1)),
                            reads=[wgk, ("zt", k)], writes=[("ps", bk)])
                    ts_ = tmps[tl % 2]
                    P.op("act", lambda h_, ts_=ts_, ps=ps, oc=oc: h_.activation(
                        out=ts_, in_=ps, func=AF.Sigmoid, bias=C.vec[:, V_BGLU + oc:V_BGLU + oc + 1], scale=1.0),
                        reads=[("ps", bk), "vec"], writes=[("tmps", tl % 2)])
                    P.op("dve", lambda h_, ts_=ts_, oc=oc, tl=tl: h_.tensor_tensor(
                        out=gl[:, oc, 512 * tl:512 * tl + 512], in0=zt[:, oc, 512 * tl:512 * tl + 512], in1=ts_, op=ALU.mult),
                        reads=[("tmps", tl % 2), ("zt", oc)], writes=[("gl", oc)])
        for sl_ in range(2):
            (wo_,), wok = C.W.next(lambda C_, sl_=sl_: [wslab(C_.w_o_ssm, 512 * sl_, 512)])
            for q in range(4):
                oc = 4 * sl_ + q
                for tl in range(2):
                    bk = 6 + (tl % 2)
                    ps = C.bank(bk)
                    for k in range(KC):
                        P.op("pe", lambda h_, wo_=wo_, k=k, q=q, tl=tl, ps=ps: h_.matmul(
                            ps, lhsT=wo_[:, k, 128 * q:128 * q + 128], rhs=gl[:, k, 512 * tl:512 * tl + 512], start=(k == 0), stop=(k == KC - 1)),
                            reads=[wok] + [("gl", kk) for kk in range(KC)], writes=[("ps", bk)])
                    xv = C.x[:, oc, 1024 * hf:1024 * hf + 1024].rearrange("p (c j) -> p j c", j=8)[:, 4 * tl:4 * tl + 4, :]
                    pv = ps.rearrange("p (j c) -> p j c", j=4)
                    P.op("dve", lambda h_, xv=xv, pv=pv, oc=oc: h_.scalar_tensor_tensor(
                        out=xv, in0=pv, scalar=C.mod[:, g1col + oc, b:b + 1], in1=xv, op0=ALU.mult, op1=ALU.add),
                        reads=[("ps", bk), ("x", oc, 2 * hf), ("x", oc, 2 * hf + 1), "mod"], writes=[("x", oc, 2 * hf), ("x", oc, 2 * hf + 1)])
        bar()
```
